# Optimizing a Trainium2 kernel written in Bass

```python
import jax, jax.numpy as jnp
from jax import lax
import numpy as np

D_MODEL = 1024
BATCH = 16
SEQ = 256
DEPTH = 2
DEC_BATCH = 4
DEC_SEQ = 4096
PAST_LEN = 256

GRID_W = 64
N_MIXERS = 2
N_A = (DEPTH + 1) // 2
N_B = DEPTH // 2
HG_HEADS = 8
HG_DK = D_MODEL // HG_HEADS
HG_DV = D_MODEL // HG_HEADS
SCAN_CHUNK = 64
MIX_CHUNK = 128
ROWS_PER_CHUNK = MIX_CHUNK // GRID_W
CM_GROUPS = 8
CM_GROUP_W = D_MODEL // CM_GROUPS
D_FF = 4 * D_MODEL
N_MOD = 6
EPS = 1e-6

kernel_name = "hybrid_hgrn2_chunkmlp_diffusion_step"


def rmsnorm(x, g):
    xf = x.astype(jnp.float32)
    y = xf * lax.rsqrt(jnp.mean(xf * xf, axis=-1, keepdims=True) + EPS)
    return (y * g.astype(jnp.float32)).astype(x.dtype)


def layernorm(x, g, b):
    xf = x.astype(jnp.float32)
    mu = jnp.mean(xf, axis=-1, keepdims=True)
    var = jnp.mean(jnp.square(xf - mu), axis=-1, keepdims=True)
    y = (xf - mu) * lax.rsqrt(var + EPS) * g.astype(jnp.float32) + b.astype(jnp.float32)
    return y.astype(x.dtype)


def adaln(cvec, w, b):
    m = jax.nn.silu(cvec) @ w + b
    return [t[:, None, :] for t in jnp.split(m, N_MOD, axis=-1)]


def gla_scan(q, k, v, g, s0):
    Bn, L, H, _ = q.shape
    DV = v.shape[-1]
    n = L // SCAN_CHUNK

    def to_chunks(a):
        return a.reshape(Bn, n, SCAN_CHUNK, a.shape[2], a.shape[3]).transpose(1, 0, 3, 2, 4)

    mask = jnp.tril(jnp.ones((SCAN_CHUNK, SCAN_CHUNK), dtype=bool))[:, :, None]

    def step(S, inp):
        qc, kc, vc, gc = inp
        b = jnp.cumsum(gc, axis=2)
        o = jnp.einsum('bhtk,bhkv->bhtv', qc * jnp.exp(b), S)
        diff = jnp.where(mask, b[:, :, :, None, :] - b[:, :, None, :, :], -jnp.inf)
        att = jnp.einsum('bhtk,bhsk,bhtsk->bhts', qc, kc, jnp.exp(diff))
        o = o + jnp.einsum('bhts,bhsv->bhtv', att, vc)
        b_last = b[:, :, -1:, :]
        S = jnp.exp(b_last[:, :, 0, :])[..., None] * S + jnp.einsum(
            'bhsk,bhsv->bhkv', kc * jnp.exp(b_last - b), vc)
        return S, o

    S, o = lax.scan(step, s0.astype(jnp.float32), (to_chunks(q), to_chunks(k), to_chunks(v), to_chunks(g)))
    o = o.transpose(1, 0, 3, 2, 4).reshape(Bn, L, H, DV)
    return o, S


def hgrn2_mixer(h, s0_fwd, s0_bwd, w_in, lb, onorm_g, w_out):
    Bn, L, _ = h.shape
    z = h @ w_in
    zq, zff, zfb, zi, zg = jnp.split(z, 5, axis=-1)
    q = jax.nn.silu(zq.astype(jnp.float32)).reshape(Bn, L, HG_HEADS, HG_DK)
    v = zi.astype(jnp.float32).reshape(Bn, L, HG_HEADS, HG_DV)

    def gates(zf, lbd):
        f = lbd + (1.0 - lbd) * jax.nn.sigmoid(zf.astype(jnp.float32))
        return (1.0 - f).reshape(Bn, L, HG_HEADS, HG_DK), jnp.log(f).reshape(Bn, L, HG_HEADS, HG_DK)

    k_f, g_f = gates(zff, lb[0])
    k_b, g_b = gates(zfb, lb[1])
    o_f, s_f = gla_scan(q, k_f, v, g_f, s0_fwd)
    rev = lambda a: jnp.flip(a, axis=1)
    o_b, s_b = gla_scan(rev(q), rev(k_b), rev(v), rev(g_b), s0_bwd)
    o = o_f + rev(o_b)
    o = o * lax.rsqrt(jnp.mean(o * o, axis=-1, keepdims=True) + EPS) * onorm_g.astype(jnp.float32)
    o = o.reshape(Bn, L, D_MODEL) * jax.nn.silu(zg.astype(jnp.float32))
    return o.astype(h.dtype) @ w_out, s_f, s_b


def chunk_mlp_mixer(h, n_chunks, w_in, ln_g, ln_b, w_s, b_s, w_out):
    Bn, L, _ = h.shape
    z = jax.nn.gelu(h @ w_in)
    u, v = jnp.split(z, 2, axis=-1)
    v = layernorm(v, ln_g, ln_b).reshape(Bn, n_chunks, MIX_CHUNK, CM_GROUPS, CM_GROUP_W)
    s = jnp.einsum('gpq,bnqgc->bnpgc', w_s, v) + b_s.T[:, :, None]
    return (u * s.reshape(Bn, L, D_MODEL)) @ w_out


def sqrelu_mlp(h, w1, w2):
    return jnp.square(jax.nn.relu(h @ w1)) @ w2


def setup_inputs(seed: int = 0) -> dict:
    key = jax.random.key(seed)
    ks = jax.random.split(key, 24)
    nrm = lambda k, shape, s=1.0: jax.random.normal(k, shape, jnp.float32) * s
    D = D_MODEL
    return {
        "x_prompt": nrm(ks[0], (BATCH, SEQ, D)),
        "x_sample": nrm(ks[1], (DEC_BATCH, DEC_SEQ, D)),
        "state_hgrn": nrm(ks[2], (DEC_BATCH, N_A, 2, HG_HEADS, HG_DK, HG_DV), 0.5),
        "c": nrm(ks[3], (DEC_BATCH, D)),
        "c_ctx": nrm(ks[4], (D,)),
        "ada_w": nrm(ks[5], (DEPTH, D, N_MOD * D), 0.5 * D ** -0.5),
        "ada_b": nrm(ks[6], (DEPTH, N_MOD * D), 0.02),
        "norm_mix_g": 1.0 + nrm(ks[7], (DEPTH, D), 0.02),
        "norm_mlp_g": 1.0 + nrm(ks[8], (DEPTH, D), 0.02),
        "mlp_w1": nrm(ks[9], (DEPTH, D, D_FF), D ** -0.5),
        "mlp_w2": nrm(ks[10], (DEPTH, D_FF, D), D_FF ** -0.5),
        "hgrn_w_in": nrm(ks[11], (N_A, D, 5 * D), D ** -0.5),
        "hgrn_lb_logits": nrm(ks[12], (N_A + 1, 2, HG_HEADS * HG_DK)),
        "hgrn_onorm_g": 1.0 + nrm(ks[13], (N_A, HG_HEADS, HG_DV), 0.02),
        "hgrn_w_out": nrm(ks[14], (N_A, D, D), D ** -0.5),
        "cm_w_in": nrm(ks[15], (N_B, D, 2 * D), D ** -0.5),
        "cm_ln_g": 1.0 + nrm(ks[16], (N_B, D), 0.02),
        "cm_ln_b": nrm(ks[17], (N_B, D), 0.02),
        "cm_w_s": nrm(ks[18], (N_B, CM_GROUPS, MIX_CHUNK, MIX_CHUNK), MIX_CHUNK ** -0.5),
        "cm_b_s": 1.0 + nrm(ks[19], (N_B, CM_GROUPS, MIX_CHUNK), 0.02),
        "cm_w_out": nrm(ks[20], (N_B, D, D), D ** -0.5),
        "final_norm_g": 1.0 + nrm(ks[21], (D,), 0.02),
    }


def reference(x_prompt, x_sample, state_hgrn, c, c_ctx, ada_w, ada_b, norm_mix_g, norm_mlp_g,
              mlp_w1, mlp_w2, hgrn_w_in, hgrn_lb_logits, hgrn_onorm_g, hgrn_w_out,
              cm_w_in, cm_ln_g, cm_ln_b, cm_w_s, cm_b_s, cm_w_out, final_norm_g):
    ctx = x_prompt
    lat = x_sample
    n_ctx_batch, ctx_len, _ = ctx.shape
    rows = lat.shape[1] // GRID_W
    ctx_cond = c_ctx[None, :]
    lb_all = jnp.cumsum(jax.nn.softmax(hgrn_lb_logits.astype(jnp.float32), axis=0), axis=0)
    zero_state = jnp.zeros((n_ctx_batch, HG_HEADS, HG_DK, HG_DV), jnp.float32)
    new_states = []
    for i in range(DEPTH):
        j = i // N_MIXERS
        sh1c, sc1c, gt1c, sh2c, sc2c, gt2c = adaln(ctx_cond, ada_w[i], ada_b[i])
        sh1l, sc1l, gt1l, sh2l, sc2l, gt2l = adaln(c, ada_w[i], ada_b[i])
        h_ctx = rmsnorm(ctx, norm_mix_g[i]) * (1.0 + sc1c) + sh1c
        h_lat = rmsnorm(lat, norm_mix_g[i]) * (1.0 + sc1l) + sh1l
        if i % N_MIXERS == 0:
            o_ctx, s_f, s_b = hgrn2_mixer(h_ctx, zero_state, zero_state, hgrn_w_in[j], lb_all[j],
                                          hgrn_onorm_g[j], hgrn_w_out[j])
            o_lat, _, _ = hgrn2_mixer(h_lat, state_hgrn[:, j, 0], state_hgrn[:, j, 1], hgrn_w_in[j],
                                      lb_all[j], hgrn_onorm_g[j], hgrn_w_out[j])
            new_states.append(jnp.stack([s_f, s_b], axis=1).astype(x_prompt.dtype))
        else:
            o_ctx = chunk_mlp_mixer(h_ctx, ctx_len // MIX_CHUNK, cm_w_in[j], cm_ln_g[j], cm_ln_b[j],
                                    cm_w_s[j], cm_b_s[j], cm_w_out[j])
            o_lat = chunk_mlp_mixer(h_lat, rows // ROWS_PER_CHUNK, cm_w_in[j], cm_ln_g[j], cm_ln_b[j],
                                    cm_w_s[j], cm_b_s[j], cm_w_out[j])
        ctx = ctx + gt1c * o_ctx
        lat = lat + gt1l * o_lat
        h_ctx = rmsnorm(ctx, norm_mlp_g[i]) * (1.0 + sc2c) + sh2c
        h_lat = rmsnorm(lat, norm_mlp_g[i]) * (1.0 + sc2l) + sh2l
        ctx = ctx + gt2c * sqrelu_mlp(h_ctx, mlp_w1[i], mlp_w2[i])
        lat = lat + gt2l * sqrelu_mlp(h_lat, mlp_w1[i], mlp_w2[i])
    y_prompt = rmsnorm(ctx, final_norm_g)
    y_sample = rmsnorm(lat, final_norm_g)
    new_state_hgrn = jnp.stack(new_states, axis=1)
    return (y_prompt, y_sample, new_state_hgrn)
```

```python
import os
import numpy as np
from contextlib import ExitStack
import concourse.bass as bass
import concourse.mybir as mybir
from concourse.bass_utils import run_bass_kernel_spmd

F32 = mybir.dt.float32
BF16 = mybir.dt.bfloat16
AF = mybir.ActivationFunctionType
ALU = mybir.AluOpType

D = 1024
KC = 8
TB = 512
NBLK = 5
NTOK = 2560
NOTH = 2048
EPS = 1e-6
NV = 192
DEBUG = bool(int(os.environ.get("KDEBUG", "0")))
STAGE = int(os.environ.get("KSTAGE", "99"))
KSUB = int(os.environ.get("KSUB", "99"))
KP = int(os.environ.get("KP", "99"))


class T:
    __slots__ = ("writer", "readers", "name")

    def __init__(self, name=""):
        self.writer = None
        self.readers = []
        self.name = name


class Sched:
    NDQ = 24

    def __init__(self, nc, es):
        self.nc = nc
        self.es = es
        self.engs = ["pe", "act", "dve", "pool", "sp"]
        self.sem, self.count, self.clock, self.hist = {}, {}, {}, {}
        self.prog = {e: [] for e in self.engs}
        for e in self.engs:
            self.sem[e] = es.enter_context(nc.semaphore("s_" + e))
            self.count[e] = 0
            self.clock[e] = {}
            self.hist[e] = {}
        self.dq = []
        for i in range(self.NDQ):
            q = "dq%d" % i
            self.sem[q] = es.enter_context(nc.semaphore("s_" + q))
            self.count[q] = 0
            self.hist[q] = {}
            self.dq.append(q)
        self.dq_next = {"sp": 0, "pool": 0}
        self.dq_set = {"sp": self.dq[:self.NDQ // 2], "pool": self.dq[self.NDQ // 2:]}

    def _waits_for(self, X, deps):
        clk = self.clock[X]
        need = {}
        for (E, n) in deps:
            if E == X and X == "pe":
                continue
            if clk.get(E, 0) >= n:
                continue
            if need.get(E, 0) < n:
                need[E] = n
        waits = []
        for E, n in need.items():
            assert n <= self.count[E]
            mult = 16 if E.startswith("dq") else 1
            waits.append((self.sem[E], n * mult))
            h = self.hist[E].get(n)
            if h:
                for k, v in h.items():
                    if k != X and clk.get(k, 0) < v:
                        clk[k] = v
            if clk.get(E, 0) < n:
                clk[E] = n
        return waits

    @staticmethod
    def _deps(reads, writes):
        deps = set()
        for t in reads:
            if t.writer:
                deps.add(t.writer)
        for t in writes:
            if t.writer:
                deps.add(t.writer)
            for r in t.readers:
                deps.add(r)
        return deps

    def op(self, X, fn, reads=(), writes=()):
        deps = self._deps(reads, writes)
        waits = self._waits_for(X, deps)
        self.count[X] += 1
        seq = self.count[X]
        h = dict(self.clock[X])
        h[X] = seq
        self.hist[X][seq] = h
        ident = (X, seq)
        for t in reads:
            t.readers.append(ident)
        for t in writes:
            t.writer = ident
            t.readers = []
        self.prog[X].append((waits, fn, (self.sem[X], 1)))
        return ident

    def dma(self, X, out_ap, in_ap, reads=(), writes=()):
        qs = self.dq_set[X]
        q = qs[self.dq_next[X]]
        self.dq_next[X] = (self.dq_next[X] + 1) % len(qs)
        deps = self._deps(reads, writes)
        if self.count[q] > 0:
            deps.add((q, self.count[q]))
        waits = self._waits_for(X, deps)
        self.count[q] += 1
        seq = self.count[q]
        h = dict(self.clock[X])
        h[q] = seq
        self.hist[q][seq] = h
        ident = (q, seq)
        for t in reads:
            t.readers.append(ident)
        for t in writes:
            t.writer = ident
            t.readers = []
        fn = lambda eng, o=out_ap, i=in_ap: eng.dma_start(out=o, in_=i)
        self.prog[X].append((waits, fn, (self.sem[q], 16)))
        return ident

    def barrier(self):
        allq = [(e, self.count[e]) for e in self.engs if self.count[e] > 0]
        allq += [(q, self.count[q]) for q in self.dq if self.count[q] > 0]
        for X in self.engs:
            waits = self._waits_for(X, set(d for d in allq if d[0] != X))
            if waits:
                self.prog[X].append((waits, None, None))

    def replay(self, block):
        sched = self

        def run(eng, X):
            for waits, fn, inc in sched.prog[X]:
                for (s, v) in waits:
                    eng.wait_ge(s, v)
                if fn is None:
                    continue
                ins = fn(eng)
                ins.then_inc(inc[0], inc[1])

        @block.tensor
        def _(e):
            run(e, "pe")

        @block.scalar
        def _(e):
            run(e, "act")

        @block.vector
        def _(e):
            run(e, "dve")

        @block.gpsimd
        def _(e):
            run(e, "pool")

        @block.sync
        def _(e):
            run(e, "sp")


def build_program():
    nc = bass.Bass("TRN2", target_bir_lowering=False)
    din = lambda n, s: nc.dram_tensor(n, list(s), F32, kind="ExternalInput").ap()
    xo = din("xo", [NTOK, D])
    xh = din("xh", [NOTH, D])
    cT = din("cT", [128, 16])
    vecs = din("vecs", [128, NV])
    ada_w = din("ada_w", [2, D, 6 * D])
    w_hg = din("w_hg", [8, D, 640])
    w_ho = din("w_ho", [D, D])
    w1 = din("w1", [2, D, 4 * D])
    w2 = din("w2", [2, 4 * D, D])
    cm_wi = din("cm_wi", [D, 2 * D])
    cm_wo = din("cm_wo", [D, D])
    wsT = din("wsT", [128, 1024])
    bsb = din("bsb", [128, 1024])
    sa0 = din("sa0", [8, 128, 128])
    sb0 = din("sb0", [8, 128, 128])
    y = nc.dram_tensor("y", [NTOK, D], F32, kind="ExternalOutput").ap()
    ns = nc.dram_tensor("ns", [2, 2, 8, 128, 128], F32, kind="ExternalOutput").ap()
    if DEBUG:
        dbgx = nc.dram_tensor("dbgx", [128, 40960], BF16, kind="ExternalOutput").ap()
        dbga = nc.dram_tensor("dbga", [128, 20480], BF16, kind="ExternalOutput").ap()

    with ExitStack() as es:
        S = Sched(nc, es)
        sb = lambda n, s, dt: es.enter_context(nc.sbuf_tensor(n, list(s), dt))
        XRAW = sb("xraw", [128, 40960], BF16)
        ABUF = sb("abuf", [128, 20480], BF16)
        WAR = sb("war", [128, 16384], BF16)
        FAR = sb("far", [128, 4096], F32)
        BAR = sb("bar", [128, 10240], BF16)
        VEC = sb("vec", [128, NV], F32)
        MOD = sb("mod", [128, 2 * 96], F32)
        AMOD = sb("amod", [128, 64], F32)
        LB = sb("lb", [128, 16], F32)
        OML = sb("oml", [128, 16], F32)
        OGS = sb("ogs", [128, 8], F32)
        FGS = sb("fgs", [128, 8], F32)
        CS = sb("cs", [128, 16], BF16)
        CTF = sb("ctf", [128, 16], F32)
        IDF = sb("idf", [128, 128], F32)
        IDB = sb("idb", [128, 128], BF16)
        ONEB = sb("oneb", [128, 128], BF16)
        ONEF = sb("onef", [128, 128], F32)
        MSKA = sb("mska", [128, 256], F32)
        MSKB = sb("mskb", [128, 256], F32)
        RST = sb("rst", [128, 512], F32)
        S32 = sb("s32", [128, 2, 128], F32)
        SBF = sb("sbf", [128, 4, 128], BF16)
        SBSV = sb("sbsv", [128, 4, 128], F32)
        SINI = sb("sini", [128, 2, 2, 128], F32)
        NSST = sb("nsst", [128, 1, 4, 128], F32)
        SMALL = sb("small", [128, 64], F32)
        EPSC = sb("epsc", [128, 4], F32)
        SMD = sb("smd", [128, 2, 2, 64], F32)
        PSF = [es.enter_context(nc.psum_tensor("psf%d" % i, [128, 512], F32)) for i in range(6)]
        PSB = [es.enter_context(nc.psum_tensor("psb%d" % i, [128, 1024], BF16)) for i in range(2)]
        tPSF = [T("psf%d" % i) for i in range(6)]
        tPSB = [T("psb%d" % i) for i in range(2)]
        rr = {"f": 0, "b": 0}

        pinned = set()

        def run_streams(gens):
            res = [None] * len(gens)
            alive = list(range(len(gens)))
            while alive:
                for gi in list(alive):
                    try:
                        next(gens[gi])
                    except StopIteration as st:
                        res[gi] = st.value
                        alive.remove(gi)
            return res

        def pbank(pin=False):
            i = rr["f"]
            n_try = 0
            while i in pinned:
                i = (i + 1) % 6
                n_try += 1
                assert n_try < 7, "all PSUM banks pinned"
            rr["f"] = (i + 1) % 6
            if pin:
                pinned.add(i)
            return PSF[i], tPSF[i]

        def unpin(tp):
            pinned.discard(tPSF.index(tp))

        def pbankb():
            i = rr["b"]
            rr["b"] = (i + 1) % 2
            return PSB[i], tPSB[i]

        XF = XRAW[:].bitcast(F32).rearrange("p (j t) -> p j t", j=8)
        tX = [[T("x%d_%d" % (j, b)) for b in range(NBLK)] for j in range(KC)]
        XB = lambda j, b: XF[:, j, b * TB:(b + 1) * TB]
        HTO = XRAW[:, 0:20480].rearrange("p (j t) -> p j t", j=8)
        HTH = XRAW[:, 20480:36864].rearrange("p (j t) -> p j t", j=8)
        tHTO = [T("hto%d" % b) for b in range(20)]
        tHTH = [T("hth%d" % b) for b in range(16)]
        AB3 = ABUF[:].rearrange("p (j t) -> p j t", j=8)
        tAB = [[T("ab%d_%d" % (j, b)) for b in range(NBLK)] for j in range(KC)]
        FT = [FAR[:, i * 512:(i + 1) * 512] for i in range(8)]
        tFT = [T("ft%d" % i) for i in range(8)]
        XS = [FAR[:, 0:1024], FAR[:, 1024:2048]]
        tXS = [T("xs0"), T("xs1")]
        BT = [BAR[:, i * 512:(i + 1) * 512] for i in range(20)]
        tBT = [T("bt%d" % i) for i in range(20)]
        tVEC, tMOD, tAMOD, tLB, tCS, tCONST, tSMALL = T(), T(), T(), T(), T(), T(), T()

        S.dma("sp", VEC[:], vecs, writes=[tVEC])
        S.dma("sp", CTF[:], cT, writes=[tCS])
        S.op("pool", lambda e: e.memset(ONEF[:], 1.0), writes=[tCONST])
        S.op("pool", lambda e: e.memset(ONEB[:], 1.0), writes=[tCONST])
        S.op("pool", lambda e: e.memset(EPSC[:, 0:1], EPS), reads=[tCONST], writes=[tCONST])
        S.op("pool", lambda e: e.memset(EPSC[:, 1:2], float(D * EPS)), reads=[tCONST], writes=[tCONST])
        S.op("pool", lambda e: e.memset(EPSC[:, 2:3], float(128 * EPS)), reads=[tCONST], writes=[tCONST])
        S.op("pool", lambda e: e.memset(EPSC[:, 3:4], 1.0), reads=[tCONST], writes=[tCONST])
        S.op("pool", lambda e: e.affine_select(out=IDF[:], in_=ONEF[:], pattern=[[-1, 128]],
                                               compare_op=ALU.is_equal, fill=0.0, base=0, channel_multiplier=1),
             reads=[tCONST], writes=[tCONST])
        S.op("pool", lambda e: e.tensor_copy(out=IDB[:], in_=IDF[:]), reads=[tCONST], writes=[tCONST])
        MD = [MSKA[:, 0:128], MSKB[:, 0:128]]
        MO = [MSKA[:, 128:256], MSKB[:, 128:256]]
        S.op("pool", lambda e: e.affine_select(out=MD[0], in_=ONEF[:], pattern=[[1, 128]],
                                               compare_op=ALU.is_ge, fill=0.0, base=0, channel_multiplier=-1),
             reads=[tCONST], writes=[tCONST])
        S.op("pool", lambda e: e.memset(MD[0][0:32, 32:128], 0.0), reads=[tCONST], writes=[tCONST])
        S.op("pool", lambda e: e.memset(MD[0][32:64, 64:128], 0.0), reads=[tCONST], writes=[tCONST])
        S.op("pool", lambda e: e.memset(MD[0][64:96, 96:128], 0.0), reads=[tCONST], writes=[tCONST])
        S.op("pool", lambda e: e.memset(MO[0], 0.0), reads=[tCONST], writes=[tCONST])
        S.op("pool", lambda e: e.memset(MO[0][0:32, 32:64], 1.0), reads=[tCONST], writes=[tCONST])
        S.op("pool", lambda e: e.memset(MO[0][64:96, 96:128], 1.0), reads=[tCONST], writes=[tCONST])
        pmk, tpmk = pbank()
        S.op("pe", lambda e: e.transpose(out=pmk[:, 0:128], in_=MD[0], identity=IDF[:]), reads=[tCONST], writes=[tpmk])
        S.op("pe", lambda e: e.transpose(out=pmk[:, 128:256], in_=MO[0], identity=IDF[:]), reads=[tCONST], writes=[tpmk])
        S.op("act", lambda e: e.activation(out=MSKB[:, 0:256], in_=pmk[:, 0:256], func=AF.Copy), reads=[tpmk, tCONST], writes=[tCONST])
        S.op("pool", lambda e: e.memset(RST[:], 1.0), reads=[tCONST], writes=[tCONST])
        S.op("pool", lambda e: e.memset(RST[:, 0::32], 0.0), reads=[tCONST], writes=[tCONST])
        S.op("pool", lambda e: e.memset(SMD[:], 1.0), reads=[tCONST], writes=[tCONST])
        S.op("pool", lambda e: e.memset(SBF[:], 0.0), reads=[tCONST], writes=[tCONST])
        S.op("dve", lambda e: e.tensor_tensor(out=LB[:], in0=VEC[:, 144:160], in1=VEC[:, 160:176], op=ALU.subtract),
             reads=[tVEC], writes=[tLB])
        S.op("act", lambda e: e.activation(out=LB[:], in_=LB[:], func=AF.Sigmoid), reads=[tLB], writes=[tLB])
        S.op("dve", lambda e: e.tensor_scalar(out=OML[:], in0=LB[:], scalar1=-1.0, scalar2=1.0, op0=ALU.mult, op1=ALU.add),
             reads=[tLB], writes=[tLB])
        S.op("dve", lambda e: e.tensor_scalar(out=OGS[:], in0=VEC[:, 136:144], scalar1=float(np.sqrt(128.0)), scalar2=None,
                                              op0=ALU.mult), reads=[tVEC], writes=[tLB])
        S.op("dve", lambda e: e.tensor_scalar(out=FGS[:], in0=VEC[:, 32:40], scalar1=32.0, scalar2=None, op0=ALU.mult),
             reads=[tVEC], writes=[tLB])
        S.op("act", lambda e: e.activation(out=CS[:], in_=CTF[:], func=AF.Silu), reads=[tCS], writes=[tCS])

        wslot = [WAR[:, i * 4096:(i + 1) * 4096].rearrange("p (k n) -> p k n", k=8) for i in range(4)]
        tW = [T("w%d" % i) for i in range(4)]
        pmL = [pbank(pin=True), pbank(pin=True)]
        ada_state = {"li": 0}

        def ada_chunk(l, m, hf):
            pm, tpm = pmL[l]
            s_ = ada_state["li"] % 4
            ada_state["li"] += 1
            src = ada_w[l, :, m * 1024 + hf * 512: m * 1024 + (hf + 1) * 512].rearrange("(k p) n -> p k n", p=128)
            S.dma("pool", wslot[s_], src, writes=[tW[s_]])
            for jj in range(4):
                col = (m * 8 + hf * 4 + jj) * 2
                for k in range(8):
                    S.op("pe", lambda e, s_=s_, jj=jj, k=k, col=col, pm=pm: e.matmul(
                        pm[:, col:col + 2], lhsT=wslot[s_][:, k, jj * 128:(jj + 1) * 128], rhs=CS[:, k * 2:k * 2 + 2],
                        start=(k == 0), stop=(k == 7)), reads=[tW[s_], tCS], writes=[tpm])

        tMODm = [[T() for _ in range(6)] for _ in range(2)]

        def ada_finish(l, m):
            pm, tpm = pmL[l]
            mo = MOD[:, l * 96 + m * 16: l * 96 + (m + 1) * 16].rearrange("p (c o) -> p c o", o=2)
            ab = VEC[:, 40 + l * 48 + m * 8: 40 + l * 48 + (m + 1) * 8].unsqueeze(2).broadcast_to([128, 8, 2])
            S.op("dve", lambda e: e.tensor_tensor(out=mo, in0=pm[:, m * 16:(m + 1) * 16].rearrange("p (c o) -> p c o", o=2),
                                                  in1=ab, op=ALU.add), reads=[tpm, tVEC], writes=[tMOD, tMODm[l][m]])

        def modv(l, m, j, cond):
            c = l * 96 + (m * 8 + j) * 2 + cond
            return MOD[:, c:c + 1]

        def amod(l, nrm, j, cond):
            c = ((l * 2 + nrm) * 8 + j) * 2 + cond
            return AMOD[:, c:c + 1]

        def amod_finish(l, nrm):
            scm = 1 + 3 * nrm
            base = l * 96 + scm * 16
            gcol = l * 16 + nrm * 8
            o = AMOD[:, (l * 2 + nrm) * 16:(l * 2 + nrm + 1) * 16].rearrange("p (j o) -> p j o", o=2)
            i0 = MOD[:, base:base + 16].rearrange("p (j o) -> p j o", o=2)
            gg = VEC[:, gcol:gcol + 8].unsqueeze(2).broadcast_to([128, 8, 2])
            S.op("dve", lambda e: e.tensor_scalar(out=o, in0=i0, scalar1=1.0, scalar2=None, op0=ALU.add),
                 reads=[tMOD], writes=[tAMOD])
            fac = 1.0 if (l == 0 and nrm == 0) else 32.0
            S.op("dve", lambda e: e.scalar_tensor_tensor(out=o, in0=o, scalar=fac, in1=gg, op0=ALU.mult, op1=ALU.mult),
                 reads=[tAMOD, tVEC], writes=[tAMOD])

        for (m_, hf_) in ((0, 0), (0, 1), (1, 0), (1, 1)):
            ada_chunk(0, m_, hf_)
        ada_finish(0, 0)
        ada_finish(0, 1)
        amod_finish(0, 0)
        ada_rest = [(0, m_, hf_) for m_ in range(2, 6) for hf_ in range(2)] + [(1, m_, hf_) for m_ in range(6) for hf_ in range(2)]

        tSM2 = [T(), T()]

        def norm_tokmajor(src_rows, tt, dstv, tdst, cond):
            s = tt % 2
            S.dma("sp", XS[s], src_rows, writes=[tXS[s]])
            tjunk = tBT[8 + 2 * s]
            junk = BAR[:, (8 + 2 * s) * 512:(10 + 2 * s) * 512]
            ss = SMALL[:, s * 4:s * 4 + 1]
            rs = SMALL[:, s * 4 + 1:s * 4 + 2]
            S.op("act", lambda e: e.activation(out=junk, in_=XS[s], func=AF.Square, accum_out=ss),
                 reads=[tXS[s]], writes=[tjunk, tSM2[s]])
            yield
            S.op("act", lambda e: e.activation(out=rs, in_=ss, func=AF.Sqrt, scale=1.0 / D, bias=EPSC[:, 0:1]),
                 reads=[tSM2[s], tCONST], writes=[tSM2[s]])
            yield
            S.op("dve", lambda e: e.reciprocal(out=rs, in_=rs), reads=[tSM2[s]], writes=[tSM2[s]])
            yield
            S.op("dve", lambda e: e.tensor_scalar(out=junk, in0=XS[s], scalar1=rs, scalar2=None, op0=ALU.mult),
                 reads=[tXS[s], tSM2[s], tjunk], writes=[tjunk])
            yield
            pb, tpb = PSB[s], tPSB[s]
            for j in range(8):
                S.op("pe", lambda e, j=j: e.transpose(out=pb[:, j * 128:(j + 1) * 128], in_=junk[:, j * 128:(j + 1) * 128],
                                                      identity=IDB[:]), reads=[tjunk, tCONST], writes=[tpb])
            yield
            for j in range(8):
                S.op("act", lambda e, j=j: e.activation(out=dstv(j), in_=pb[:, j * 128:(j + 1) * 128], func=AF.Identity,
                                                        scale=amod(0, 0, j, cond), bias=modv(0, 0, j, cond)),
                     reads=[tpb, tAMOD, tMOD], writes=[tdst])
                if j % 4 == 3:
                    yield

        def norm_featmajor(l, nrm, b, cond, dstv, tdsts):
            pr, tpr = pbank()
            for j in range(8):
                sq, tsq = BT[8 + j % 4], tBT[8 + j % 4]
                S.op("act", lambda e, j=j, sq=sq: e.activation(out=sq, in_=XB(j, b), func=AF.Square),
                     reads=[tX[j][b]], writes=[tsq])
                S.op("pe", lambda e, j=j, sq=sq: e.matmul(pr[:], lhsT=ONEB[:], rhs=sq, start=(j == 0), stop=(j == 7)),
                     reads=[tsq, tCONST], writes=[tpr])
            rs, trs = FT[7], tFT[7]
            S.op("act", lambda e: e.activation(out=rs, in_=pr[:], func=AF.Ln, bias=EPSC[:, 1:2]), reads=[tpr, tCONST], writes=[trs])
            S.op("act", lambda e: e.activation(out=rs, in_=rs, func=AF.Exp, scale=-0.5), reads=[trs], writes=[trs])
            for j in range(8):
                tmp, ttmp = FT[4 + j % 3], tFT[4 + j % 3]
                S.op("dve", lambda e, j=j, tmp=tmp: e.tensor_tensor(out=tmp, in0=XB(j, b), in1=rs, op=ALU.mult),
                     reads=[tX[j][b], trs], writes=[ttmp])
                sh = modv(l, 3 * nrm, j, cond)
                S.op("act", lambda e, j=j, tmp=tmp, sh=sh: e.activation(out=dstv(j), in_=tmp, func=AF.Identity,
                                                                        scale=amod(l, nrm, j, cond), bias=sh),
                     reads=[ttmp, tAMOD, tMOD], writes=[tdsts[j]])

        def resid_add(l, m, oc, b, cond, ps, tps):
            S.op("dve", lambda e: e.scalar_tensor_tensor(out=XB(oc, b), in0=ps, scalar=modv(l, m, oc, cond), in1=XB(oc, b),
                                                         op0=ALU.mult, op1=ALU.add),
                 reads=[tps, tMOD, tX[oc][b]], writes=[tX[oc][b]])

        def dump_debug():
            if DEBUG:
                S.barrier()
                td = T()
                S.dma("sp", dbgx, XRAW[:], writes=[td])
                S.dma("sp", dbga, ABUF[:], writes=[td])

        tiles0b = []
        for tt in range(20):
            cond = 0 if tt < 4 else 1
            tiles0b.append((xo[tt * 128:(tt + 1) * 128, :], tt, (lambda j, tt=tt: HTO[:, j, tt * 128:(tt + 1) * 128]), tHTO[tt], cond))
        for tt in range(16):
            tiles0b.append((xh[tt * 128:(tt + 1) * 128, :], tt, (lambda j, tt=tt: HTH[:, j, tt * 128:(tt + 1) * 128]), tHTH[tt], 1))
        for pi in range(0, 36, 2):
            run_streams([norm_tokmajor(*tiles0b[pi]), norm_tokmajor(*tiles0b[pi + 1])])
            if ada_rest:
                ada_chunk(*ada_rest.pop(0))
            if ada_rest and pi % 4 == 0:
                ada_chunk(*ada_rest.pop(0))
        while ada_rest:
            ada_chunk(*ada_rest.pop(0))
        for l_ in range(2):
            for m_ in range(6):
                if not (l_ == 0 and m_ < 2):
                    ada_finish(l_, m_)
        amod_finish(0, 1)
        amod_finish(1, 0)
        amod_finish(1, 1)
        unpin(pmL[0][1])
        unpin(pmL[1][1])
        S.barrier()

        tS32 = [T("s32a"), T("s32b")]
        tSBF = [[T("sbfa0"), T("sbfa1")], [T("sbfb0"), T("sbfb1")]]
        tSBSV, tSINI, tNSST = T(), [T(), T()], [T(), T()]
        whs = [WAR[:, i * 5120:(i + 1) * 5120].rearrange("p (k n) -> p k n", k=8) for i in range(2)]
        tWH = [T("wh0"), T("wh1")]
        dF = [FT[0], FT[1]]; tdF = [tFT[0], tFT[1]]
        dG = [FT[2], FT[3]]; tdG = [tFT[2], tFT[3]]
        dB = [FT[4], FT[5]]; tdB = [tFT[4], tFT[5]]
        QF, tQF = FT[6], tFT[6]
        dQB = [BT[0], BT[1]]; tdQB = [tBT[0], tBT[1]]
        dKM = [BT[2], BT[3]]; tdKM = [tBT[2], tBT[3]]
        dK2 = [BT[4], BT[5]]; tdK2 = [tBT[4], tBT[5]]
        dKT = [BT[6], BT[7]]; tdKT = [tBT[6], tBT[7]]
        dKTo = [BT[12], BT[13]]; tdKTo = [tBT[12], tBT[13]]
        dKH = [BT[14], BT[15]]; tdKH = [tBT[14], tBT[15]]
        dAT2 = [BT[16], BT[17]]; tdAT2 = [tBT[16], tBT[17]]
        tSMD = [[T("smd00"), T("smd01")], [T("smd10"), T("smd11")]]
        E2 = [WAR[:, 10240 + i * 1024: 10240 + (i + 1) * 1024].bitcast(F32) for i in range(2)]
        tE2 = [T("e2a"), T("e2b")]
        for d_ in range(2):
            S.op("pool", lambda e, d_=d_: e.memset(dKT[d_][64:128, :], 0.0), writes=[tdKT[d_]])
            S.op("pool", lambda e, d_=d_: e.memset(dKTo[d_][0:64, :], 0.0), writes=[tdKTo[d_]])
        dAT = [BT[8], BT[9]]; tdAT = [tBT[8], tBT[9]]
        ZGSs, tZGSs = [BT[10], BT[18]], [tBT[10], tBT[18]]
        VTKs, tVTKs = [BT[11], BT[19]], [tBT[11], tBT[19]]
        VTK3s = [v.rearrange("p (i v) -> p i v", i=4) for v in VTKs]

        def hg_proj(wh, twh, htv, thts, col, ps, tps):
            for k in range(8):
                S.op("pe", lambda e, k=k: e.matmul(ps[:], lhsT=wh[:, k, col * 128:(col + 1) * 128], rhs=htv(k),
                                                   start=(k == 0), stop=(k == 7)), reads=[twh] + thts, writes=[tps])

        def hg_vtok(wh, twh, httile, thts, vpar):
            VTK, tVTK = VTKs[vpar], tVTKs[vpar]
            ps, tps = pbank()
            for i in range(4):
                for k in range(8):
                    S.op("pe", lambda e, i=i, k=k: e.matmul(ps[:, i * 128:(i + 1) * 128], lhsT=httile(k, i), rhs=wh[:, k, 512:640],
                                                            start=(k == 0), stop=(k == 7)), reads=[twh] + thts, writes=[tps])
            S.op("act", lambda e: e.activation(out=VTK, in_=ps[:], func=AF.Copy), reads=[tps], writes=[tVTK])

        def hg_prep(d, h, zps, tzps, need_o, par=0, vpar=0, ts=None):
            if ts is None:
                ts = d
            F, tF, G, tG, Bc, tB = dF[ts], tdF[ts], dG[ts], tdG[ts], dB[ts], tdB[ts]
            VTK3, tVTK = VTK3s[vpar], tVTKs[vpar]
            lbc = LB[:, d * 8 + h:d * 8 + h + 1]
            omc = OML[:, d * 8 + h:d * 8 + h + 1]
            one1 = EPSC[:, 3:4]
            S.op("act", lambda e: e.activation(out=F, in_=zps[:], func=AF.Exp, scale=-1.0), reads=[tzps], writes=[tF])
            if tzps in tPSF:
                unpin(tzps)
            yield
            S.op("act", lambda e: e.activation(out=G, in_=F, func=AF.Ln, scale=lbc, bias=one1), reads=[tF, tLB, tCONST], writes=[tG])
            yield
            S.op("act", lambda e: e.activation(out=Bc, in_=F, func=AF.Ln, bias=one1), reads=[tF, tCONST], writes=[tB])
            yield
            S.op("dve", lambda e: e.tensor_tensor(out=G, in0=G, in1=Bc, op=ALU.subtract), reads=[tG, tB], writes=[tG])
            yield
            XK, tXK = E2[ts], tE2[ts]
            S.op("act", lambda e: e.activation(out=XK, in_=Bc, func=AF.Exp, scale=-1.0), reads=[tB, tXK], writes=[tXK])
            yield
            v16 = lambda ap: ap.rearrange("p (c t) -> p c t", t=32)
            if d == 0:
                S.op("dve", lambda e: e.tensor_tensor_scan(out=Bc, data0=RST[:], data1=G, initial=0.0, op0=ALU.mult, op1=ALU.add),
                     reads=[tG, tCONST], writes=[tB])
                lastc = 31
            else:
                S.op("dve", lambda e: e.tensor_tensor_scan(out=Bc[:, ::-1], data0=RST[:], data1=G[:, ::-1], initial=0.0,
                                                           op0=ALU.mult, op1=ALU.add), reads=[tG, tCONST], writes=[tB])
                lastc = 0
            yield
            S.op("dve", lambda e: e.scalar_tensor_tensor(out=F, in0=F, scalar=omc, in1=XK, op0=ALU.mult, op1=ALU.mult),
                 reads=[tF, tXK, tLB], writes=[tF])
            yield
            bbT, tbb, X, tX_ = Bc, tB, G, tG
            QB, tQB, KM, tKM, KT, tKT, AT, tAT = (dQB[ts], tdQB[ts], dKM[ts], tdKM[ts], dKT[ts], tdKT[ts], dAT[ts], tdAT[ts])
            QD, tQD, KH, tKH, AT2, tAT2 = dK2[ts], tdK2[ts], dKH[ts], tdKH[ts], dAT2[ts], tdAT2[ts]
            sm = SMD[:, ts, par, :]
            EDC, EC, EH, EH2 = sm[:, 0:16], sm[:, 16:24], sm[:, 32:48], sm[:, 48:64]
            tsm = tSMD[ts][par]
            X2, tX2 = E2[ts], tE2[ts]
            S.op("act", lambda e: e.activation(out=EDC, in_=bbT[:, lastc::32], func=AF.Exp), reads=[tbb, tsm], writes=[tsm])
            yield
            S.op("dve", lambda e: e.tensor_tensor(out=EC, in0=EDC[:, 0::2], in1=EDC[:, 1::2], op=ALU.mult), reads=[tsm], writes=[tsm])
            yield
            if d == 0:
                if need_o:
                    S.op("dve", lambda e: e.tensor_copy(out=EH[:, 1::2], in_=EDC[:, 0::2]), reads=[tsm], writes=[tsm])
                    yield
                S.op("dve", lambda e: e.tensor_copy(out=EH2[:, 0::2], in_=EDC[:, 1::2]), reads=[tsm], writes=[tsm])
                yield
            else:
                if need_o:
                    S.op("dve", lambda e: e.tensor_copy(out=EH[:, 0::2], in_=EDC[:, 1::2]), reads=[tsm], writes=[tsm])
                    yield
                S.op("dve", lambda e: e.tensor_copy(out=EH2[:, 1::2], in_=EDC[:, 0::2]), reads=[tsm], writes=[tsm])
                yield
            if need_o:
                S.op("act", lambda e: e.activation(out=X2, in_=bbT, func=AF.Exp, scale=-1.0), reads=[tbb, tX2], writes=[tX2])
                yield
                S.op("dve", lambda e: e.tensor_tensor(out=KM, in0=F, in1=X2, op=ALU.mult), reads=[tF, tX2], writes=[tKM])
                yield
                S.op("act", lambda e: e.activation(out=X, in_=bbT, func=AF.Exp), reads=[tbb, tX_], writes=[tX_])
                yield
                Y, tY = X2, tX2
            else:
                Y, tY = X, tX_
            blh = bbT[:, lastc::32].unsqueeze(2).broadcast_to([128, 16, 32])
            S.op("dve", lambda e: e.tensor_tensor(out=v16(Y), in0=blh, in1=v16(bbT), op=ALU.subtract),
                 reads=[tbb, tY], writes=[tY])
            yield
            S.op("act", lambda e: e.activation(out=Y, in_=Y, func=AF.Exp), reads=[tY], writes=[tY])
            yield
            S.op("dve", lambda e: e.tensor_tensor(out=KH, in0=F, in1=Y, op=ALU.mult), reads=[tF, tY], writes=[tKH])
            yield
            K2, tK2 = AT2, tAT2
            S.op("dve", lambda e: e.tensor_tensor(out=v16(K2), in0=v16(KH), in1=EH2.unsqueeze(2).broadcast_to([128, 16, 32]),
                                                  op=ALU.mult), reads=[tKH, tsm], writes=[tK2])
            yield
            if need_o:
                S.op("pool", lambda e: e.tensor_tensor(out=QD, in0=QF, in1=X, op=ALU.mult), reads=[tQF, tX_, tK2], writes=[tQD])
                yield
                S.op("pool", lambda e: e.tensor_tensor(out=v16(QB), in0=v16(QD), in1=EH.unsqueeze(2).broadcast_to([128, 16, 32]),
                                                       op=ALU.mult), reads=[tQD, tsm], writes=[tQB])
                yield
            if KP < 3:
                return None
            pb, tpb = pbankb()
            for i in range(4):
                S.op("pe", lambda e, i=i: e.transpose(out=pb[:, i * 128:(i + 1) * 128], in_=K2[:, i * 128:(i + 1) * 128],
                                                      identity=IDB[:]), reads=[tK2, tCONST], writes=[tpb])
                yield
            KTo, tKTo = dKTo[ts], tdKTo[ts]
            S.op("act", lambda e: e.activation(out=KT[0:64, :], in_=pb[0:64, 0:512], func=AF.Copy), reads=[tpb, tKT], writes=[tKT])
            yield
            S.op("dve", lambda e: e.tensor_copy(out=KTo[64:128, :], in_=pb[64:128, 0:512]), reads=[tpb, tKTo], writes=[tKTo])
            yield
            if need_o:
                pa, tpa = pbank(pin=True)
                pa2, tpa2 = pbank(pin=True)
                for i in range(4):
                    S.op("pe", lambda e, i=i: e.matmul(pa[:, i * 128:(i + 1) * 128], lhsT=KM[:, i * 128:(i + 1) * 128],
                                                       rhs=QD[:, i * 128:(i + 1) * 128], start=True, stop=True),
                         reads=[tKM, tQD], writes=[tpa])
                    yield
                for i in range(4):
                    S.op("pe", lambda e, i=i: e.matmul(pa2[:, i * 128:(i + 1) * 128], lhsT=KH[:, i * 128:(i + 1) * 128],
                                                       rhs=QD[:, i * 128:(i + 1) * 128], start=True, stop=True),
                         reads=[tKH, tQD], writes=[tpa2])
                    yield
                v4 = lambda ap: ap.rearrange("p (i t) -> p i t", i=4)
                S.op("dve", lambda e: e.tensor_tensor(out=v4(AT), in0=v4(pa[:]), in1=MD[d].unsqueeze(1).broadcast_to([128, 4, 128]),
                                                      op=ALU.mult), reads=[tpa, tCONST], writes=[tAT])
                yield
                S.op("dve", lambda e: e.tensor_tensor(out=v4(AT2), in0=v4(pa2[:]), in1=MO[d].unsqueeze(1).broadcast_to([128, 4, 128]),
                                                      op=ALU.mult), reads=[tpa2, tCONST, tKT, tKTo], writes=[tAT2])
                unpin(tpa)
                unpin(tpa2)
                yield
            if KP < 4:
                return None
            KT3 = [KT.rearrange("p (i k) -> p i k", i=4), KTo.rearrange("p (i k) -> p i k", i=4)]
            pp = [pbank(pin=True), pbank(pin=True)]
            for c in range(8):
                i, r = c // 2, c % 2
                ps, tps = pp[c // 4]
                cc = c % 4
                S.op("pe", lambda e, i=i, r=r, ps=ps, cc=cc: e.matmul(ps[:, cc * 128:(cc + 1) * 128], lhsT=KT3[r][:, i, :],
                                                                      rhs=VTK3[:, i, :], start=True, stop=True),
                     reads=[tKT, tKTo, tVTK], writes=[tps])
                yield
            return dict(d=d, EC=EC, tsm=tsm, QB=QB, tQB=tQB, AT=AT, tAT=tAT, AT2=AT2, tAT2=tAT2, pp=pp, VTK3=VTK3, tVTK=tVTK)

        def hg_chain(ctxs, po, tpo, need_o, resets=(), saves=None, nsbuf=0):
            if KP < 5 or any(c is None for c in ctxs):
                return
            yield
            if need_o:
                first = True
                for cx in ctxs:
                    for (nm, tnm) in (("AT", "tAT"), ("AT2", "tAT2")):
                        AT3 = cx[nm].rearrange("p (i t) -> p i t", i=4)
                        for i in range(4):
                            S.op("pe", lambda e, i=i, AT3=AT3, first=first: e.matmul(
                                po[:, i * 128:(i + 1) * 128], lhsT=cx["VTK3"][:, i, :], rhs=AT3[:, i, :], start=first, stop=False,
                                skip_group_check=True), reads=[cx["tVTK"], cx[tnm]], writes=[tpo])
                            first = False
            nd = len(ctxs)
            for step in range(8):
                for xi, cx in enumerate(ctxs):
                    d = cx["d"]
                    c = step if d == 0 else 7 - step
                    sbi = step % 2
                    sv = S32[:, d, :]
                    if (d, c) in resets:
                        S.op("pool", lambda e, sv=sv: e.memset(sv, 0.0), reads=[tS32[d]], writes=[tS32[d]])
                        S.op("pool", lambda e, d=d, sbi=sbi: e.memset(SBF[:, d * 2 + sbi, :], 0.0),
                             reads=[tSBF[d][sbi]], writes=[tSBF[d][sbi]])
                    if need_o:
                        last = (xi == nd - 1)
                        S.op("pe", lambda e, d=d, c=c, sbi=sbi, cx=cx, last=last: e.matmul(
                            po[:, c * 64:(c + 1) * 64], lhsT=SBF[:, d * 2 + sbi, :], rhs=cx["QB"][:, c * 64:(c + 1) * 64],
                            start=False, stop=last, skip_group_check=True),
                            reads=[tSBF[d][sbi], cx["tQB"]], writes=[tpo])
                    ps, tps = cx["pp"][c // 4]
                    cc = c % 4
                    esc = cx["EC"][:, c:c + 1]
                    S.op("dve", lambda e, sv=sv, ps=ps, cc=cc, esc=esc: e.scalar_tensor_tensor(
                        out=sv, in0=sv, scalar=esc, in1=ps[:, cc * 128:(cc + 1) * 128], op0=ALU.mult, op1=ALU.add),
                        reads=[tS32[d], cx["tsm"], tps], writes=[tS32[d]])
                    if need_o:
                        nsb = (step + 1) % 2
                        if d == 0:
                            S.op("pool", lambda e, sv=sv, d=d, nsb=nsb: e.tensor_copy(out=SBF[:, d * 2 + nsb, :], in_=sv),
                                 reads=[tS32[d]], writes=[tSBF[d][nsb]])
                        else:
                            S.op("act", lambda e, sv=sv, d=d, nsb=nsb: e.activation(out=SBF[:, d * 2 + nsb, :], in_=sv, func=AF.Copy),
                                 reads=[tS32[d]], writes=[tSBF[d][nsb]])
                    if saves and (d, c) in saves:
                        slot = saves[(d, c)]
                        S.op("pool", lambda e, sv=sv, slot=slot: e.tensor_copy(out=NSST[:, nsbuf, slot, :], in_=sv),
                             reads=[tS32[d]], writes=[tNSST[nsbuf]])
                    yield
            for cx in ctxs:
                for (_, tp) in cx["pp"]:
                    unpin(tp)

        for h in range(int(os.environ.get("KHEADS", "8"))):
            if STAGE < 2:
                break
            wsl = h % 2
            wh, twh = whs[wsl], tWH[wsl]
            if h == 0:
                S.dma("pool", wh, w_hg[h].rearrange("(k p) n -> p k n", p=128), writes=[twh])
            S.dma("sp", SINI[:, wsl, 0, :], sa0[h], writes=[tSINI[wsl]])
            S.dma("sp", SINI[:, wsl, 1, :], sb0[h], writes=[tSINI[wsl]])
            S.op("act", lambda e, wsl=wsl: e.activation(out=S32[:, 1, :], in_=SINI[:, wsl, 1, :], func=AF.Copy),
                 reads=[tSINI[wsl], tS32[1]], writes=[tS32[1]])
            seqB = [("h", b) for b in (3, 2, 1, 0)] + [("o", b) for b in (4, 3, 2)]

            def chain_b(cx, tgt):
                yield from hg_chain([cx], None, None, False)
                if tgt is not None:
                    S.op("pool", lambda e, tgt=tgt: e.tensor_copy(out=SBSV[:, tgt - 1, :], in_=S32[:, 1, :]),
                         reads=[tS32[1], tSBSV], writes=[tSBSV])
                yield

            def proj_pre(kb, vpar):
                kind, b = kb
                if kind == "h":
                    htv = lambda k, b=b: HTH[:, k, b * 512:(b + 1) * 512]
                    htt = lambda k, i, b=b: HTH[:, k, b * 512 + i * 128: b * 512 + (i + 1) * 128]
                    thts = tHTH[b * 4:(b + 1) * 4]
                else:
                    htv = lambda k, b=b: HTO[:, k, b * 512:(b + 1) * 512]
                    htt = lambda k, i, b=b: HTO[:, k, b * 512 + i * 128: b * 512 + (i + 1) * 128]
                    thts = tHTO[b * 4:(b + 1) * 4]
                zps, tzps = pbank(pin=True)
                hg_proj(wh, twh, htv, thts, 2, zps, tzps)
                hg_vtok(wh, twh, htt, thts, vpar)
                return zps, tzps

            def tgt_of(kb):
                kind, b = kb
                return 4 if kind == "h" and b == 0 else (b - 1 if kind == "o" else None)

            idx = 0
            while idx < len(seqB):
                grp = seqB[idx:idx + 2]
                zs = [proj_pre(kb, gi) for gi, kb in enumerate(grp)]
                gens = [hg_prep(1, h, zs[gi][0], zs[gi][1], False, par=gi, vpar=gi, ts=(1 if gi == 0 else 0)) for gi in range(len(grp))]
                cxs = run_streams(gens)
                for gi, kb in enumerate(grp):
                    run_streams([chain_b(cxs[gi], tgt_of(kb))])
                idx += 2
            if h + 1 < 8:
                S.dma("pool", whs[(h + 1) % 2], w_hg[h + 1].rearrange("(k p) n -> p k n", p=128), writes=[tWH[(h + 1) % 2]])
            def proj_shared(b, vpar):
                htv = lambda k, b=b: HTO[:, k, b * 512:(b + 1) * 512]
                htt = lambda k, i, b=b: HTO[:, k, b * 512 + i * 128: b * 512 + (i + 1) * 128]
                thts = tHTO[b * 4:(b + 1) * 4]
                zq, tzq = pbank()
                hg_proj(wh, twh, htv, thts, 0, zq, tzq)
                S.op("act", lambda e, zq=zq: e.activation(out=QF, in_=zq[:], func=AF.Silu), reads=[tzq], writes=[tQF])
                zg, tzg = pbank()
                hg_proj(wh, twh, htv, thts, 3, zg, tzg)
                S.op("act", lambda e, zg=zg: e.activation(out=ZGSs[vpar], in_=zg[:], func=AF.Silu), reads=[tzg], writes=[tZGSs[vpar]])
                hg_vtok(wh, twh, htt, thts, vpar)

            proj_shared(0, 0)
            for b in range(NBLK if KP >= 6 else 0):
                vpar = b % 2
                htv = lambda k, b=b: HTO[:, k, b * 512:(b + 1) * 512]
                thts = tHTO[b * 4:(b + 1) * 4]
                za, tza = pbank(pin=True)
                hg_proj(wh, twh, htv, thts, 1, za, tza)
                zb, tzb = pbank(pin=True)
                hg_proj(wh, twh, htv, thts, 2, zb, tzb)
                cxa, cxb = run_streams([hg_prep(0, h, za, tza, True, vpar=vpar), hg_prep(1, h, zb, tzb, True, vpar=vpar)])
                resets, saves = (), None
                if b == 0:
                    resets = ((0, 0), (0, 4), (1, 7), (1, 3))
                    saves = {(0, 3): 0, (1, 0): 1, (0, 7): 2, (1, 4): 3}
                elif b == 1:
                    S.op("act", lambda e, wsl=wsl: e.activation(out=S32[:, 0, :], in_=SINI[:, wsl, 0, :], func=AF.Copy),
                         reads=[tSINI[wsl], tS32[0]], writes=[tS32[0]])
                    S.op("act", lambda e: e.activation(out=SBF[:, 0, :], in_=S32[:, 0, :], func=AF.Copy),
                         reads=[tS32[0], tSBF[0][0]], writes=[tSBF[0][0]])
                if b >= 1:
                    S.op("act", lambda e, b=b: e.activation(out=S32[:, 1, :], in_=SBSV[:, b - 1, :], func=AF.Copy),
                         reads=[tSBSV, tS32[1]], writes=[tS32[1]])
                    S.op("act", lambda e: e.activation(out=SBF[:, 2, :], in_=S32[:, 1, :], func=AF.Copy),
                         reads=[tS32[1], tSBF[1][0]], writes=[tSBF[1][0]])
                po, tpo = pbank(pin=True)
                if b + 1 < NBLK:
                    proj_shared(b + 1, (b + 1) % 2)
                run_streams([hg_chain([cxa, cxb], po, tpo, True, resets, saves, nsbuf=0)])
                if b == 0:
                    S.dma("sp", ns[:, :, h, :, :].rearrange("s d k v -> k (s d) v"), NSST[:, 0, :, :],
                          reads=[tNSST[0]], writes=[T()])
                osq, tosq = BT[9], tBT[9]
                S.op("act", lambda e, po=po, osq=osq: e.activation(out=osq, in_=po[:], func=AF.Square), reads=[tpo], writes=[tosq])
                pr, tpr = pbank()
                S.op("pe", lambda e, pr=pr, osq=osq: e.matmul(pr[:], lhsT=ONEB[:], rhs=osq, start=True, stop=True),
                     reads=[tosq, tCONST], writes=[tpr])
                rs, trs = FT[7], tFT[7]
                S.op("act", lambda e, pr=pr, rs=rs: e.activation(out=rs, in_=pr[:], func=AF.Ln, bias=EPSC[:, 2:3]),
                     reads=[tpr, tCONST], writes=[trs])
                S.op("act", lambda e, rs=rs: e.activation(out=rs, in_=rs, func=AF.Exp, scale=-0.5), reads=[trs], writes=[trs])
                S.op("dve", lambda e, po=po, rs=rs: e.tensor_tensor(out=rs, in0=po[:], in1=rs, op=ALU.mult),
                     reads=[tpo, trs], writes=[trs])
                S.op("dve", lambda e, rs=rs, h=h, b=b, vpar=vpar: e.scalar_tensor_tensor(
                    out=AB3[:, h, b * 512:(b + 1) * 512], in0=rs, scalar=OGS[:, h:h + 1], in1=ZGSs[vpar], op0=ALU.mult, op1=ALU.mult),
                    reads=[trs, tLB, tZGSs[vpar]], writes=[tAB[h][b]])
                unpin(tpo)
        S.barrier()
        if STAGE == 2:
            dump_debug()

        def load_weight(dst, tdst, src):
            S.dma("pool", dst, src, writes=[tdst])

        if STAGE >= 3:
            WO = WAR[:, 0:8192].rearrange("p (k n) -> p k n", k=8)
            tWO = T("wo")
            load_weight(WO, tWO, w_ho.rearrange("(k p) n -> p k n", p=128))
            for tt in range(20):
                s = tt % 2
                b = tt // 4
                S.dma("sp", XS[s], xo[tt * 128:(tt + 1) * 128, :], writes=[tXS[s]])
                for hf in range(2):
                    ps, tps = pbank()
                    for jj in range(4):
                        j = hf * 4 + jj
                        S.op("pe", lambda e, jj=jj, j=j, s=s, ps=ps: e.transpose(
                            out=ps[:, jj * 128:(jj + 1) * 128], in_=XS[s][:, j * 128:(j + 1) * 128], identity=IDF[:]),
                            reads=[tXS[s], tCONST], writes=[tps])
                    S.op("act", lambda e, ps=ps, hf=hf, tt=tt: e.activation(
                        out=XF[:, hf * 4:(hf + 1) * 4, tt * 128:(tt + 1) * 128],
                        in_=ps[:].rearrange("p (j t) -> p j t", j=4), func=AF.Copy),
                        reads=[tps], writes=[tX[hf * 4 + jj][b] for jj in range(4)])
            for b in range(NBLK):
                cond = 0 if b == 0 else 1
                for oc in range(8):
                    ps, tps = pbank()
                    for k in range(8):
                        S.op("pe", lambda e, k=k, oc=oc, b=b, ps=ps: e.matmul(
                            ps[:], lhsT=WO[:, k, oc * 128:(oc + 1) * 128], rhs=AB3[:, k, b * 512:(b + 1) * 512],
                            start=(k == 0), stop=(k == 7)), reads=[tWO, tAB[k][b]], writes=[tps])
                    resid_add(0, 2, oc, b, cond, ps[:], tps)
            S.barrier()
        if STAGE == 3:
            dump_debug()

        def mlp(l):
            for b in range(NBLK):
                cond = 0 if b == 0 else 1
                norm_featmajor(l, 1, b, cond, lambda j, b=b: AB3[:, j, b * 512:(b + 1) * 512], [tAB[j][b] for j in range(8)])
            W1 = [WAR[:, i * 4096:(i + 1) * 4096].rearrange("p (k n) -> p k n", k=8) for i in range(2)]
            W2 = [WAR[:, 8192 + i * 4096: 8192 + (i + 1) * 4096].rearrange("p (k n) -> p k n", k=4) for i in range(2)]
            tW1 = [T(), T()]
            tW2 = [T(), T()]
            HID = [FAR[:, i * 1024:(i + 1) * 1024].bitcast(BF16).rearrange("p (c t) -> p c t", c=4) for i in range(2)]
            tHID = [[T() for _ in range(4)] for _ in range(2)]
            it = 0
            def ld_mlp(G_):
                s_ = G_ % 2
                S.dma("pool", W1[s_], w1[l, :, G_ * 512:(G_ + 1) * 512].rearrange("(k p) n -> p k n", p=128), writes=[tW1[s_]])
                S.dma("pool", W2[s_], w2[l, G_ * 512:(G_ + 1) * 512, :].rearrange("(k p) n -> p k n", p=128), writes=[tW2[s_]])

            ld_mlp(0)
            for G in range(8):
                s = G % 2
                if G + 1 < 8:
                    ld_mlp(G + 1)
                for b in range(NBLK):
                    cond = 0 if b == 0 else 1
                    hs = it % 2
                    it += 1
                    for hc in range(4):
                        ps, tps = pbank()
                        for k in range(8):
                            S.op("pe", lambda e, k=k, hc=hc, b=b, ps=ps, s=s: e.matmul(
                                ps[:], lhsT=W1[s][:, k, hc * 128:(hc + 1) * 128], rhs=AB3[:, k, b * 512:(b + 1) * 512],
                                start=(k == 0), stop=(k == 7)), reads=[tW1[s], tAB[k][b]], writes=[tps])
                        r, tr = BT[hc % 4], tBT[hc % 4]
                        S.op("act", lambda e, ps=ps, r=r: e.activation(out=r, in_=ps[:], func=AF.Relu), reads=[tps], writes=[tr])
                        S.op("pool", lambda e, r=r, hs=hs, hc=hc: e.tensor_tensor(out=HID[hs][:, hc, :], in0=r, in1=r, op=ALU.mult),
                             reads=[tr], writes=[tHID[hs][hc]])
                    for oc in range(8):
                        ps, tps = pbank()
                        for k in range(4):
                            S.op("pe", lambda e, k=k, oc=oc, ps=ps, s=s, hs=hs: e.matmul(
                                ps[:], lhsT=W2[s][:, k, oc * 128:(oc + 1) * 128], rhs=HID[hs][:, k, :],
                                start=(k == 0), stop=(k == 3)), reads=[tW2[s], tHID[hs][k]], writes=[tps])
                        resid_add(l, 5, oc, b, cond, ps[:], tps)
            S.barrier()

        if STAGE >= 4:
            mlp(0)
        if STAGE == 4:
            dump_debug()

        if STAGE >= 5:
            CWI = ABUF[:, 0:16384].rearrange("p (k n) -> p k n", k=8)
            HTB = ABUF[:, 16384:20480].rearrange("p (k t) -> p k t", k=8)
            CWO = WAR[:, 0:8192].rearrange("p (k n) -> p k n", k=8)
            WST = WAR[:, 8192:9216].rearrange("p (g t) -> p g t", g=8)
            T2 = WAR[:, 10240:12288].bitcast(F32).rearrange("p (g t) -> p g t", g=8)
            UT = BAR[:, 0:4096].rearrange("p (k t) -> p k t", k=8)
            VN = [BAR[:, 4096:5120], BAR[:, 5120:6144]]
            tCWI, tCWO, tWST, tT2 = T(), T(), T(), T()
            tHTB = [T() for _ in range(8)]
            tUT = tBT[0:8]
            tVNs = [[tBT[8], tBT[9]], [tBT[10], tBT[11]]]
            load_weight(CWI, tCWI, cm_wi.rearrange("(k p) n -> p k n", p=128))
            load_weight(CWO, tCWO, cm_wo.rearrange("(k p) n -> p k n", p=128))
            load_weight(WST, tWST, wsT.rearrange("p (g t) -> p g t", g=8))
            S.dma("sp", XS[0], bsb, writes=[tXS[0]])
            for hf in range(2):
                ps, tps = pbank()
                S.op("pe", lambda e, ps=ps, hf=hf: e.matmul(ps[:], lhsT=ONEB[:], rhs=WAR[:, 8192 + hf * 512: 8192 + (hf + 1) * 512],
                                                            start=True, stop=True), reads=[tWST, tCONST], writes=[tps])
                for gg in range(4):
                    g = hf * 4 + gg
                    S.op("dve", lambda e, ps=ps, gg=gg, g=g: e.scalar_tensor_tensor(
                        out=T2[:, g, :], in0=ps[:, gg * 128:(gg + 1) * 128], scalar=VEC[:, 184 + g:185 + g],
                        in1=XS[0][:, g * 128:(g + 1) * 128], op0=ALU.mult, op1=ALU.add),
                        reads=[tps, tVEC, tXS[0]], writes=[tT2])
            S.barrier()
            for b in range(NBLK):
                cond = 0 if b == 0 else 1
                norm_featmajor(1, 0, b, cond, lambda j: HTB[:, j, :], tHTB)
                for uc in range(8):
                    ps, tps = pbank()
                    for k in range(8):
                        S.op("pe", lambda e, k=k, uc=uc, ps=ps: e.matmul(ps[:], lhsT=CWI[:, k, uc * 128:(uc + 1) * 128], rhs=HTB[:, k, :],
                                                                         start=(k == 0), stop=(k == 7)),
                             reads=[tCWI, tHTB[k]], writes=[tps])
                    S.op("act", lambda e, ps=ps, uc=uc: e.activation(out=UT[:, uc, :], in_=ps[:], func=AF.Gelu_apprx_tanh),
                         reads=[tps], writes=[tUT[uc]])
                def l1_tile(tt, b=b, cond=cond):
                    s = tt % 2
                    vg, tvg = XS[s], tXS[s]
                    st = SMALL[:, 16 + s * 8: 16 + s * 8 + 8]
                    for hf in range(2):
                        ps, tps = pbank()
                        for k in range(8):
                            S.op("pe", lambda e, k=k, hf=hf, tt=tt, ps=ps: e.matmul(
                                ps[:], lhsT=HTB[:, k, tt * 128:(tt + 1) * 128], rhs=CWI[:, k, 1024 + hf * 512: 1024 + (hf + 1) * 512],
                                start=(k == 0), stop=(k == 7)), reads=[tCWI, tHTB[k]], writes=[tps])
                        S.op("act", lambda e, ps=ps, hf=hf, vg=vg, st=st: e.activation(
                            out=vg[:, hf * 512:(hf + 1) * 512], in_=ps[:], func=AF.Gelu_apprx_tanh, accum_out=st[:, hf:hf + 1]),
                            reads=[tps], writes=[tvg, tSM2[s]])
                    S.op("act", lambda e, vg=vg, s=s, st=st: e.activation(out=VN[s], in_=vg, func=AF.Square, accum_out=st[:, 2:3]),
                         reads=[tvg], writes=tVNs[s] + [tSM2[s]])
                    yield
                    S.op("dve", lambda e, st=st: e.tensor_tensor(out=st[:, 3:4], in0=st[:, 0:1], in1=st[:, 1:2], op=ALU.add),
                         reads=[tSM2[s]], writes=[tSM2[s]])
                    yield
                    S.op("dve", lambda e, st=st: e.tensor_scalar(out=st[:, 3:4], in0=st[:, 3:4], scalar1=1.0 / D, scalar2=None, op0=ALU.mult),
                         reads=[tSM2[s]], writes=[tSM2[s]])
                    yield
                    S.op("dve", lambda e, st=st: e.tensor_tensor(out=st[:, 4:5], in0=st[:, 3:4], in1=st[:, 3:4], op=ALU.mult),
                         reads=[tSM2[s]], writes=[tSM2[s]])
                    yield
                    S.op("dve", lambda e, st=st: e.scalar_tensor_tensor(out=st[:, 5:6], in0=st[:, 2:3], scalar=1.0 / D, in1=st[:, 4:5],
                                                                        op0=ALU.mult, op1=ALU.subtract),
                         reads=[tSM2[s]], writes=[tSM2[s]])
                    yield
                    S.op("act", lambda e, st=st: e.activation(out=st[:, 5:6], in_=st[:, 5:6], func=AF.Sqrt, bias=EPSC[:, 0:1]),
                         reads=[tSM2[s], tCONST], writes=[tSM2[s]])
                    yield
                    S.op("dve", lambda e, st=st: e.reciprocal(out=st[:, 5:6], in_=st[:, 5:6]), reads=[tSM2[s]], writes=[tSM2[s]])
                    yield
                    S.op("dve", lambda e, vg=vg, s=s, st=st: e.tensor_scalar(out=VN[s], in0=vg, scalar1=st[:, 3:4], scalar2=st[:, 5:6],
                                                                             op0=ALU.subtract, op1=ALU.mult),
                         reads=[tvg, tSM2[s]], writes=tVNs[s])
                    yield
                    pm = [pbank(pin=True), pbank(pin=True)]
                    for g in range(8):
                        ps, tps = pm[g // 4]
                        gg = g % 4
                        S.op("pe", lambda e, g=g, gg=gg, ps=ps, s=s: e.matmul(ps[:, gg * 128:(gg + 1) * 128], lhsT=VN[s][:, g * 128:(g + 1) * 128],
                                                                              rhs=WST[:, g, :], start=True, stop=True),
                             reads=tVNs[s] + [tWST], writes=[tps])
                    for g in range(8):
                        ps, tps = pm[g // 4]
                        gg = g % 4
                        tmp, ttmp = FT[4 + g % 3], tFT[4 + g % 3]
                        S.op("dve", lambda e, g=g, gg=gg, ps=ps, tmp=tmp: e.scalar_tensor_tensor(
                            out=tmp[:, 0:128], in0=ps[:, gg * 128:(gg + 1) * 128], scalar=VEC[:, 176 + g:177 + g], in1=T2[:, g, :],
                            op0=ALU.mult, op1=ALU.add), reads=[tps, tVEC, tT2], writes=[ttmp])
                        S.op("pool", lambda e, g=g, tt=tt, tmp=tmp: e.tensor_tensor(
                            out=UT[:, g, tt * 128:(tt + 1) * 128], in0=UT[:, g, tt * 128:(tt + 1) * 128], in1=tmp[:, 0:128], op=ALU.mult),
                            reads=[ttmp, tUT[g]], writes=[tUT[g]])
                        if g % 2 == 1:
                            yield
                    unpin(pm[0][1])
                    unpin(pm[1][1])
                run_streams([l1_tile(0), l1_tile(1)])
                run_streams([l1_tile(2), l1_tile(3)])
                for oc in range(8):
                    ps, tps = pbank()
                    for k in range(8):
                        S.op("pe", lambda e, k=k, oc=oc, ps=ps: e.matmul(ps[:], lhsT=CWO[:, k, oc * 128:(oc + 1) * 128], rhs=UT[:, k, :],
                                                                         start=(k == 0), stop=(k == 7)),
                             reads=[tCWO, tUT[k]], writes=[tps])
                    resid_add(1, 2, oc, b, cond, ps[:], tps)
            S.barrier()
        if STAGE == 5:
            dump_debug()
        if STAGE >= 6:
            mlp(1)
        if STAGE == 6:
            dump_debug()

        ty = T("y")
        if STAGE >= 7:
            for b in range(NBLK):
                pr, tpr = pbank()
                for j in range(8):
                    sq, tsq = BT[8 + j % 4], tBT[8 + j % 4]
                    S.op("act", lambda e, j=j, sq=sq, b=b: e.activation(out=sq, in_=XB(j, b), func=AF.Square),
                         reads=[tX[j][b]], writes=[tsq])
                    S.op("pe", lambda e, j=j, sq=sq, pr=pr: e.matmul(pr[:], lhsT=ONEB[:], rhs=sq, start=(j == 0), stop=(j == 7)),
                         reads=[tsq, tCONST], writes=[tpr])
                rs, trs = FT[7], tFT[7]
                S.op("act", lambda e, pr=pr, rs=rs: e.activation(out=rs, in_=pr[:], func=AF.Ln, bias=EPSC[:, 1:2]),
                     reads=[tpr, tCONST], writes=[trs])
                S.op("act", lambda e, rs=rs: e.activation(out=rs, in_=rs, func=AF.Exp, scale=-0.5), reads=[trs], writes=[trs])
                for j in range(8):
                    S.op("dve", lambda e, j=j, b=b, rs=rs: e.scalar_tensor_tensor(
                        out=XB(j, b), in0=XB(j, b), scalar=FGS[:, j:j + 1], in1=rs, op0=ALU.mult, op1=ALU.mult),
                        reads=[tX[j][b], tLB, trs], writes=[tX[j][b]])
                for ti in range(4):
                    tt = b * 4 + ti
                    s = tt % 2
                    for hf in range(2):
                        ps, tps = pbank()
                        for jj in range(4):
                            j = hf * 4 + jj
                            S.op("pe", lambda e, jj=jj, j=j, tt=tt, ps=ps: e.transpose(
                                out=ps[:, jj * 128:(jj + 1) * 128], in_=XF[:, j, tt * 128:(tt + 1) * 128], identity=IDF[:]),
                                reads=[tX[j][b], tCONST], writes=[tps])
                        S.op("act", lambda e, ps=ps, hf=hf, s=s: e.activation(out=XS[s][:, hf * 512:(hf + 1) * 512], in_=ps[:], func=AF.Copy),
                             reads=[tps, tXS[s]], writes=[tXS[s]])
                    S.dma("sp", y[tt * 128:(tt + 1) * 128, :], XS[s], reads=[tXS[s]], writes=[ty])
        S.barrier()
        with nc.Block() as block:
            S.replay(block)
    return nc


_NC_CACHE = {}


def _prep_inputs(inp):
    f = lambda a: np.ascontiguousarray(np.asarray(a, dtype=np.float32))
    x_prompt, x_sample = f(inp["x_prompt"]), f(inp["x_sample"])
    st, c, c_ctx = f(inp["state_hgrn"]), f(inp["c"]), f(inp["c_ctx"])
    hw = f(inp["hgrn_w_in"])[0]
    fm = lambda v: np.ascontiguousarray(v.reshape(8, 128).T)
    maps = []
    for core in range(8):
        bq, half = core // 2, core % 2
        rev = half == 1
        R = (lambda a: a[::-1]) if rev else (lambda a: a)
        ca, cb = x_prompt[2 * core], x_prompt[2 * core + 1]
        lat = x_sample[bq]
        own = lat[half * 2048:(half + 1) * 2048]
        oth = lat[(1 - half) * 2048:(2 - half) * 2048]
        xo = np.concatenate([R(ca), R(cb), R(own)], axis=0)
        xh = R(oth)
        cT = np.zeros((128, 16), np.float32)
        cc = np.stack([c_ctx, c[bq]], axis=1)
        cT[:] = cc.reshape(8, 128, 2).transpose(1, 0, 2).reshape(128, 16)
        vec = np.zeros((128, NV), np.float32)
        for l in range(2):
            vec[:, l * 16:l * 16 + 8] = fm(inp["norm_mix_g"][l])
            vec[:, l * 16 + 8:l * 16 + 16] = fm(inp["norm_mlp_g"][l])
            ab = f(inp["ada_b"])[l].reshape(6, 8, 128)
            vec[:, 40 + l * 48: 88 + l * 48] = ab.transpose(2, 0, 1).reshape(128, 48)
        vec[:, 32:40] = fm(inp["final_norm_g"])
        vec[:, 136:144] = f(inp["hgrn_onorm_g"])[0].T
        lbl = f(inp["hgrn_lb_logits"])
        dA, dB = (1, 0) if rev else (0, 1)
        vec[:, 144:152] = fm(lbl[0, dA]); vec[:, 152:160] = fm(lbl[0, dB])
        vec[:, 160:168] = fm(lbl[1, dA]); vec[:, 168:176] = fm(lbl[1, dB])
        vec[:, 176:184] = fm(inp["cm_ln_g"][0]); vec[:, 184:192] = fm(inp["cm_ln_b"][0])
        q, ff, fb, ii, gg = [hw[:, k * 1024:(k + 1) * 1024] for k in range(5)]
        fA, fB = (fb, ff) if rev else (ff, fb)
        w_hg = np.stack([np.concatenate([m[:, h * 128:(h + 1) * 128] for m in (q, fA, fB, gg, ii)], axis=1) for h in range(8)])
        ws = f(inp["cm_w_s"])[0]
        bs = f(inp["cm_b_s"])[0]
        if rev:
            ws = ws[:, ::-1, ::-1]
            bs = bs[:, ::-1]
        wsT = np.ascontiguousarray(ws.transpose(2, 0, 1).reshape(128, 1024))
        bsb = np.ascontiguousarray(np.broadcast_to(bs.reshape(1, 1024), (128, 1024)))
        sa0 = st[bq, 0, 1 if rev else 0]
        sb0 = st[bq, 0, 0 if rev else 1]
        maps.append(dict(
            xo=np.ascontiguousarray(xo), xh=np.ascontiguousarray(xh), cT=cT, vecs=vec,
            ada_w=f(inp["ada_w"]), w_hg=np.ascontiguousarray(w_hg), w_ho=f(inp["hgrn_w_out"])[0],
            w1=f(inp["mlp_w1"]), w2=f(inp["mlp_w2"]), cm_wi=f(inp["cm_w_in"])[0], cm_wo=f(inp["cm_w_out"])[0],
            wsT=wsT, bsb=bsb, sa0=np.ascontiguousarray(sa0), sb0=np.ascontiguousarray(sb0)))
    return maps


def kernel(**inputs):
    maps = _prep_inputs(inputs)
    if "nc" not in _NC_CACHE:
        _NC_CACHE["nc"] = build_program()
    nc = _NC_CACHE["nc"]
    res = run_bass_kernel_spmd(nc, maps, core_ids=list(range(8)))
    y_prompt = np.zeros((16, 256, D), np.float32)
    y_sample = np.zeros((4, 4096, D), np.float32)
    new_state = np.zeros((16, 1, 2, 8, 128, 128), np.float32)
    for core in range(8):
        r = res.results[core]
        bq, half = core // 2, core % 2
        rev = half == 1
        R = (lambda a: a[::-1]) if rev else (lambda a: a)
        yy = np.asarray(r["y"])
        y_prompt[2 * core] = R(yy[0:256])
        y_prompt[2 * core + 1] = R(yy[256:512])
        y_sample[bq, half * 2048:(half + 1) * 2048] = R(yy[512:])
        nsr = np.asarray(r["ns"])
        for s in range(2):
            if rev:
                new_state[2 * core + s, 0, 0] = nsr[s, 1]
                new_state[2 * core + s, 0, 1] = nsr[s, 0]
            else:
                new_state[2 * core + s, 0, 0] = nsr[s, 0]
                new_state[2 * core + s, 0, 1] = nsr[s, 1]
        if DEBUG:
            _NC_CACHE.setdefault("dbg", {})[core] = (np.asarray(r["dbgx"]), np.asarray(r["dbga"]))
    return y_prompt, y_sample, new_state
```

```python
import os
import numpy as np
from contextlib import ExitStack
import concourse.bass as bass
import concourse.mybir as mybir
from concourse.bass_utils import run_bass_kernel_spmd

F32 = mybir.dt.float32
BF16 = mybir.dt.bfloat16
AF = mybir.ActivationFunctionType
ALU = mybir.AluOpType

D = 1024
KC = 8
TB = 512
NBLK = 5
NTOK = 2560
NOTH = 2048
EPS = 1e-6
NV = 192
DEBUG = bool(int(os.environ.get("KDEBUG", "0")))
STAGE = int(os.environ.get("KSTAGE", "99"))
KSUB = int(os.environ.get("KSUB", "99"))
KP = int(os.environ.get("KP", "99"))


class T:
    __slots__ = ("writer", "readers", "name")

    def __init__(self, name=""):
        self.writer = None
        self.readers = []
        self.name = name


class Sched:
    NDQ = 24

    def __init__(self, nc, es):
        self.nc = nc
        self.es = es
        self.engs = ["pe", "act", "dve", "pool", "sp"]
        self.sem, self.count, self.clock, self.hist = {}, {}, {}, {}
        self.prog = {e: [] for e in self.engs}
        for e in self.engs:
            self.sem[e] = es.enter_context(nc.semaphore("s_" + e))
            self.count[e] = 0
            self.clock[e] = {}
            self.hist[e] = {}
        self.dq = []
        for i in range(self.NDQ):
            q = "dq%d" % i
            self.sem[q] = es.enter_context(nc.semaphore("s_" + q))
            self.count[q] = 0
            self.hist[q] = {}
            self.dq.append(q)
        self.dq_next = {"sp": 0, "pool": 0}
        self.dq_set = {"sp": self.dq[:self.NDQ // 2], "pool": self.dq[self.NDQ // 2:]}

    def _waits_for(self, X, deps):
        clk = self.clock[X]
        need = {}
        for (E, n) in deps:
            if E == X and X == "pe":
                continue
            if clk.get(E, 0) >= n:
                continue
            if need.get(E, 0) < n:
                need[E] = n
        waits = []
        for E, n in need.items():
            assert n <= self.count[E]
            mult = 16 if E.startswith("dq") else 1
            waits.append((self.sem[E], n * mult))
            h = self.hist[E].get(n)
            if h:
                for k, v in h.items():
                    if k != X and clk.get(k, 0) < v:
                        clk[k] = v
            if clk.get(E, 0) < n:
                clk[E] = n
        return waits

    @staticmethod
    def _deps(reads, writes):
        deps = set()
        for t in reads:
            if t.writer:
                deps.add(t.writer)
        for t in writes:
            if t.writer:
                deps.add(t.writer)
            for r in t.readers:
                deps.add(r)
        return deps

    def op(self, X, fn, reads=(), writes=()):
        deps = self._deps(reads, writes)
        waits = self._waits_for(X, deps)
        self.count[X] += 1
        seq = self.count[X]
        h = dict(self.clock[X])
        h[X] = seq
        self.hist[X][seq] = h
        ident = (X, seq)
        for t in reads:
            t.readers.append(ident)
        for t in writes:
            t.writer = ident
            t.readers = []
        self.prog[X].append((waits, fn, (self.sem[X], 1)))
        return ident

    def dma(self, X, out_ap, in_ap, reads=(), writes=()):
        qs = self.dq_set[X]
        q = qs[self.dq_next[X]]
        self.dq_next[X] = (self.dq_next[X] + 1) % len(qs)
        deps = self._deps(reads, writes)
        if self.count[q] > 0:
            deps.add((q, self.count[q]))
        waits = self._waits_for(X, deps)
        self.count[q] += 1
        seq = self.count[q]
        h = dict(self.clock[X])
        h[q] = seq
        self.hist[q][seq] = h
        ident = (q, seq)
        for t in reads:
            t.readers.append(ident)
        for t in writes:
            t.writer = ident
            t.readers = []
        fn = lambda eng, o=out_ap, i=in_ap: eng.dma_start(out=o, in_=i)
        self.prog[X].append((waits, fn, (self.sem[q], 16)))
        return ident

    def barrier(self):
        allq = [(e, self.count[e]) for e in self.engs if self.count[e] > 0]
        allq += [(q, self.count[q]) for q in self.dq if self.count[q] > 0]
        for X in self.engs:
            waits = self._waits_for(X, set(d for d in allq if d[0] != X))
            if waits:
                self.prog[X].append((waits, None, None))

    def replay(self, block):
        sched = self

        def run(eng, X):
            for waits, fn, inc in sched.prog[X]:
                for (s, v) in waits:
                    eng.wait_ge(s, v)
                if fn is None:
                    continue
                ins = fn(eng)
                ins.then_inc(inc[0], inc[1])

        @block.tensor
        def _(e):
            run(e, "pe")

        @block.scalar
        def _(e):
            run(e, "act")

        @block.vector
        def _(e):
            run(e, "dve")

        @block.gpsimd
        def _(e):
            run(e, "pool")

        @block.sync
        def _(e):
            run(e, "sp")


def build_program():
    nc = bass.Bass("TRN2", target_bir_lowering=False)
    din = lambda n, s: nc.dram_tensor(n, list(s), F32, kind="ExternalInput").ap()
    xo = din("xo", [NTOK, D])
    xh = din("xh", [NOTH, D])
    cT = din("cT", [128, 16])
    vecs = din("vecs", [128, NV])
    ada_w = din("ada_w", [2, D, 6 * D])
    w_hg = din("w_hg", [8, D, 640])
    w_ho = din("w_ho", [D, D])
    w1 = din("w1", [2, D, 4 * D])
    w2 = din("w2", [2, 4 * D, D])
    cm_wi = din("cm_wi", [D, 2 * D])
    cm_wo = din("cm_wo", [D, D])
    wsT = din("wsT", [128, 1024])
    bsb = din("bsb", [128, 1024])
    sa0 = din("sa0", [8, 128, 128])
    sb0 = din("sb0", [8, 128, 128])
    y = nc.dram_tensor("y", [NTOK, D], F32, kind="ExternalOutput").ap()
    ns = nc.dram_tensor("ns", [2, 2, 8, 128, 128], F32, kind="ExternalOutput").ap()
    if DEBUG:
        dbgx = nc.dram_tensor("dbgx", [128, 40960], BF16, kind="ExternalOutput").ap()
        dbga = nc.dram_tensor("dbga", [128, 20480], BF16, kind="ExternalOutput").ap()

    with ExitStack() as es:
        S = Sched(nc, es)
        sb = lambda n, s, dt: es.enter_context(nc.sbuf_tensor(n, list(s), dt))
        XRAW = sb("xraw", [128, 40960], BF16)
        ABUF = sb("abuf", [128, 20480], BF16)
        WAR = sb("war", [128, 16384], BF16)
        FAR = sb("far", [128, 4096], F32)
        BAR = sb("bar", [128, 10240], BF16)
        VEC = sb("vec", [128, NV], F32)
        MOD = sb("mod", [128, 2 * 96], F32)
        AMOD = sb("amod", [128, 64], F32)
        LB = sb("lb", [128, 16], F32)
        OML = sb("oml", [128, 16], F32)
        OGS = sb("ogs", [128, 8], F32)
        FGS = sb("fgs", [128, 8], F32)
        CS = sb("cs", [128, 16], BF16)
        CTF = sb("ctf", [128, 16], F32)
        IDF = sb("idf", [128, 128], F32)
        IDB = sb("idb", [128, 128], BF16)
        ONEB = sb("oneb", [128, 128], BF16)
        ONEF = sb("onef", [128, 128], F32)
        MSKA = sb("mska", [128, 256], F32)
        MSKB = sb("mskb", [128, 256], F32)
        RST = sb("rst", [128, 512], F32)
        S32 = sb("s32", [128, 2, 128], F32)
        SBF = sb("sbf", [128, 4, 128], BF16)
        SBSV = sb("sbsv", [128, 4, 128], F32)
        SINI = sb("sini", [128, 2, 2, 128], F32)
        NSST = sb("nsst", [128, 1, 4, 128], F32)
        SMALL = sb("small", [128, 64], F32)
        EPSC = sb("epsc", [128, 4], F32)
        SMD = sb("smd", [128, 2, 2, 64], F32)
        PSF = [es.enter_context(nc.psum_tensor("psf%d" % i, [128, 512], F32)) for i in range(6)]
        PSB = [es.enter_context(nc.psum_tensor("psb%d" % i, [128, 1024], BF16)) for i in range(2)]
        tPSF = [T("psf%d" % i) for i in range(6)]
        tPSB = [T("psb%d" % i) for i in range(2)]
        rr = {"f": 0, "b": 0}

        pinned = set()

        def run_streams(gens):
            res = [None] * len(gens)
            alive = list(range(len(gens)))
            while alive:
                for gi in list(alive):
                    try:
                        next(gens[gi])
                    except StopIteration as st:
                        res[gi] = st.value
                        alive.remove(gi)
            return res

        def pbank(pin=False):
            i = rr["f"]
            n_try = 0
            while i in pinned:
                i = (i + 1) % 6
                n_try += 1
                assert n_try < 7, "all PSUM banks pinned"
            rr["f"] = (i + 1) % 6
            if pin:
                pinned.add(i)
            return PSF[i], tPSF[i]

        def unpin(tp):
            pinned.discard(tPSF.index(tp))

        def pbankb():
            i = rr["b"]
            rr["b"] = (i + 1) % 2
            return PSB[i], tPSB[i]

        XF = XRAW[:].bitcast(F32).rearrange("p (j t) -> p j t", j=8)
        tX = [[T("x%d_%d" % (j, b)) for b in range(NBLK)] for j in range(KC)]
        XB = lambda j, b: XF[:, j, b * TB:(b + 1) * TB]
        HTO = XRAW[:, 0:20480].rearrange("p (j t) -> p j t", j=8)
        HTH = XRAW[:, 20480:36864].rearrange("p (j t) -> p j t", j=8)
        tHTO = [T("hto%d" % b) for b in range(20)]
        tHTH = [T("hth%d" % b) for b in range(16)]
        AB3 = ABUF[:].rearrange("p (j t) -> p j t", j=8)
        tAB = [[T("ab%d_%d" % (j, b)) for b in range(NBLK)] for j in range(KC)]
        FT = [FAR[:, i * 512:(i + 1) * 512] for i in range(8)]
        tFT = [T("ft%d" % i) for i in range(8)]
        XS = [FAR[:, 0:1024], FAR[:, 1024:2048]]
        tXS = [T("xs0"), T("xs1")]
        BT = [BAR[:, i * 512:(i + 1) * 512] for i in range(20)]
        tBT = [T("bt%d" % i) for i in range(20)]
        tVEC, tMOD, tAMOD, tLB, tCS, tCONST, tSMALL = T(), T(), T(), T(), T(), T(), T()

        S.dma("sp", VEC[:], vecs, writes=[tVEC])
        S.dma("sp", CTF[:], cT, writes=[tCS])
        S.op("pool", lambda e: e.memset(ONEF[:], 1.0), writes=[tCONST])
        S.op("pool", lambda e: e.memset(ONEB[:], 1.0), writes=[tCONST])
        S.op("pool", lambda e: e.memset(EPSC[:, 0:1], EPS), reads=[tCONST], writes=[tCONST])
        S.op("pool", lambda e: e.memset(EPSC[:, 1:2], float(D * EPS)), reads=[tCONST], writes=[tCONST])
        S.op("pool", lambda e: e.memset(EPSC[:, 2:3], float(128 * EPS)), reads=[tCONST], writes=[tCONST])
        S.op("pool", lambda e: e.memset(EPSC[:, 3:4], 1.0), reads=[tCONST], writes=[tCONST])
        S.op("pool", lambda e: e.affine_select(out=IDF[:], in_=ONEF[:], pattern=[[-1, 128]],
                                               compare_op=ALU.is_equal, fill=0.0, base=0, channel_multiplier=1),
             reads=[tCONST], writes=[tCONST])
        S.op("pool", lambda e: e.tensor_copy(out=IDB[:], in_=IDF[:]), reads=[tCONST], writes=[tCONST])
        MD = [MSKA[:, 0:128], MSKB[:, 0:128]]
        MO = [MSKA[:, 128:256], MSKB[:, 128:256]]
        S.op("pool", lambda e: e.affine_select(out=MD[0], in_=ONEF[:], pattern=[[1, 128]],
                                               compare_op=ALU.is_ge, fill=0.0, base=0, channel_multiplier=-1),
             reads=[tCONST], writes=[tCONST])
        S.op("pool", lambda e: e.memset(MD[0][0:32, 32:128], 0.0), reads=[tCONST], writes=[tCONST])
        S.op("pool", lambda e: e.memset(MD[0][32:64, 64:128], 0.0), reads=[tCONST], writes=[tCONST])
        S.op("pool", lambda e: e.memset(MD[0][64:96, 96:128], 0.0), reads=[tCONST], writes=[tCONST])
        S.op("pool", lambda e: e.memset(MO[0], 0.0), reads=[tCONST], writes=[tCONST])
        S.op("pool", lambda e: e.memset(MO[0][0:32, 32:64], 1.0), reads=[tCONST], writes=[tCONST])
        S.op("pool", lambda e: e.memset(MO[0][64:96, 96:128], 1.0), reads=[tCONST], writes=[tCONST])
        pmk, tpmk = pbank()
        S.op("pe", lambda e: e.transpose(out=pmk[:, 0:128], in_=MD[0], identity=IDF[:]), reads=[tCONST], writes=[tpmk])
        S.op("pe", lambda e: e.transpose(out=pmk[:, 128:256], in_=MO[0], identity=IDF[:]), reads=[tCONST], writes=[tpmk])
        S.op("act", lambda e: e.activation(out=MSKB[:, 0:256], in_=pmk[:, 0:256], func=AF.Copy), reads=[tpmk, tCONST], writes=[tCONST])
        S.op("pool", lambda e: e.memset(RST[:], 1.0), reads=[tCONST], writes=[tCONST])
        S.op("pool", lambda e: e.memset(RST[:, 0::32], 0.0), reads=[tCONST], writes=[tCONST])
        S.op("pool", lambda e: e.memset(SMD[:], 1.0), reads=[tCONST], writes=[tCONST])
        S.op("pool", lambda e: e.memset(SBF[:], 0.0), reads=[tCONST], writes=[tCONST])
        S.op("dve", lambda e: e.tensor_tensor(out=LB[:], in0=VEC[:, 144:160], in1=VEC[:, 160:176], op=ALU.subtract),
             reads=[tVEC], writes=[tLB])
        S.op("act", lambda e: e.activation(out=LB[:], in_=LB[:], func=AF.Sigmoid), reads=[tLB], writes=[tLB])
        S.op("dve", lambda e: e.tensor_scalar(out=OML[:], in0=LB[:], scalar1=-1.0, scalar2=1.0, op0=ALU.mult, op1=ALU.add),
             reads=[tLB], writes=[tLB])
        S.op("dve", lambda e: e.tensor_scalar(out=OGS[:], in0=VEC[:, 136:144], scalar1=float(np.sqrt(128.0)), scalar2=None,
                                              op0=ALU.mult), reads=[tVEC], writes=[tLB])
        S.op("dve", lambda e: e.tensor_scalar(out=FGS[:], in0=VEC[:, 32:40], scalar1=32.0, scalar2=None, op0=ALU.mult),
             reads=[tVEC], writes=[tLB])
        S.op("act", lambda e: e.activation(out=CS[:], in_=CTF[:], func=AF.Silu), reads=[tCS], writes=[tCS])

        wslot = [WAR[:, i * 4096:(i + 1) * 4096].rearrange("p (k n) -> p k n", k=8) for i in range(4)]
        tW = [T("w%d" % i) for i in range(4)]
        pmL = [pbank(pin=True), pbank(pin=True)]
        ada_state = {"li": 0}

        def ada_chunk(l, m, hf):
            pm, tpm = pmL[l]
            s_ = ada_state["li"] % 4
            ada_state["li"] += 1
            src = ada_w[l, :, m * 1024 + hf * 512: m * 1024 + (hf + 1) * 512].rearrange("(k p) n -> p k n", p=128)
            S.dma("pool", wslot[s_], src, writes=[tW[s_]])
            for jj in range(4):
                col = (m * 8 + hf * 4 + jj) * 2
                for k in range(8):
                    S.op("pe", lambda e, s_=s_, jj=jj, k=k, col=col, pm=pm: e.matmul(
                        pm[:, col:col + 2], lhsT=wslot[s_][:, k, jj * 128:(jj + 1) * 128], rhs=CS[:, k * 2:k * 2 + 2],
                        start=(k == 0), stop=(k == 7)), reads=[tW[s_], tCS], writes=[tpm])

        tMODm = [[T() for _ in range(6)] for _ in range(2)]

        def ada_finish(l, m):
            pm, tpm = pmL[l]
            mo = MOD[:, l * 96 + m * 16: l * 96 + (m + 1) * 16].rearrange("p (c o) -> p c o", o=2)
            ab = VEC[:, 40 + l * 48 + m * 8: 40 + l * 48 + (m + 1) * 8].unsqueeze(2).broadcast_to([128, 8, 2])
            S.op("dve", lambda e: e.tensor_tensor(out=mo, in0=pm[:, m * 16:(m + 1) * 16].rearrange("p (c o) -> p c o", o=2),
                                                  in1=ab, op=ALU.add), reads=[tpm, tVEC], writes=[tMOD, tMODm[l][m]])

        def modv(l, m, j, cond):
            c = l * 96 + (m * 8 + j) * 2 + cond
            return MOD[:, c:c + 1]

        def amod(l, nrm, j, cond):
            c = ((l * 2 + nrm) * 8 + j) * 2 + cond
            return AMOD[:, c:c + 1]

        def amod_finish(l, nrm):
            scm = 1 + 3 * nrm
            base = l * 96 + scm * 16
            gcol = l * 16 + nrm * 8
            o = AMOD[:, (l * 2 + nrm) * 16:(l * 2 + nrm + 1) * 16].rearrange("p (j o) -> p j o", o=2)
            i0 = MOD[:, base:base + 16].rearrange("p (j o) -> p j o", o=2)
            gg = VEC[:, gcol:gcol + 8].unsqueeze(2).broadcast_to([128, 8, 2])
            S.op("dve", lambda e: e.tensor_scalar(out=o, in0=i0, scalar1=1.0, scalar2=None, op0=ALU.add),
                 reads=[tMOD], writes=[tAMOD])
            fac = 1.0 if (l == 0 and nrm == 0) else 32.0
            S.op("dve", lambda e: e.scalar_tensor_tensor(out=o, in0=o, scalar=fac, in1=gg, op0=ALU.mult, op1=ALU.mult),
                 reads=[tAMOD, tVEC], writes=[tAMOD])

        for (m_, hf_) in ((0, 0), (0, 1), (1, 0), (1, 1)):
            ada_chunk(0, m_, hf_)
        ada_finish(0, 0)
        ada_finish(0, 1)
        amod_finish(0, 0)
        ada_rest = [(0, m_, hf_) for m_ in range(2, 6) for hf_ in range(2)] + [(1, m_, hf_) for m_ in range(6) for hf_ in range(2)]

        tSM2 = [T(), T()]

        def norm_tokmajor(src_rows, tt, dstv, tdst, cond):
            s = tt % 2
            S.dma("sp", XS[s], src_rows, writes=[tXS[s]])
            tjunk = tBT[8 + 2 * s]
            junk = BAR[:, (8 + 2 * s) * 512:(10 + 2 * s) * 512]
            ss = SMALL[:, s * 4:s * 4 + 1]
            rs = SMALL[:, s * 4 + 1:s * 4 + 2]
            S.op("act", lambda e: e.activation(out=junk, in_=XS[s], func=AF.Square, accum_out=ss),
                 reads=[tXS[s]], writes=[tjunk, tSM2[s]])
            yield
            S.op("act", lambda e: e.activation(out=rs, in_=ss, func=AF.Sqrt, scale=1.0 / D, bias=EPSC[:, 0:1]),
                 reads=[tSM2[s], tCONST], writes=[tSM2[s]])
            yield
            S.op("dve", lambda e: e.reciprocal(out=rs, in_=rs), reads=[tSM2[s]], writes=[tSM2[s]])
            yield
            S.op("dve", lambda e: e.tensor_scalar(out=junk, in0=XS[s], scalar1=rs, scalar2=None, op0=ALU.mult),
                 reads=[tXS[s], tSM2[s], tjunk], writes=[tjunk])
            yield
            pb, tpb = PSB[s], tPSB[s]
            for j in range(8):
                S.op("pe", lambda e, j=j: e.transpose(out=pb[:, j * 128:(j + 1) * 128], in_=junk[:, j * 128:(j + 1) * 128],
                                                      identity=IDB[:]), reads=[tjunk, tCONST], writes=[tpb])
            yield
            for j in range(8):
                S.op("act", lambda e, j=j: e.activation(out=dstv(j), in_=pb[:, j * 128:(j + 1) * 128], func=AF.Identity,
                                                        scale=amod(0, 0, j, cond), bias=modv(0, 0, j, cond)),
                     reads=[tpb, tAMOD, tMOD], writes=[tdst])
                if j % 4 == 3:
                    yield

        def norm_featmajor(l, nrm, b, cond, dstv, tdsts):
            pr, tpr = pbank()
            for j in range(8):
                sq, tsq = BT[8 + j % 4], tBT[8 + j % 4]
                S.op("act", lambda e, j=j, sq=sq: e.activation(out=sq, in_=XB(j, b), func=AF.Square),
                     reads=[tX[j][b]], writes=[tsq])
                S.op("pe", lambda e, j=j, sq=sq: e.matmul(pr[:], lhsT=ONEB[:], rhs=sq, start=(j == 0), stop=(j == 7)),
                     reads=[tsq, tCONST], writes=[tpr])
            rs, trs = FT[7], tFT[7]
            S.op("act", lambda e: e.activation(out=rs, in_=pr[:], func=AF.Ln, bias=EPSC[:, 1:2]), reads=[tpr, tCONST], writes=[trs])
            S.op("act", lambda e: e.activation(out=rs, in_=rs, func=AF.Exp, scale=-0.5), reads=[trs], writes=[trs])
            for j in range(8):
                tmp, ttmp = FT[4 + j % 3], tFT[4 + j % 3]
                S.op("dve", lambda e, j=j, tmp=tmp: e.tensor_tensor(out=tmp, in0=XB(j, b), in1=rs, op=ALU.mult),
                     reads=[tX[j][b], trs], writes=[ttmp])
                sh = modv(l, 3 * nrm, j, cond)
                S.op("act", lambda e, j=j, tmp=tmp, sh=sh: e.activation(out=dstv(j), in_=tmp, func=AF.Identity,
                                                                        scale=amod(l, nrm, j, cond), bias=sh),
                     reads=[ttmp, tAMOD, tMOD], writes=[tdsts[j]])

        def resid_add(l, m, oc, b, cond, ps, tps):
            S.op("dve", lambda e: e.scalar_tensor_tensor(out=XB(oc, b), in0=ps, scalar=modv(l, m, oc, cond), in1=XB(oc, b),
                                                         op0=ALU.mult, op1=ALU.add),
                 reads=[tps, tMOD, tX[oc][b]], writes=[tX[oc][b]])

        def dump_debug():
            if DEBUG:
                S.barrier()
                td = T()
                S.dma("sp", dbgx, XRAW[:], writes=[td])
                S.dma("sp", dbga, ABUF[:], writes=[td])

        tiles0b = []
        for tt in range(20):
            cond = 0 if tt < 4 else 1
            tiles0b.append((xo[tt * 128:(tt + 1) * 128, :], tt, (lambda j, tt=tt: HTO[:, j, tt * 128:(tt + 1) * 128]), tHTO[tt], cond))
        for tt in range(16):
            tiles0b.append((xh[tt * 128:(tt + 1) * 128, :], tt, (lambda j, tt=tt: HTH[:, j, tt * 128:(tt + 1) * 128]), tHTH[tt], 1))
        for pi in range(0, 36, 2):
            run_streams([norm_tokmajor(*tiles0b[pi]), norm_tokmajor(*tiles0b[pi + 1])])
            if ada_rest:
                ada_chunk(*ada_rest.pop(0))
            if ada_rest and pi % 4 == 0:
                ada_chunk(*ada_rest.pop(0))
        while ada_rest:
            ada_chunk(*ada_rest.pop(0))
        for l_ in range(2):
            for m_ in range(6):
                if not (l_ == 0 and m_ < 2):
                    ada_finish(l_, m_)
        amod_finish(0, 1)
        amod_finish(1, 0)
        amod_finish(1, 1)
        unpin(pmL[0][1])
        unpin(pmL[1][1])
        S.barrier()

        tS32 = [T("s32a"), T("s32b")]
        S32P = WAR[:, 12288:13312].bitcast(F32).rearrange("p (d q v) -> p d q v", d=2, q=2)
        tS32P = [[T("s32p00"), T("s32p01")], [T("s32p10"), T("s32p11")]]
        cur = [0, 0]
        tSBF = [[T("sbfa0"), T("sbfa1")], [T("sbfb0"), T("sbfb1")]]
        tSBSV, tSINI, tNSST = T(), [T(), T()], [T(), T()]
        whs = [WAR[:, i * 5120:(i + 1) * 5120].rearrange("p (k n) -> p k n", k=8) for i in range(2)]
        tWH = [T("wh0"), T("wh1")]
        dF = [FT[0], FT[1]]; tdF = [tFT[0], tFT[1]]
        dG = [FT[2], FT[3]]; tdG = [tFT[2], tFT[3]]
        dB = [FT[4], FT[5]]; tdB = [tFT[4], tFT[5]]
        QF, tQF = FT[6], tFT[6]
        dQB = [BT[0], BT[1]]; tdQB = [tBT[0], tBT[1]]
        dKM = [BT[2], BT[3]]; tdKM = [tBT[2], tBT[3]]
        dK2 = [BT[4], BT[5]]; tdK2 = [tBT[4], tBT[5]]
        dKT = [BT[6], BT[7]]; tdKT = [tBT[6], tBT[7]]
        dKTo = [BT[12], BT[13]]; tdKTo = [tBT[12], tBT[13]]
        dKH = [BT[14], BT[15]]; tdKH = [tBT[14], tBT[15]]
        dAT2 = [BT[16], BT[17]]; tdAT2 = [tBT[16], tBT[17]]
        tSMD = [[T("smd00"), T("smd01")], [T("smd10"), T("smd11")]]
        E2 = [WAR[:, 10240 + i * 1024: 10240 + (i + 1) * 1024].bitcast(F32) for i in range(2)]
        tE2 = [T("e2a"), T("e2b")]
        for d_ in range(2):
            S.op("pool", lambda e, d_=d_: e.memset(dKT[d_][64:128, :], 0.0), writes=[tdKT[d_]])
            S.op("pool", lambda e, d_=d_: e.memset(dKTo[d_][0:64, :], 0.0), writes=[tdKTo[d_]])
        dAT = [BT[8], BT[9]]; tdAT = [tBT[8], tBT[9]]
        ZGSs, tZGSs = [BT[10], BT[18]], [tBT[10], tBT[18]]
        VTKs, tVTKs = [BT[11], BT[19]], [tBT[11], tBT[19]]
        VTK3s = [v.rearrange("p (i v) -> p i v", i=4) for v in VTKs]

        def hg_proj(wh, twh, htv, thts, col, ps, tps):
            for k in range(8):
                S.op("pe", lambda e, k=k: e.matmul(ps[:], lhsT=wh[:, k, col * 128:(col + 1) * 128], rhs=htv(k),
                                                   start=(k == 0), stop=(k == 7)), reads=[twh] + thts, writes=[tps])

        def hg_vtok(wh, twh, httile, thts, vpar):
            VTK, tVTK = VTKs[vpar], tVTKs[vpar]
            ps, tps = pbank()
            for i in range(4):
                for k in range(8):
                    S.op("pe", lambda e, i=i, k=k: e.matmul(ps[:, i * 128:(i + 1) * 128], lhsT=httile(k, i), rhs=wh[:, k, 512:640],
                                                            start=(k == 0), stop=(k == 7)), reads=[twh] + thts, writes=[tps])
            S.op("act", lambda e: e.activation(out=VTK, in_=ps[:], func=AF.Copy), reads=[tps], writes=[tVTK])

        def hg_prep(d, h, zps, tzps, need_o, par=0, vpar=0, ts=None):
            if ts is None:
                ts = d
            F, tF, G, tG, Bc, tB = dF[ts], tdF[ts], dG[ts], tdG[ts], dB[ts], tdB[ts]
            VTK3, tVTK = VTK3s[vpar], tVTKs[vpar]
            lbc = LB[:, d * 8 + h:d * 8 + h + 1]
            omc = OML[:, d * 8 + h:d * 8 + h + 1]
            one1 = EPSC[:, 3:4]
            S.op("act", lambda e: e.activation(out=F, in_=zps[:], func=AF.Exp, scale=-1.0), reads=[tzps], writes=[tF])
            if tzps in tPSF:
                unpin(tzps)
            yield
            S.op("act", lambda e: e.activation(out=G, in_=F, func=AF.Ln, scale=lbc, bias=one1), reads=[tF, tLB, tCONST], writes=[tG])
            yield
            S.op("act", lambda e: e.activation(out=Bc, in_=F, func=AF.Ln, bias=one1), reads=[tF, tCONST], writes=[tB])
            yield
            S.op("dve", lambda e: e.tensor_tensor(out=G, in0=G, in1=Bc, op=ALU.subtract), reads=[tG, tB], writes=[tG])
            yield
            XK, tXK = E2[ts], tE2[ts]
            S.op("act", lambda e: e.activation(out=XK, in_=Bc, func=AF.Exp, scale=-1.0), reads=[tB, tXK], writes=[tXK])
            yield
            v16 = lambda ap: ap.rearrange("p (c t) -> p c t", t=32)
            if d == 0:
                S.op("dve", lambda e: e.tensor_tensor_scan(out=Bc, data0=RST[:], data1=G, initial=0.0, op0=ALU.mult, op1=ALU.add),
                     reads=[tG, tCONST], writes=[tB])
                lastc = 31
            else:
                S.op("dve", lambda e: e.tensor_tensor_scan(out=Bc[:, ::-1], data0=RST[:], data1=G[:, ::-1], initial=0.0,
                                                           op0=ALU.mult, op1=ALU.add), reads=[tG, tCONST], writes=[tB])
                lastc = 0
            yield
            S.op("dve", lambda e: e.scalar_tensor_tensor(out=F, in0=F, scalar=omc, in1=XK, op0=ALU.mult, op1=ALU.mult),
                 reads=[tF, tXK, tLB], writes=[tF])
            yield
            bbT, tbb, X, tX_ = Bc, tB, G, tG
            QB, tQB, KM, tKM, KT, tKT, AT, tAT = (dQB[ts], tdQB[ts], dKM[ts], tdKM[ts], dKT[ts], tdKT[ts], dAT[ts], tdAT[ts])
            QD, tQD, KH, tKH, AT2, tAT2 = dK2[ts], tdK2[ts], dKH[ts], tdKH[ts], dAT2[ts], tdAT2[ts]
            sm = SMD[:, ts, par, :]
            EDC, EC, EH, EH2 = sm[:, 0:16], sm[:, 16:24], sm[:, 32:48], sm[:, 48:64]
            tsm = tSMD[ts][par]
            X2, tX2 = E2[ts], tE2[ts]
            S.op("act", lambda e: e.activation(out=EDC, in_=bbT[:, lastc::32], func=AF.Exp), reads=[tbb, tsm], writes=[tsm])
            yield
            S.op("dve", lambda e: e.tensor_tensor(out=EC, in0=EDC[:, 0::2], in1=EDC[:, 1::2], op=ALU.mult), reads=[tsm], writes=[tsm])
            yield
            if d == 0:
                if need_o:
                    S.op("dve", lambda e: e.tensor_copy(out=EH[:, 1::2], in_=EDC[:, 0::2]), reads=[tsm], writes=[tsm])
                    yield
                S.op("dve", lambda e: e.tensor_copy(out=EH2[:, 0::2], in_=EDC[:, 1::2]), reads=[tsm], writes=[tsm])
                yield
            else:
                if need_o:
                    S.op("dve", lambda e: e.tensor_copy(out=EH[:, 0::2], in_=EDC[:, 1::2]), reads=[tsm], writes=[tsm])
                    yield
                S.op("dve", lambda e: e.tensor_copy(out=EH2[:, 1::2], in_=EDC[:, 0::2]), reads=[tsm], writes=[tsm])
                yield
            if need_o:
                S.op("act", lambda e: e.activation(out=X2, in_=bbT, func=AF.Exp, scale=-1.0), reads=[tbb, tX2], writes=[tX2])
                yield
                S.op("dve", lambda e: e.tensor_tensor(out=KM, in0=F, in1=X2, op=ALU.mult), reads=[tF, tX2], writes=[tKM])
                yield
                S.op("act", lambda e: e.activation(out=X, in_=bbT, func=AF.Exp), reads=[tbb, tX_], writes=[tX_])
                yield
                Y, tY = X2, tX2
            else:
                Y, tY = X, tX_
            blh = bbT[:, lastc::32].unsqueeze(2).broadcast_to([128, 16, 32])
            S.op("dve", lambda e: e.tensor_tensor(out=v16(Y), in0=blh, in1=v16(bbT), op=ALU.subtract),
                 reads=[tbb, tY], writes=[tY])
            yield
            S.op("act", lambda e: e.activation(out=Y, in_=Y, func=AF.Exp), reads=[tY], writes=[tY])
            yield
            S.op("dve", lambda e: e.tensor_tensor(out=KH, in0=F, in1=Y, op=ALU.mult), reads=[tF, tY], writes=[tKH])
            yield
            K2, tK2 = AT2, tAT2
            S.op("dve", lambda e: e.tensor_tensor(out=v16(K2), in0=v16(KH), in1=EH2.unsqueeze(2).broadcast_to([128, 16, 32]),
                                                  op=ALU.mult), reads=[tKH, tsm], writes=[tK2])
            yield
            if need_o:
                S.op("pool", lambda e: e.tensor_tensor(out=QD, in0=QF, in1=X, op=ALU.mult), reads=[tQF, tX_, tK2], writes=[tQD])
                yield
                S.op("pool", lambda e: e.tensor_tensor(out=v16(QB), in0=v16(QD), in1=EH.unsqueeze(2).broadcast_to([128, 16, 32]),
                                                       op=ALU.mult), reads=[tQD, tsm], writes=[tQB])
                yield
            if KP < 3:
                return None
            pb, tpb = pbankb()
            for i in range(4):
                S.op("pe", lambda e, i=i: e.transpose(out=pb[:, i * 128:(i + 1) * 128], in_=K2[:, i * 128:(i + 1) * 128],
                                                      identity=IDB[:]), reads=[tK2, tCONST], writes=[tpb])
                yield
            KTo, tKTo = dKTo[ts], tdKTo[ts]
            S.op("act", lambda e: e.activation(out=KT[0:64, :], in_=pb[0:64, 0:512], func=AF.Copy), reads=[tpb, tKT], writes=[tKT])
            yield
            S.op("dve", lambda e: e.tensor_copy(out=KTo[64:128, :], in_=pb[64:128, 0:512]), reads=[tpb, tKTo], writes=[tKTo])
            yield
            if need_o:
                pa, tpa = pbank(pin=True)
                pa2, tpa2 = pbank(pin=True)
                for i in range(4):
                    S.op("pe", lambda e, i=i: e.matmul(pa[:, i * 128:(i + 1) * 128], lhsT=KM[:, i * 128:(i + 1) * 128],
                                                       rhs=QD[:, i * 128:(i + 1) * 128], start=True, stop=True),
                         reads=[tKM, tQD], writes=[tpa])
                    yield
                for i in range(4):
                    S.op("pe", lambda e, i=i: e.matmul(pa2[:, i * 128:(i + 1) * 128], lhsT=KH[:, i * 128:(i + 1) * 128],
                                                       rhs=QD[:, i * 128:(i + 1) * 128], start=True, stop=True),
                         reads=[tKH, tQD], writes=[tpa2])
                    yield
                v4 = lambda ap: ap.rearrange("p (i t) -> p i t", i=4)
                S.op("dve", lambda e: e.tensor_tensor(out=v4(AT), in0=v4(pa[:]), in1=MD[d].unsqueeze(1).broadcast_to([128, 4, 128]),
                                                      op=ALU.mult), reads=[tpa, tCONST], writes=[tAT])
                yield
                S.op("dve", lambda e: e.tensor_tensor(out=v4(AT2), in0=v4(pa2[:]), in1=MO[d].unsqueeze(1).broadcast_to([128, 4, 128]),
                                                      op=ALU.mult), reads=[tpa2, tCONST, tKT, tKTo], writes=[tAT2])
                unpin(tpa)
                unpin(tpa2)
                yield
            if KP < 4:
                return None
            KT3 = [KT.rearrange("p (i k) -> p i k", i=4), KTo.rearrange("p (i k) -> p i k", i=4)]
            pp = [pbank(pin=True), pbank(pin=True)]
            for c in range(8):
                i, r = c // 2, c % 2
                ps, tps = pp[c // 4]
                cc = c % 4
                S.op("pe", lambda e, i=i, r=r, ps=ps, cc=cc: e.matmul(ps[:, cc * 128:(cc + 1) * 128], lhsT=KT3[r][:, i, :],
                                                                      rhs=VTK3[:, i, :], start=True, stop=True),
                     reads=[tKT, tKTo, tVTK], writes=[tps])
                yield
            return dict(d=d, EC=EC, tsm=tsm, QB=QB, tQB=tQB, AT=AT, tAT=tAT, AT2=AT2, tAT2=tAT2, pp=pp, VTK3=VTK3, tVTK=tVTK)

        def hg_chain(ctxs, po, tpo, need_o, resets=(), saves=None, nsbuf=0):
            if KP < 5 or any(c is None for c in ctxs):
                return
            yield
            if need_o:
                first = True
                for cx in ctxs:
                    for (nm, tnm) in (("AT", "tAT"), ("AT2", "tAT2")):
                        AT3 = cx[nm].rearrange("p (i t) -> p i t", i=4)
                        for i in range(4):
                            S.op("pe", lambda e, i=i, AT3=AT3, first=first: e.matmul(
                                po[:, i * 128:(i + 1) * 128], lhsT=cx["VTK3"][:, i, :], rhs=AT3[:, i, :], start=first, stop=False,
                                skip_group_check=True), reads=[cx["tVTK"], cx[tnm]], writes=[tpo])
                            first = False
            nd = len(ctxs)
            for step in range(8):
                for xi, cx in enumerate(ctxs):
                    d = cx["d"]
                    c = step if d == 0 else 7 - step
                    sbi = step % 2
                    q_ = cur[d]
                    sv, tsv = S32P[:, d, q_, :], tS32P[d][q_]
                    sn, tsn = S32P[:, d, 1 - q_, :], tS32P[d][1 - q_]
                    if (d, c) in resets:
                        S.op("pool", lambda e, sv=sv: e.memset(sv, 0.0), reads=[tsv], writes=[tsv])
                        S.op("pool", lambda e, d=d, sbi=sbi: e.memset(SBF[:, d * 2 + sbi, :], 0.0),
                             reads=[tSBF[d][sbi]], writes=[tSBF[d][sbi]])
                    if need_o:
                        last = (xi == nd - 1)
                        S.op("pe", lambda e, d=d, c=c, sbi=sbi, cx=cx, last=last: e.matmul(
                            po[:, c * 64:(c + 1) * 64], lhsT=SBF[:, d * 2 + sbi, :], rhs=cx["QB"][:, c * 64:(c + 1) * 64],
                            start=False, stop=last, skip_group_check=True),
                            reads=[tSBF[d][sbi], cx["tQB"]], writes=[tpo])
                    ps, tps = cx["pp"][c // 4]
                    cc = c % 4
                    esc = cx["EC"][:, c:c + 1]
                    S.op("dve", lambda e, sv=sv, sn=sn, ps=ps, cc=cc, esc=esc: e.scalar_tensor_tensor(
                        out=sn, in0=sv, scalar=esc, in1=ps[:, cc * 128:(cc + 1) * 128], op0=ALU.mult, op1=ALU.add),
                        reads=[tsv, cx["tsm"], tps], writes=[tsn])
                    cur[d] = 1 - q_
                    if need_o:
                        nsb = (step + 1) % 2
                        if d == 0:
                            S.op("pool", lambda e, sn=sn, d=d, nsb=nsb: e.tensor_copy(out=SBF[:, d * 2 + nsb, :], in_=sn),
                                 reads=[tsn], writes=[tSBF[d][nsb]])
                        else:
                            S.op("act", lambda e, sn=sn, d=d, nsb=nsb: e.activation(out=SBF[:, d * 2 + nsb, :], in_=sn, func=AF.Copy),
                                 reads=[tsn], writes=[tSBF[d][nsb]])
                    if saves and (d, c) in saves:
                        slot = saves[(d, c)]
                        S.op("pool", lambda e, sn=sn, slot=slot: e.tensor_copy(out=NSST[:, nsbuf, slot, :], in_=sn),
                             reads=[tsn], writes=[tNSST[nsbuf]])
                    yield
            for cx in ctxs:
                for (_, tp) in cx["pp"]:
                    unpin(tp)

        for h in range(int(os.environ.get("KHEADS", "8"))):
            if STAGE < 2:
                break
            wsl = h % 2
            wh, twh = whs[wsl], tWH[wsl]
            if h == 0:
                S.dma("pool", wh, w_hg[h].rearrange("(k p) n -> p k n", p=128), writes=[twh])
            S.dma("sp", SINI[:, wsl, 0, :], sa0[h], writes=[tSINI[wsl]])
            S.dma("sp", SINI[:, wsl, 1, :], sb0[h], writes=[tSINI[wsl]])
            S.op("act", lambda e, wsl=wsl, o_=S32P[:, 1, cur[1], :]: e.activation(out=o_, in_=SINI[:, wsl, 1, :], func=AF.Copy),
                 reads=[tSINI[wsl], tS32P[1][cur[1]]], writes=[tS32P[1][cur[1]]])
            seqB = [("h", b) for b in (3, 2, 1, 0)] + [("o", b) for b in (4, 3, 2)]

            def chain_b(cx, tgt):
                yield from hg_chain([cx], None, None, False)
                if tgt is not None:
                    S.op("pool", lambda e, tgt=tgt, i_=S32P[:, 1, cur[1], :]: e.tensor_copy(out=SBSV[:, tgt - 1, :], in_=i_),
                         reads=[tS32P[1][cur[1]], tSBSV], writes=[tSBSV])
                yield

            def proj_pre(kb, vpar):
                kind, b = kb
                if kind == "h":
                    htv = lambda k, b=b: HTH[:, k, b * 512:(b + 1) * 512]
                    htt = lambda k, i, b=b: HTH[:, k, b * 512 + i * 128: b * 512 + (i + 1) * 128]
                    thts = tHTH[b * 4:(b + 1) * 4]
                else:
                    htv = lambda k, b=b: HTO[:, k, b * 512:(b + 1) * 512]
                    htt = lambda k, i, b=b: HTO[:, k, b * 512 + i * 128: b * 512 + (i + 1) * 128]
                    thts = tHTO[b * 4:(b + 1) * 4]
                zps, tzps = pbank(pin=True)
                hg_proj(wh, twh, htv, thts, 2, zps, tzps)
                hg_vtok(wh, twh, htt, thts, vpar)
                return zps, tzps

            def tgt_of(kb):
                kind, b = kb
                return 4 if kind == "h" and b == 0 else (b - 1 if kind == "o" else None)

            idx = 0
            while idx < len(seqB):
                grp = seqB[idx:idx + 2]
                zs = [proj_pre(kb, gi) for gi, kb in enumerate(grp)]
                gens = [hg_prep(1, h, zs[gi][0], zs[gi][1], False, par=gi, vpar=gi, ts=(1 if gi == 0 else 0)) for gi in range(len(grp))]
                cxs = run_streams(gens)
                for gi, kb in enumerate(grp):
                    run_streams([chain_b(cxs[gi], tgt_of(kb))])
                idx += 2
            if h + 1 < 8:
                S.dma("pool", whs[(h + 1) % 2], w_hg[h + 1].rearrange("(k p) n -> p k n", p=128), writes=[tWH[(h + 1) % 2]])
            def proj_shared(b, vpar):
                htv = lambda k, b=b: HTO[:, k, b * 512:(b + 1) * 512]
                htt = lambda k, i, b=b: HTO[:, k, b * 512 + i * 128: b * 512 + (i + 1) * 128]
                thts = tHTO[b * 4:(b + 1) * 4]
                zq, tzq = pbank()
                hg_proj(wh, twh, htv, thts, 0, zq, tzq)
                S.op("act", lambda e, zq=zq: e.activation(out=QF, in_=zq[:], func=AF.Silu), reads=[tzq], writes=[tQF])
                zg, tzg = pbank()
                hg_proj(wh, twh, htv, thts, 3, zg, tzg)
                S.op("act", lambda e, zg=zg: e.activation(out=ZGSs[vpar], in_=zg[:], func=AF.Silu), reads=[tzg], writes=[tZGSs[vpar]])
                hg_vtok(wh, twh, htt, thts, vpar)

            proj_shared(0, 0)
            for b in range(NBLK if KP >= 6 else 0):
                vpar = b % 2
                htv = lambda k, b=b: HTO[:, k, b * 512:(b + 1) * 512]
                thts = tHTO[b * 4:(b + 1) * 4]
                za, tza = pbank(pin=True)
                hg_proj(wh, twh, htv, thts, 1, za, tza)
                zb, tzb = pbank(pin=True)
                hg_proj(wh, twh, htv, thts, 2, zb, tzb)
                cxa, cxb = run_streams([hg_prep(0, h, za, tza, True, vpar=vpar), hg_prep(1, h, zb, tzb, True, vpar=vpar)])
                resets, saves = (), None
                if b == 0:
                    resets = ((0, 0), (0, 4), (1, 7), (1, 3))
                    saves = {(0, 3): 0, (1, 0): 1, (0, 7): 2, (1, 4): 3}
                elif b == 1:
                    S.op("act", lambda e, wsl=wsl, o_=S32P[:, 0, cur[0], :]: e.activation(out=o_, in_=SINI[:, wsl, 0, :], func=AF.Copy),
                         reads=[tSINI[wsl], tS32P[0][cur[0]]], writes=[tS32P[0][cur[0]]])
                    S.op("act", lambda e, i_=S32P[:, 0, cur[0], :]: e.activation(out=SBF[:, 0, :], in_=i_, func=AF.Copy),
                         reads=[tS32P[0][cur[0]], tSBF[0][0]], writes=[tSBF[0][0]])
                if b >= 1:
                    S.op("act", lambda e, b=b, o_=S32P[:, 1, cur[1], :]: e.activation(out=o_, in_=SBSV[:, b - 1, :], func=AF.Copy),
                         reads=[tSBSV, tS32P[1][cur[1]]], writes=[tS32P[1][cur[1]]])
                    S.op("act", lambda e, i_=S32P[:, 1, cur[1], :]: e.activation(out=SBF[:, 2, :], in_=i_, func=AF.Copy),
                         reads=[tS32P[1][cur[1]], tSBF[1][0]], writes=[tSBF[1][0]])
                po, tpo = pbank(pin=True)
                if b + 1 < NBLK:
                    proj_shared(b + 1, (b + 1) % 2)
                run_streams([hg_chain([cxa, cxb], po, tpo, True, resets, saves, nsbuf=0)])
                if b == 0:
                    S.dma("sp", ns[:, :, h, :, :].rearrange("s d k v -> k (s d) v"), NSST[:, 0, :, :],
                          reads=[tNSST[0]], writes=[T()])
                osq, tosq = BT[9], tBT[9]
                S.op("act", lambda e, po=po, osq=osq: e.activation(out=osq, in_=po[:], func=AF.Square), reads=[tpo], writes=[tosq])
                pr, tpr = pbank()
                S.op("pe", lambda e, pr=pr, osq=osq: e.matmul(pr[:], lhsT=ONEB[:], rhs=osq, start=True, stop=True),
                     reads=[tosq, tCONST], writes=[tpr])
                rs, trs = FT[7], tFT[7]
                S.op("act", lambda e, pr=pr, rs=rs: e.activation(out=rs, in_=pr[:], func=AF.Ln, bias=EPSC[:, 2:3]),
                     reads=[tpr, tCONST], writes=[trs])
                S.op("act", lambda e, rs=rs: e.activation(out=rs, in_=rs, func=AF.Exp, scale=-0.5), reads=[trs], writes=[trs])
                S.op("dve", lambda e, po=po, rs=rs: e.tensor_tensor(out=rs, in0=po[:], in1=rs, op=ALU.mult),
                     reads=[tpo, trs], writes=[trs])
                S.op("dve", lambda e, rs=rs, h=h, b=b, vpar=vpar: e.scalar_tensor_tensor(
                    out=AB3[:, h, b * 512:(b + 1) * 512], in0=rs, scalar=OGS[:, h:h + 1], in1=ZGSs[vpar], op0=ALU.mult, op1=ALU.mult),
                    reads=[trs, tLB, tZGSs[vpar]], writes=[tAB[h][b]])
                unpin(tpo)
        S.barrier()
        if STAGE == 2:
            dump_debug()

        def load_weight(dst, tdst, src):
            S.dma("pool", dst, src, writes=[tdst])

        if STAGE >= 3:
            WO = WAR[:, 0:8192].rearrange("p (k n) -> p k n", k=8)
            tWO = T("wo")
            load_weight(WO, tWO, w_ho.rearrange("(k p) n -> p k n", p=128))
            for tt in range(20):
                s = tt % 2
                b = tt // 4
                S.dma("sp", XS[s], xo[tt * 128:(tt + 1) * 128, :], writes=[tXS[s]])
                for hf in range(2):
                    ps, tps = pbank()
                    for jj in range(4):
                        j = hf * 4 + jj
                        S.op("pe", lambda e, jj=jj, j=j, s=s, ps=ps: e.transpose(
                            out=ps[:, jj * 128:(jj + 1) * 128], in_=XS[s][:, j * 128:(j + 1) * 128], identity=IDF[:]),
                            reads=[tXS[s], tCONST], writes=[tps])
                    S.op("act", lambda e, ps=ps, hf=hf, tt=tt: e.activation(
                        out=XF[:, hf * 4:(hf + 1) * 4, tt * 128:(tt + 1) * 128],
                        in_=ps[:].rearrange("p (j t) -> p j t", j=4), func=AF.Copy),
                        reads=[tps], writes=[tX[hf * 4 + jj][b] for jj in range(4)])
            for b in range(NBLK):
                cond = 0 if b == 0 else 1
                for oc in range(8):
                    ps, tps = pbank()
                    for k in range(8):
                        S.op("pe", lambda e, k=k, oc=oc, b=b, ps=ps: e.matmul(
                            ps[:], lhsT=WO[:, k, oc * 128:(oc + 1) * 128], rhs=AB3[:, k, b * 512:(b + 1) * 512],
                            start=(k == 0), stop=(k == 7)), reads=[tWO, tAB[k][b]], writes=[tps])
                    resid_add(0, 2, oc, b, cond, ps[:], tps)
            S.barrier()
        if STAGE == 3:
            dump_debug()

        def mlp(l):
            for b in range(NBLK):
                cond = 0 if b == 0 else 1
                norm_featmajor(l, 1, b, cond, lambda j, b=b: AB3[:, j, b * 512:(b + 1) * 512], [tAB[j][b] for j in range(8)])
            W1 = [WAR[:, i * 4096:(i + 1) * 4096].rearrange("p (k n) -> p k n", k=8) for i in range(2)]
            W2 = [WAR[:, 8192 + i * 4096: 8192 + (i + 1) * 4096].rearrange("p (k n) -> p k n", k=4) for i in range(2)]
            tW1 = [T(), T()]
            tW2 = [T(), T()]
            HID = [FAR[:, i * 1024:(i + 1) * 1024].bitcast(BF16).rearrange("p (c t) -> p c t", c=4) for i in range(2)]
            tHID = [[T() for _ in range(4)] for _ in range(2)]
            it = 0
            def ld_mlp(G_):
                s_ = G_ % 2
                S.dma("pool", W1[s_], w1[l, :, G_ * 512:(G_ + 1) * 512].rearrange("(k p) n -> p k n", p=128), writes=[tW1[s_]])
                S.dma("pool", W2[s_], w2[l, G_ * 512:(G_ + 1) * 512, :].rearrange("(k p) n -> p k n", p=128), writes=[tW2[s_]])

            ld_mlp(0)
            for G in range(8):
                s = G % 2
                if G + 1 < 8:
                    ld_mlp(G + 1)
                for b in range(NBLK):
                    cond = 0 if b == 0 else 1
                    hs = it % 2
                    it += 1
                    for hc in range(4):
                        ps, tps = pbank()
                        for k in range(8):
                            S.op("pe", lambda e, k=k, hc=hc, b=b, ps=ps, s=s: e.matmul(
                                ps[:], lhsT=W1[s][:, k, hc * 128:(hc + 1) * 128], rhs=AB3[:, k, b * 512:(b + 1) * 512],
                                start=(k == 0), stop=(k == 7)), reads=[tW1[s], tAB[k][b]], writes=[tps])
                        r, tr = BT[hc % 4], tBT[hc % 4]
                        S.op("act", lambda e, ps=ps, r=r: e.activation(out=r, in_=ps[:], func=AF.Relu), reads=[tps], writes=[tr])
                        S.op("pool", lambda e, r=r, hs=hs, hc=hc: e.tensor_tensor(out=HID[hs][:, hc, :], in0=r, in1=r, op=ALU.mult),
                             reads=[tr], writes=[tHID[hs][hc]])
                    for oc in range(8):
                        ps, tps = pbank()
                        for k in range(4):
                            S.op("pe", lambda e, k=k, oc=oc, ps=ps, s=s, hs=hs: e.matmul(
                                ps[:], lhsT=W2[s][:, k, oc * 128:(oc + 1) * 128], rhs=HID[hs][:, k, :],
                                start=(k == 0), stop=(k == 3)), reads=[tW2[s], tHID[hs][k]], writes=[tps])
                        resid_add(l, 5, oc, b, cond, ps[:], tps)
            S.barrier()

        if STAGE >= 4:
            mlp(0)
        if STAGE == 4:
            dump_debug()

        if STAGE >= 5:
            CWI = ABUF[:, 0:16384].rearrange("p (k n) -> p k n", k=8)
            HTB = ABUF[:, 16384:20480].rearrange("p (k t) -> p k t", k=8)
            CWO = WAR[:, 0:8192].rearrange("p (k n) -> p k n", k=8)
            WST = WAR[:, 8192:9216].rearrange("p (g t) -> p g t", g=8)
            T2 = WAR[:, 10240:12288].bitcast(F32).rearrange("p (g t) -> p g t", g=8)
            UT = BAR[:, 0:4096].rearrange("p (k t) -> p k t", k=8)
            VN = [BAR[:, 4096:5120], BAR[:, 5120:6144]]
            tCWI, tCWO, tWST, tT2 = T(), T(), T(), T()
            tHTB = [T() for _ in range(8)]
            tUT = tBT[0:8]
            tVNs = [[tBT[8], tBT[9]], [tBT[10], tBT[11]]]
            load_weight(CWI, tCWI, cm_wi.rearrange("(k p) n -> p k n", p=128))
            load_weight(CWO, tCWO, cm_wo.rearrange("(k p) n -> p k n", p=128))
            load_weight(WST, tWST, wsT.rearrange("p (g t) -> p g t", g=8))
            S.dma("sp", XS[0], bsb, writes=[tXS[0]])
            for hf in range(2):
                ps, tps = pbank()
                S.op("pe", lambda e, ps=ps, hf=hf: e.matmul(ps[:], lhsT=ONEB[:], rhs=WAR[:, 8192 + hf * 512: 8192 + (hf + 1) * 512],
                                                            start=True, stop=True), reads=[tWST, tCONST], writes=[tps])
                for gg in range(4):
                    g = hf * 4 + gg
                    S.op("dve", lambda e, ps=ps, gg=gg, g=g: e.scalar_tensor_tensor(
                        out=T2[:, g, :], in0=ps[:, gg * 128:(gg + 1) * 128], scalar=VEC[:, 184 + g:185 + g],
                        in1=XS[0][:, g * 128:(g + 1) * 128], op0=ALU.mult, op1=ALU.add),
                        reads=[tps, tVEC, tXS[0]], writes=[tT2])
            S.barrier()
            for b in range(NBLK):
                cond = 0 if b == 0 else 1
                norm_featmajor(1, 0, b, cond, lambda j: HTB[:, j, :], tHTB)
                for uc in range(8):
                    ps, tps = pbank()
                    for k in range(8):
                        S.op("pe", lambda e, k=k, uc=uc, ps=ps: e.matmul(ps[:], lhsT=CWI[:, k, uc * 128:(uc + 1) * 128], rhs=HTB[:, k, :],
                                                                         start=(k == 0), stop=(k == 7)),
                             reads=[tCWI, tHTB[k]], writes=[tps])
                    S.op("act", lambda e, ps=ps, uc=uc: e.activation(out=UT[:, uc, :], in_=ps[:], func=AF.Gelu_apprx_tanh),
                         reads=[tps], writes=[tUT[uc]])
                def l1_tile(tt, b=b, cond=cond):
                    s = tt % 2
                    vg, tvg = XS[s], tXS[s]
                    st = SMALL[:, 16 + s * 8: 16 + s * 8 + 8]
                    for hf in range(2):
                        ps, tps = pbank()
                        for k in range(8):
                            S.op("pe", lambda e, k=k, hf=hf, tt=tt, ps=ps: e.matmul(
                                ps[:], lhsT=HTB[:, k, tt * 128:(tt + 1) * 128], rhs=CWI[:, k, 1024 + hf * 512: 1024 + (hf + 1) * 512],
                                start=(k == 0), stop=(k == 7)), reads=[tCWI, tHTB[k]], writes=[tps])
                        S.op("act", lambda e, ps=ps, hf=hf, vg=vg, st=st: e.activation(
                            out=vg[:, hf * 512:(hf + 1) * 512], in_=ps[:], func=AF.Gelu_apprx_tanh, accum_out=st[:, hf:hf + 1]),
                            reads=[tps], writes=[tvg, tSM2[s]])
                    S.op("act", lambda e, vg=vg, s=s, st=st: e.activation(out=VN[s], in_=vg, func=AF.Square, accum_out=st[:, 2:3]),
                         reads=[tvg], writes=tVNs[s] + [tSM2[s]])
                    yield
                    S.op("dve", lambda e, st=st: e.tensor_tensor(out=st[:, 3:4], in0=st[:, 0:1], in1=st[:, 1:2], op=ALU.add),
                         reads=[tSM2[s]], writes=[tSM2[s]])
                    yield
                    S.op("dve", lambda e, st=st: e.tensor_scalar(out=st[:, 3:4], in0=st[:, 3:4], scalar1=1.0 / D, scalar2=None, op0=ALU.mult),
                         reads=[tSM2[s]], writes=[tSM2[s]])
                    yield
                    S.op("dve", lambda e, st=st: e.tensor_tensor(out=st[:, 4:5], in0=st[:, 3:4], in1=st[:, 3:4], op=ALU.mult),
                         reads=[tSM2[s]], writes=[tSM2[s]])
                    yield
                    S.op("dve", lambda e, st=st: e.scalar_tensor_tensor(out=st[:, 5:6], in0=st[:, 2:3], scalar=1.0 / D, in1=st[:, 4:5],
                                                                        op0=ALU.mult, op1=ALU.subtract),
                         reads=[tSM2[s]], writes=[tSM2[s]])
                    yield
                    S.op("act", lambda e, st=st: e.activation(out=st[:, 5:6], in_=st[:, 5:6], func=AF.Sqrt, bias=EPSC[:, 0:1]),
                         reads=[tSM2[s], tCONST], writes=[tSM2[s]])
                    yield
                    S.op("dve", lambda e, st=st: e.reciprocal(out=st[:, 5:6], in_=st[:, 5:6]), reads=[tSM2[s]], writes=[tSM2[s]])
                    yield
                    S.op("dve", lambda e, vg=vg, s=s, st=st: e.tensor_scalar(out=VN[s], in0=vg, scalar1=st[:, 3:4], scalar2=st[:, 5:6],
                                                                             op0=ALU.subtract, op1=ALU.mult),
                         reads=[tvg, tSM2[s]], writes=tVNs[s])
                    yield
                    pm = [pbank(pin=True), pbank(pin=True)]
                    for g in range(8):
                        ps, tps = pm[g // 4]
                        gg = g % 4
                        S.op("pe", lambda e, g=g, gg=gg, ps=ps, s=s: e.matmul(ps[:, gg * 128:(gg + 1) * 128], lhsT=VN[s][:, g * 128:(g + 1) * 128],
                                                                              rhs=WST[:, g, :], start=True, stop=True),
                             reads=tVNs[s] + [tWST], writes=[tps])
                    for g in range(8):
                        ps, tps = pm[g // 4]
                        gg = g % 4
                        tmp, ttmp = FT[4 + g % 3], tFT[4 + g % 3]
                        S.op("dve", lambda e, g=g, gg=gg, ps=ps, tmp=tmp: e.scalar_tensor_tensor(
                            out=tmp[:, 0:128], in0=ps[:, gg * 128:(gg + 1) * 128], scalar=VEC[:, 176 + g:177 + g], in1=T2[:, g, :],
                            op0=ALU.mult, op1=ALU.add), reads=[tps, tVEC, tT2], writes=[ttmp])
                        S.op("pool", lambda e, g=g, tt=tt, tmp=tmp: e.tensor_tensor(
                            out=UT[:, g, tt * 128:(tt + 1) * 128], in0=UT[:, g, tt * 128:(tt + 1) * 128], in1=tmp[:, 0:128], op=ALU.mult),
                            reads=[ttmp, tUT[g]], writes=[tUT[g]])
                        if g % 2 == 1:
                            yield
                    unpin(pm[0][1])
                    unpin(pm[1][1])
                run_streams([l1_tile(0), l1_tile(1)])
                run_streams([l1_tile(2), l1_tile(3)])
                for oc in range(8):
                    ps, tps = pbank()
                    for k in range(8):
                        S.op("pe", lambda e, k=k, oc=oc, ps=ps: e.matmul(ps[:], lhsT=CWO[:, k, oc * 128:(oc + 1) * 128], rhs=UT[:, k, :],
                                                                         start=(k == 0), stop=(k == 7)),
                             reads=[tCWO, tUT[k]], writes=[tps])
                    resid_add(1, 2, oc, b, cond, ps[:], tps)
            S.barrier()
        if STAGE == 5:
            dump_debug()
        if STAGE >= 6:
            mlp(1)
        if STAGE == 6:
            dump_debug()

        ty = T("y")
        if STAGE >= 7:
            for b in range(NBLK):
                pr, tpr = pbank()
                for j in range(8):
                    sq, tsq = BT[8 + j % 4], tBT[8 + j % 4]
                    S.op("act", lambda e, j=j, sq=sq, b=b: e.activation(out=sq, in_=XB(j, b), func=AF.Square),
                         reads=[tX[j][b]], writes=[tsq])
                    S.op("pe", lambda e, j=j, sq=sq, pr=pr: e.matmul(pr[:], lhsT=ONEB[:], rhs=sq, start=(j == 0), stop=(j == 7)),
                         reads=[tsq, tCONST], writes=[tpr])
                rs, trs = FT[7], tFT[7]
                S.op("act", lambda e, pr=pr, rs=rs: e.activation(out=rs, in_=pr[:], func=AF.Ln, bias=EPSC[:, 1:2]),
                     reads=[tpr, tCONST], writes=[trs])
                S.op("act", lambda e, rs=rs: e.activation(out=rs, in_=rs, func=AF.Exp, scale=-0.5), reads=[trs], writes=[trs])
                for j in range(8):
                    S.op("dve", lambda e, j=j, b=b, rs=rs: e.scalar_tensor_tensor(
                        out=XB(j, b), in0=XB(j, b), scalar=FGS[:, j:j + 1], in1=rs, op0=ALU.mult, op1=ALU.mult),
                        reads=[tX[j][b], tLB, trs], writes=[tX[j][b]])
                for ti in range(4):
                    tt = b * 4 + ti
                    s = tt % 2
                    for hf in range(2):
                        ps, tps = pbank()
                        for jj in range(4):
                            j = hf * 4 + jj
                            S.op("pe", lambda e, jj=jj, j=j, tt=tt, ps=ps: e.transpose(
                                out=ps[:, jj * 128:(jj + 1) * 128], in_=XF[:, j, tt * 128:(tt + 1) * 128], identity=IDF[:]),
                                reads=[tX[j][b], tCONST], writes=[tps])
                        S.op("act", lambda e, ps=ps, hf=hf, s=s: e.activation(out=XS[s][:, hf * 512:(hf + 1) * 512], in_=ps[:], func=AF.Copy),
                             reads=[tps, tXS[s]], writes=[tXS[s]])
                    S.dma("sp", y[tt * 128:(tt + 1) * 128, :], XS[s], reads=[tXS[s]], writes=[ty])
        S.barrier()
        with nc.Block() as block:
            S.replay(block)
    return nc


_NC_CACHE = {}


def _prep_inputs(inp):
    f = lambda a: np.ascontiguousarray(np.asarray(a, dtype=np.float32))
    x_prompt, x_sample = f(inp["x_prompt"]), f(inp["x_sample"])
    st, c, c_ctx = f(inp["state_hgrn"]), f(inp["c"]), f(inp["c_ctx"])
    hw = f(inp["hgrn_w_in"])[0]
    fm = lambda v: np.ascontiguousarray(v.reshape(8, 128).T)
    maps = []
    for core in range(8):
        bq, half = core // 2, core % 2
        rev = half == 1
        R = (lambda a: a[::-1]) if rev else (lambda a: a)
        ca, cb = x_prompt[2 * core], x_prompt[2 * core + 1]
        lat = x_sample[bq]
        own = lat[half * 2048:(half + 1) * 2048]
        oth = lat[(1 - half) * 2048:(2 - half) * 2048]
        xo = np.concatenate([R(ca), R(cb), R(own)], axis=0)
        xh = R(oth)
        cT = np.zeros((128, 16), np.float32)
        cc = np.stack([c_ctx, c[bq]], axis=1)
        cT[:] = cc.reshape(8, 128, 2).transpose(1, 0, 2).reshape(128, 16)
        vec = np.zeros((128, NV), np.float32)
        for l in range(2):
            vec[:, l * 16:l * 16 + 8] = fm(inp["norm_mix_g"][l])
            vec[:, l * 16 + 8:l * 16 + 16] = fm(inp["norm_mlp_g"][l])
            ab = f(inp["ada_b"])[l].reshape(6, 8, 128)
            vec[:, 40 + l * 48: 88 + l * 48] = ab.transpose(2, 0, 1).reshape(128, 48)
        vec[:, 32:40] = fm(inp["final_norm_g"])
        vec[:, 136:144] = f(inp["hgrn_onorm_g"])[0].T
        lbl = f(inp["hgrn_lb_logits"])
        dA, dB = (1, 0) if rev else (0, 1)
        vec[:, 144:152] = fm(lbl[0, dA]); vec[:, 152:160] = fm(lbl[0, dB])
        vec[:, 160:168] = fm(lbl[1, dA]); vec[:, 168:176] = fm(lbl[1, dB])
        vec[:, 176:184] = fm(inp["cm_ln_g"][0]); vec[:, 184:192] = fm(inp["cm_ln_b"][0])
        q, ff, fb, ii, gg = [hw[:, k * 1024:(k + 1) * 1024] for k in range(5)]
        fA, fB = (fb, ff) if rev else (ff, fb)
        w_hg = np.stack([np.concatenate([m[:, h * 128:(h + 1) * 128] for m in (q, fA, fB, gg, ii)], axis=1) for h in range(8)])
        ws = f(inp["cm_w_s"])[0]
        bs = f(inp["cm_b_s"])[0]
        if rev:
            ws = ws[:, ::-1, ::-1]
            bs = bs[:, ::-1]
        wsT = np.ascontiguousarray(ws.transpose(2, 0, 1).reshape(128, 1024))
        bsb = np.ascontiguousarray(np.broadcast_to(bs.reshape(1, 1024), (128, 1024)))
        sa0 = st[bq, 0, 1 if rev else 0]
        sb0 = st[bq, 0, 0 if rev else 1]
        maps.append(dict(
            xo=np.ascontiguousarray(xo), xh=np.ascontiguousarray(xh), cT=cT, vecs=vec,
            ada_w=f(inp["ada_w"]), w_hg=np.ascontiguousarray(w_hg), w_ho=f(inp["hgrn_w_out"])[0],
            w1=f(inp["mlp_w1"]), w2=f(inp["mlp_w2"]), cm_wi=f(inp["cm_w_in"])[0], cm_wo=f(inp["cm_w_out"])[0],
            wsT=wsT, bsb=bsb, sa0=np.ascontiguousarray(sa0), sb0=np.ascontiguousarray(sb0)))
    return maps


def kernel(**inputs):
    maps = _prep_inputs(inputs)
    if "nc" not in _NC_CACHE:
        _NC_CACHE["nc"] = build_program()
    nc = _NC_CACHE["nc"]
    res = run_bass_kernel_spmd(nc, maps, core_ids=list(range(8)))
    y_prompt = np.zeros((16, 256, D), np.float32)
    y_sample = np.zeros((4, 4096, D), np.float32)
    new_state = np.zeros((16, 1, 2, 8, 128, 128), np.float32)
    for core in range(8):
        r = res.results[core]
        bq, half = core // 2, core % 2
        rev = half == 1
        R = (lambda a: a[::-1]) if rev else (lambda a: a)
        yy = np.asarray(r["y"])
        y_prompt[2 * core] = R(yy[0:256])
        y_prompt[2 * core + 1] = R(yy[256:512])
        y_sample[bq, half * 2048:(half + 1) * 2048] = R(yy[512:])
        nsr = np.asarray(r["ns"])
        for s in range(2):
            if rev:
                new_state[2 * core + s, 0, 0] = nsr[s, 1]
                new_state[2 * core + s, 0, 1] = nsr[s, 0]
            else:
                new_state[2 * core + s, 0, 0] = nsr[s, 0]
                new_state[2 * core + s, 0, 1] = nsr[s, 1]
        if DEBUG:
            _NC_CACHE.setdefault("dbg", {})[core] = (np.asarray(r["dbgx"]), np.asarray(r["dbga"]))
    return y_prompt, y_sample, new_state
```

```python
import os
import numpy as np
from contextlib import ExitStack
import concourse.bass as bass
import concourse.mybir as mybir
from concourse.bass_utils import run_bass_kernel_spmd

F32 = mybir.dt.float32
BF16 = mybir.dt.bfloat16
AF = mybir.ActivationFunctionType
ALU = mybir.AluOpType

D = 1024
KC = 8
TB = 512
NBLK = 5
NTOK = 2560
NOTH = 2048
EPS = 1e-6
NV = 192
DEBUG = bool(int(os.environ.get("KDEBUG", "0")))
STAGE = int(os.environ.get("KSTAGE", "99"))
KSUB = int(os.environ.get("KSUB", "99"))
KP = int(os.environ.get("KP", "99"))


class T:
    __slots__ = ("writer", "readers", "name")

    def __init__(self, name=""):
        self.writer = None
        self.readers = []
        self.name = name


class Sched:
    NDQ = 24

    def __init__(self, nc, es):
        self.nc = nc
        self.es = es
        self.engs = ["pe", "act", "dve", "pool", "sp"]
        self.sem, self.count, self.clock, self.hist = {}, {}, {}, {}
        self.prog = {e: [] for e in self.engs}
        for e in self.engs:
            self.sem[e] = es.enter_context(nc.semaphore("s_" + e))
            self.count[e] = 0
            self.clock[e] = {}
            self.hist[e] = {}
        self.dq = []
        for i in range(self.NDQ):
            q = "dq%d" % i
            self.sem[q] = es.enter_context(nc.semaphore("s_" + q))
            self.count[q] = 0
            self.hist[q] = {}
            self.dq.append(q)
        self.dq_next = {"sp": 0, "pool": 0}
        self.dq_set = {"sp": self.dq[:self.NDQ // 2], "pool": self.dq[self.NDQ // 2:]}

    def _waits_for(self, X, deps):
        clk = self.clock[X]
        need = {}
        for (E, n) in deps:
            if E == X and X == "pe":
                continue
            if clk.get(E, 0) >= n:
                continue
            if need.get(E, 0) < n:
                need[E] = n
        waits = []
        for E, n in need.items():
            assert n <= self.count[E]
            mult = 16 if E.startswith("dq") else 1
            waits.append((self.sem[E], n * mult))
            h = self.hist[E].get(n)
            if h:
                for k, v in h.items():
                    if k != X and clk.get(k, 0) < v:
                        clk[k] = v
            if clk.get(E, 0) < n:
                clk[E] = n
        return waits

    @staticmethod
    def _deps(reads, writes):
        deps = set()
        for t in reads:
            if t.writer:
                deps.add(t.writer)
        for t in writes:
            if t.writer:
                deps.add(t.writer)
            for r in t.readers:
                deps.add(r)
        return deps

    def op(self, X, fn, reads=(), writes=()):
        deps = self._deps(reads, writes)
        waits = self._waits_for(X, deps)
        self.count[X] += 1
        seq = self.count[X]
        h = dict(self.clock[X])
        h[X] = seq
        self.hist[X][seq] = h
        ident = (X, seq)
        for t in reads:
            t.readers.append(ident)
        for t in writes:
            t.writer = ident
            t.readers = []
        self.prog[X].append((waits, fn, (self.sem[X], 1)))
        return ident

    def dma(self, X, out_ap, in_ap, reads=(), writes=()):
        qs = self.dq_set[X]
        q = qs[self.dq_next[X]]
        self.dq_next[X] = (self.dq_next[X] + 1) % len(qs)
        deps = self._deps(reads, writes)
        if self.count[q] > 0:
            deps.add((q, self.count[q]))
        waits = self._waits_for(X, deps)
        self.count[q] += 1
        seq = self.count[q]
        h = dict(self.clock[X])
        h[q] = seq
        self.hist[q][seq] = h
        ident = (q, seq)
        for t in reads:
            t.readers.append(ident)
        for t in writes:
            t.writer = ident
            t.readers = []
        fn = lambda eng, o=out_ap, i=in_ap: eng.dma_start(out=o, in_=i)
        self.prog[X].append((waits, fn, (self.sem[q], 16)))
        return ident

    def barrier(self):
        allq = [(e, self.count[e]) for e in self.engs if self.count[e] > 0]
        allq += [(q, self.count[q]) for q in self.dq if self.count[q] > 0]
        for X in self.engs:
            waits = self._waits_for(X, set(d for d in allq if d[0] != X))
            if waits:
                self.prog[X].append((waits, None, None))

    def replay(self, block):
        sched = self

        def run(eng, X):
            for waits, fn, inc in sched.prog[X]:
                for (s, v) in waits:
                    eng.wait_ge(s, v)
                if fn is None:
                    continue
                ins = fn(eng)
                ins.then_inc(inc[0], inc[1])

        @block.tensor
        def _(e):
            run(e, "pe")

        @block.scalar
        def _(e):
            run(e, "act")

        @block.vector
        def _(e):
            run(e, "dve")

        @block.gpsimd
        def _(e):
            run(e, "pool")

        @block.sync
        def _(e):
            run(e, "sp")


def build_program():
    nc = bass.Bass("TRN2", target_bir_lowering=False)
    din = lambda n, s: nc.dram_tensor(n, list(s), F32, kind="ExternalInput").ap()
    xo = din("xo", [NTOK, D])
    xh = din("xh", [NOTH, D])
    cT = din("cT", [128, 16])
    vecs = din("vecs", [128, NV])
    ada_w = din("ada_w", [2, D, 6 * D])
    w_hg = din("w_hg", [8, D, 640])
    w_ho = din("w_ho", [D, D])
    w1 = din("w1", [2, D, 4 * D])
    w2 = din("w2", [2, 4 * D, D])
    cm_wi = din("cm_wi", [D, 2 * D])
    cm_wo = din("cm_wo", [D, D])
    wsT = din("wsT", [128, 1024])
    bsb = din("bsb", [128, 1024])
    sa0 = din("sa0", [8, 128, 128])
    sb0 = din("sb0", [8, 128, 128])
    y = nc.dram_tensor("y", [NTOK, D], F32, kind="ExternalOutput").ap()
    ns = nc.dram_tensor("ns", [2, 2, 8, 128, 128], F32, kind="ExternalOutput").ap()
    if DEBUG:
        dbgx = nc.dram_tensor("dbgx", [128, 40960], BF16, kind="ExternalOutput").ap()
        dbga = nc.dram_tensor("dbga", [128, 20480], BF16, kind="ExternalOutput").ap()

    with ExitStack() as es:
        S = Sched(nc, es)
        sb = lambda n, s, dt: es.enter_context(nc.sbuf_tensor(n, list(s), dt))
        XRAW = sb("xraw", [128, 40960], BF16)
        ABUF = sb("abuf", [128, 20480], BF16)
        WAR = sb("war", [128, 16384], BF16)
        FAR = sb("far", [128, 4096], F32)
        BAR = sb("bar", [128, 10240], BF16)
        VEC = sb("vec", [128, NV], F32)
        MOD = sb("mod", [128, 2 * 96], F32)
        AMOD = sb("amod", [128, 64], F32)
        LB = sb("lb", [128, 16], F32)
        OML = sb("oml", [128, 16], F32)
        OGS = sb("ogs", [128, 8], F32)
        FGS = sb("fgs", [128, 8], F32)
        CS = sb("cs", [128, 16], BF16)
        CTF = sb("ctf", [128, 16], F32)
        IDF = sb("idf", [128, 128], F32)
        IDB = sb("idb", [128, 128], BF16)
        ONEB = sb("oneb", [128, 128], BF16)
        ONEF = sb("onef", [128, 128], F32)
        MSKA = sb("mska", [128, 256], F32)
        MSKB = sb("mskb", [128, 256], F32)
        RST = sb("rst", [128, 512], F32)
        S32 = sb("s32", [128, 2, 128], F32)
        SBF = sb("sbf", [128, 4, 128], BF16)
        SBSV = sb("sbsv", [128, 4, 128], F32)
        SINI = sb("sini", [128, 2, 2, 128], F32)
        NSST = sb("nsst", [128, 1, 4, 128], F32)
        SMALL = sb("small", [128, 64], F32)
        EPSC = sb("epsc", [128, 4], F32)
        SMD = sb("smd", [128, 2, 2, 64], F32)
        PSF = [es.enter_context(nc.psum_tensor("psf%d" % i, [128, 512], F32)) for i in range(6)]
        PSB = [es.enter_context(nc.psum_tensor("psb%d" % i, [128, 1024], BF16)) for i in range(2)]
        tPSF = [T("psf%d" % i) for i in range(6)]
        tPSB = [T("psb%d" % i) for i in range(2)]
        rr = {"f": 0, "b": 0}

        pinned = set()

        def run_streams(gens):
            res = [None] * len(gens)
            alive = list(range(len(gens)))
            while alive:
                for gi in list(alive):
                    try:
                        next(gens[gi])
                    except StopIteration as st:
                        res[gi] = st.value
                        alive.remove(gi)
            return res

        def pbank(pin=False):
            i = rr["f"]
            n_try = 0
            while i in pinned:
                i = (i + 1) % 6
                n_try += 1
                assert n_try < 7, "all PSUM banks pinned"
            rr["f"] = (i + 1) % 6
            if pin:
                pinned.add(i)
            return PSF[i], tPSF[i]

        def unpin(tp):
            pinned.discard(tPSF.index(tp))

        def pbankb():
            i = rr["b"]
            rr["b"] = (i + 1) % 2
            return PSB[i], tPSB[i]

        XF = XRAW[:].bitcast(F32).rearrange("p (j t) -> p j t", j=8)
        tX = [[T("x%d_%d" % (j, b)) for b in range(NBLK)] for j in range(KC)]
        XB = lambda j, b: XF[:, j, b * TB:(b + 1) * TB]
        HTO = XRAW[:, 0:20480].rearrange("p (j t) -> p j t", j=8)
        HTH = XRAW[:, 20480:36864].rearrange("p (j t) -> p j t", j=8)
        tHTO = [T("hto%d" % b) for b in range(20)]
        tHTH = [T("hth%d" % b) for b in range(16)]
        AB3 = ABUF[:].rearrange("p (j t) -> p j t", j=8)
        tAB = [[T("ab%d_%d" % (j, b)) for b in range(NBLK)] for j in range(KC)]
        FT = [FAR[:, i * 512:(i + 1) * 512] for i in range(8)]
        tFT = [T("ft%d" % i) for i in range(8)]
        XS = [FAR[:, 0:1024], FAR[:, 1024:2048]]
        tXS = [T("xs0"), T("xs1")]
        BT = [BAR[:, i * 512:(i + 1) * 512] for i in range(20)]
        tBT = [T("bt%d" % i) for i in range(20)]
        tVEC, tMOD, tAMOD, tLB, tCS, tCONST, tSMALL = T(), T(), T(), T(), T(), T(), T()

        S.dma("sp", VEC[:], vecs, writes=[tVEC])
        S.dma("sp", CTF[:], cT, writes=[tCS])
        S.op("pool", lambda e: e.memset(ONEF[:], 1.0), writes=[tCONST])
        S.op("pool", lambda e: e.memset(ONEB[:], 1.0), writes=[tCONST])
        S.op("pool", lambda e: e.memset(EPSC[:, 0:1], EPS), reads=[tCONST], writes=[tCONST])
        S.op("pool", lambda e: e.memset(EPSC[:, 1:2], float(D * EPS)), reads=[tCONST], writes=[tCONST])
        S.op("pool", lambda e: e.memset(EPSC[:, 2:3], float(128 * EPS)), reads=[tCONST], writes=[tCONST])
        S.op("pool", lambda e: e.memset(EPSC[:, 3:4], 1.0), reads=[tCONST], writes=[tCONST])
        S.op("pool", lambda e: e.affine_select(out=IDF[:], in_=ONEF[:], pattern=[[-1, 128]],
                                               compare_op=ALU.is_equal, fill=0.0, base=0, channel_multiplier=1),
             reads=[tCONST], writes=[tCONST])
        S.op("pool", lambda e: e.tensor_copy(out=IDB[:], in_=IDF[:]), reads=[tCONST], writes=[tCONST])
        MD = [MSKA[:, 0:128], MSKB[:, 0:128]]
        MO = [MSKA[:, 128:256], MSKB[:, 128:256]]
        S.op("pool", lambda e: e.affine_select(out=MD[0], in_=ONEF[:], pattern=[[1, 128]],
                                               compare_op=ALU.is_ge, fill=0.0, base=0, channel_multiplier=-1),
             reads=[tCONST], writes=[tCONST])
        S.op("pool", lambda e: e.memset(MD[0][0:32, 32:128], 0.0), reads=[tCONST], writes=[tCONST])
        S.op("pool", lambda e: e.memset(MD[0][32:64, 64:128], 0.0), reads=[tCONST], writes=[tCONST])
        S.op("pool", lambda e: e.memset(MD[0][64:96, 96:128], 0.0), reads=[tCONST], writes=[tCONST])
        S.op("pool", lambda e: e.memset(MO[0], 0.0), reads=[tCONST], writes=[tCONST])
        S.op("pool", lambda e: e.memset(MO[0][0:32, 32:64], 1.0), reads=[tCONST], writes=[tCONST])
        S.op("pool", lambda e: e.memset(MO[0][64:96, 96:128], 1.0), reads=[tCONST], writes=[tCONST])
        pmk, tpmk = pbank()
        S.op("pe", lambda e: e.transpose(out=pmk[:, 0:128], in_=MD[0], identity=IDF[:]), reads=[tCONST], writes=[tpmk])
        S.op("pe", lambda e: e.transpose(out=pmk[:, 128:256], in_=MO[0], identity=IDF[:]), reads=[tCONST], writes=[tpmk])
        S.op("act", lambda e: e.activation(out=MSKB[:, 0:256], in_=pmk[:, 0:256], func=AF.Copy), reads=[tpmk, tCONST], writes=[tCONST])
        S.op("pool", lambda e: e.memset(RST[:], 1.0), reads=[tCONST], writes=[tCONST])
        S.op("pool", lambda e: e.memset(RST[:, 0::32], 0.0), reads=[tCONST], writes=[tCONST])
        S.op("pool", lambda e: e.memset(SMD[:], 1.0), reads=[tCONST], writes=[tCONST])
        S.op("pool", lambda e: e.memset(SBF[:], 0.0), reads=[tCONST], writes=[tCONST])
        S.op("dve", lambda e: e.tensor_tensor(out=LB[:], in0=VEC[:, 144:160], in1=VEC[:, 160:176], op=ALU.subtract),
             reads=[tVEC], writes=[tLB])
        S.op("act", lambda e: e.activation(out=LB[:], in_=LB[:], func=AF.Sigmoid), reads=[tLB], writes=[tLB])
        S.op("dve", lambda e: e.tensor_scalar(out=OML[:], in0=LB[:], scalar1=-1.0, scalar2=1.0, op0=ALU.mult, op1=ALU.add),
             reads=[tLB], writes=[tLB])
        S.op("dve", lambda e: e.tensor_scalar(out=OGS[:], in0=VEC[:, 136:144], scalar1=float(np.sqrt(128.0)), scalar2=None,
                                              op0=ALU.mult), reads=[tVEC], writes=[tLB])
        S.op("dve", lambda e: e.tensor_scalar(out=FGS[:], in0=VEC[:, 32:40], scalar1=32.0, scalar2=None, op0=ALU.mult),
             reads=[tVEC], writes=[tLB])
        S.op("act", lambda e: e.activation(out=CS[:], in_=CTF[:], func=AF.Silu), reads=[tCS], writes=[tCS])

        wslot = [WAR[:, i * 4096:(i + 1) * 4096].rearrange("p (k n) -> p k n", k=8) for i in range(4)]
        tW = [T("w%d" % i) for i in range(4)]
        pmL = [pbank(pin=True), pbank(pin=True)]
        ada_state = {"li": 0}

        def ada_chunk(l, m, hf):
            pm, tpm = pmL[l]
            s_ = ada_state["li"] % 4
            ada_state["li"] += 1
            src = ada_w[l, :, m * 1024 + hf * 512: m * 1024 + (hf + 1) * 512].rearrange("(k p) n -> p k n", p=128)
            S.dma("pool", wslot[s_], src, writes=[tW[s_]])
            for jj in range(4):
                col = (m * 8 + hf * 4 + jj) * 2
                for k in range(8):
                    S.op("pe", lambda e, s_=s_, jj=jj, k=k, col=col, pm=pm: e.matmul(
                        pm[:, col:col + 2], lhsT=wslot[s_][:, k, jj * 128:(jj + 1) * 128], rhs=CS[:, k * 2:k * 2 + 2],
                        start=(k == 0), stop=(k == 7)), reads=[tW[s_], tCS], writes=[tpm])

        tMODm = [[T() for _ in range(6)] for _ in range(2)]

        def ada_finish(l, m):
            pm, tpm = pmL[l]
            mo = MOD[:, l * 96 + m * 16: l * 96 + (m + 1) * 16].rearrange("p (c o) -> p c o", o=2)
            ab = VEC[:, 40 + l * 48 + m * 8: 40 + l * 48 + (m + 1) * 8].unsqueeze(2).broadcast_to([128, 8, 2])
            S.op("dve", lambda e: e.tensor_tensor(out=mo, in0=pm[:, m * 16:(m + 1) * 16].rearrange("p (c o) -> p c o", o=2),
                                                  in1=ab, op=ALU.add), reads=[tpm, tVEC], writes=[tMOD, tMODm[l][m]])

        def modv(l, m, j, cond):
            c = l * 96 + (m * 8 + j) * 2 + cond
            return MOD[:, c:c + 1]

        def amod(l, nrm, j, cond):
            c = ((l * 2 + nrm) * 8 + j) * 2 + cond
            return AMOD[:, c:c + 1]

        def amod_finish(l, nrm):
            scm = 1 + 3 * nrm
            base = l * 96 + scm * 16
            gcol = l * 16 + nrm * 8
            o = AMOD[:, (l * 2 + nrm) * 16:(l * 2 + nrm + 1) * 16].rearrange("p (j o) -> p j o", o=2)
            i0 = MOD[:, base:base + 16].rearrange("p (j o) -> p j o", o=2)
            gg = VEC[:, gcol:gcol + 8].unsqueeze(2).broadcast_to([128, 8, 2])
            S.op("dve", lambda e: e.tensor_scalar(out=o, in0=i0, scalar1=1.0, scalar2=None, op0=ALU.add),
                 reads=[tMOD], writes=[tAMOD])
            fac = 1.0 if (l == 0 and nrm == 0) else 32.0
            S.op("dve", lambda e: e.scalar_tensor_tensor(out=o, in0=o, scalar=fac, in1=gg, op0=ALU.mult, op1=ALU.mult),
                 reads=[tAMOD, tVEC], writes=[tAMOD])

        for (m_, hf_) in ((0, 0), (0, 1), (1, 0), (1, 1)):
            ada_chunk(0, m_, hf_)
        ada_finish(0, 0)
        ada_finish(0, 1)
        amod_finish(0, 0)
        ada_rest = [(0, m_, hf_) for m_ in range(2, 6) for hf_ in range(2)] + [(1, m_, hf_) for m_ in range(6) for hf_ in range(2)]

        tSM2 = [T(), T()]

        def norm_tokmajor(src_rows, tt, dstv, tdst, cond):
            s = tt % 2
            S.dma("sp", XS[s], src_rows, writes=[tXS[s]])
            tjunk = tBT[8 + 2 * s]
            junk = BAR[:, (8 + 2 * s) * 512:(10 + 2 * s) * 512]
            ss = SMALL[:, s * 4:s * 4 + 1]
            rs = SMALL[:, s * 4 + 1:s * 4 + 2]
            S.op("act", lambda e: e.activation(out=junk, in_=XS[s], func=AF.Square, accum_out=ss),
                 reads=[tXS[s]], writes=[tjunk, tSM2[s]])
            yield
            S.op("act", lambda e: e.activation(out=rs, in_=ss, func=AF.Sqrt, scale=1.0 / D, bias=EPSC[:, 0:1]),
                 reads=[tSM2[s], tCONST], writes=[tSM2[s]])
            yield
            S.op("dve", lambda e: e.reciprocal(out=rs, in_=rs), reads=[tSM2[s]], writes=[tSM2[s]])
            yield
            S.op("dve", lambda e: e.tensor_scalar(out=junk, in0=XS[s], scalar1=rs, scalar2=None, op0=ALU.mult),
                 reads=[tXS[s], tSM2[s], tjunk], writes=[tjunk])
            yield
            pb, tpb = PSB[s], tPSB[s]
            for j in range(8):
                S.op("pe", lambda e, j=j: e.transpose(out=pb[:, j * 128:(j + 1) * 128], in_=junk[:, j * 128:(j + 1) * 128],
                                                      identity=IDB[:]), reads=[tjunk, tCONST], writes=[tpb])
            yield
            for j in range(8):
                S.op("act", lambda e, j=j: e.activation(out=dstv(j), in_=pb[:, j * 128:(j + 1) * 128], func=AF.Identity,
                                                        scale=amod(0, 0, j, cond), bias=modv(0, 0, j, cond)),
                     reads=[tpb, tAMOD, tMOD], writes=[tdst])
                if j % 4 == 3:
                    yield

        def norm_featmajor(l, nrm, b, cond, dstv, tdsts):
            pr, tpr = pbank()
            for j in range(8):
                sq, tsq = BT[8 + j % 4], tBT[8 + j % 4]
                S.op("act", lambda e, j=j, sq=sq: e.activation(out=sq, in_=XB(j, b), func=AF.Square),
                     reads=[tX[j][b]], writes=[tsq])
                S.op("pe", lambda e, j=j, sq=sq: e.matmul(pr[:], lhsT=ONEB[:], rhs=sq, start=(j == 0), stop=(j == 7)),
                     reads=[tsq, tCONST], writes=[tpr])
            rs, trs = FT[7], tFT[7]
            S.op("act", lambda e: e.activation(out=rs, in_=pr[:], func=AF.Ln, bias=EPSC[:, 1:2]), reads=[tpr, tCONST], writes=[trs])
            S.op("act", lambda e: e.activation(out=rs, in_=rs, func=AF.Exp, scale=-0.5), reads=[trs], writes=[trs])
            for j in range(8):
                tmp, ttmp = FT[4 + j % 3], tFT[4 + j % 3]
                S.op("dve", lambda e, j=j, tmp=tmp: e.tensor_tensor(out=tmp, in0=XB(j, b), in1=rs, op=ALU.mult),
                     reads=[tX[j][b], trs], writes=[ttmp])
                sh = modv(l, 3 * nrm, j, cond)
                S.op("act", lambda e, j=j, tmp=tmp, sh=sh: e.activation(out=dstv(j), in_=tmp, func=AF.Identity,
                                                                        scale=amod(l, nrm, j, cond), bias=sh),
                     reads=[ttmp, tAMOD, tMOD], writes=[tdsts[j]])

        def resid_add(l, m, oc, b, cond, ps, tps):
            S.op("dve", lambda e: e.scalar_tensor_tensor(out=XB(oc, b), in0=ps, scalar=modv(l, m, oc, cond), in1=XB(oc, b),
                                                         op0=ALU.mult, op1=ALU.add),
                 reads=[tps, tMOD, tX[oc][b]], writes=[tX[oc][b]])

        def dump_debug():
            if DEBUG:
                S.barrier()
                td = T()
                S.dma("sp", dbgx, XRAW[:], writes=[td])
                S.dma("sp", dbga, ABUF[:], writes=[td])

        tiles0b = []
        for tt in range(20):
            cond = 0 if tt < 4 else 1
            tiles0b.append((xo[tt * 128:(tt + 1) * 128, :], tt, (lambda j, tt=tt: HTO[:, j, tt * 128:(tt + 1) * 128]), tHTO[tt], cond))
        for tt in range(16):
            tiles0b.append((xh[tt * 128:(tt + 1) * 128, :], tt, (lambda j, tt=tt: HTH[:, j, tt * 128:(tt + 1) * 128]), tHTH[tt], 1))
        for pi in range(0, 36, 2):
            run_streams([norm_tokmajor(*tiles0b[pi]), norm_tokmajor(*tiles0b[pi + 1])])
            if ada_rest:
                ada_chunk(*ada_rest.pop(0))
            if ada_rest and pi % 4 == 0:
                ada_chunk(*ada_rest.pop(0))
        while ada_rest:
            ada_chunk(*ada_rest.pop(0))
        for l_ in range(2):
            for m_ in range(6):
                if not (l_ == 0 and m_ < 2):
                    ada_finish(l_, m_)
        amod_finish(0, 1)
        amod_finish(1, 0)
        amod_finish(1, 1)
        unpin(pmL[0][1])
        unpin(pmL[1][1])
        S.barrier()

        tS32 = [T("s32a"), T("s32b")]
        S32P = WAR[:, 12288:13312].bitcast(F32).rearrange("p (d q v) -> p d q v", d=2, q=2)
        tS32P = [[T("s32p00"), T("s32p01")], [T("s32p10"), T("s32p11")]]
        cur = [0, 0]
        tSBF = [[T("sbfa0"), T("sbfa1")], [T("sbfb0"), T("sbfb1")]]
        tSBSV, tSINI, tNSST = T(), [T(), T()], [T(), T()]
        whs = [WAR[:, i * 5120:(i + 1) * 5120].rearrange("p (k n) -> p k n", k=8) for i in range(2)]
        tWH = [T("wh0"), T("wh1")]
        dF = [FT[0], FT[1]]; tdF = [tFT[0], tFT[1]]
        dG = [FT[2], FT[3]]; tdG = [tFT[2], tFT[3]]
        dB = [FT[4], FT[5]]; tdB = [tFT[4], tFT[5]]
        QF, tQF = FT[6], tFT[6]
        dQB = [BT[0], BT[1]]; tdQB = [tBT[0], tBT[1]]
        dKM = [BT[2], BT[3]]; tdKM = [tBT[2], tBT[3]]
        dK2 = [BT[4], BT[5]]; tdK2 = [tBT[4], tBT[5]]
        dKT = [BT[6], BT[7]]; tdKT = [tBT[6], tBT[7]]
        dKTo = [BT[12], BT[13]]; tdKTo = [tBT[12], tBT[13]]
        dKH = [BT[14], BT[15]]; tdKH = [tBT[14], tBT[15]]
        dAT2 = [BT[16], BT[17]]; tdAT2 = [tBT[16], tBT[17]]
        tSMD = [[T("smd00"), T("smd01")], [T("smd10"), T("smd11")]]
        E2 = [WAR[:, 10240 + i * 1024: 10240 + (i + 1) * 1024].bitcast(F32) for i in range(2)]
        tE2 = [T("e2a"), T("e2b")]
        E3 = [WAR[:, 13312 + i * 1024: 13312 + (i + 1) * 1024].bitcast(F32) for i in range(2)]
        tE3 = [T("e3a"), T("e3b")]
        tDUM = T("dummy")
        for d_ in range(2):
            S.op("pool", lambda e, d_=d_: e.memset(dKT[d_][64:128, :], 0.0), writes=[tdKT[d_]])
            S.op("pool", lambda e, d_=d_: e.memset(dKTo[d_][0:64, :], 0.0), writes=[tdKTo[d_]])
        dAT = [BT[8], BT[9]]; tdAT = [tBT[8], tBT[9]]
        ZGSs, tZGSs = [BT[10], BT[18]], [tBT[10], tBT[18]]
        VTKs, tVTKs = [BT[11], BT[19]], [tBT[11], tBT[19]]
        VTK3s = [v.rearrange("p (i v) -> p i v", i=4) for v in VTKs]

        def hg_proj(wh, twh, htv, thts, col, ps, tps):
            for k in range(8):
                S.op("pe", lambda e, k=k: e.matmul(ps[:], lhsT=wh[:, k, col * 128:(col + 1) * 128], rhs=htv(k),
                                                   start=(k == 0), stop=(k == 7)), reads=[twh] + thts, writes=[tps])

        def hg_vtok(wh, twh, httile, thts, vpar):
            VTK, tVTK = VTKs[vpar], tVTKs[vpar]
            ps, tps = pbank()
            for i in range(4):
                for k in range(8):
                    S.op("pe", lambda e, i=i, k=k: e.matmul(ps[:, i * 128:(i + 1) * 128], lhsT=httile(k, i), rhs=wh[:, k, 512:640],
                                                            start=(k == 0), stop=(k == 7)), reads=[twh] + thts, writes=[tps])
            S.op("act", lambda e: e.activation(out=VTK, in_=ps[:], func=AF.Copy), reads=[tps], writes=[tVTK])

        def hg_prep(d, h, zps, tzps, need_o, par=0, vpar=0, ts=None):
            if ts is None:
                ts = d
            F, tF, G, tG, Bc, tB = dF[ts], tdF[ts], dG[ts], tdG[ts], dB[ts], tdB[ts]
            VTK3, tVTK = VTK3s[vpar], tVTKs[vpar]
            lbc = LB[:, d * 8 + h:d * 8 + h + 1]
            omc = OML[:, d * 8 + h:d * 8 + h + 1]
            one1 = EPSC[:, 3:4]
            S.op("act", lambda e: e.activation(out=F, in_=zps[:], func=AF.Exp, scale=-1.0), reads=[tzps], writes=[tF])
            if tzps in tPSF:
                unpin(tzps)
            yield
            S.op("act", lambda e: e.activation(out=G, in_=F, func=AF.Ln, scale=lbc, bias=one1), reads=[tF, tLB, tCONST], writes=[tG])
            yield
            S.op("act", lambda e: e.activation(out=Bc, in_=F, func=AF.Ln, bias=one1), reads=[tF, tCONST], writes=[tB])
            yield
            S.op("dve", lambda e: e.tensor_tensor(out=G, in0=G, in1=Bc, op=ALU.subtract), reads=[tG, tB], writes=[tG])
            yield
            XK, tXK = E3[ts], tE3[ts]
            S.op("act", lambda e: e.activation(out=XK, in_=Bc, func=AF.Exp, scale=-1.0), reads=[tB, tXK], writes=[tXK])
            yield
            v16 = lambda ap: ap.rearrange("p (c t) -> p c t", t=32)
            if d == 0:
                S.op("dve", lambda e: e.tensor_tensor_scan(out=Bc, data0=RST[:], data1=G, initial=0.0, op0=ALU.mult, op1=ALU.add),
                     reads=[tG, tCONST], writes=[tB])
                lastc = 31
            else:
                S.op("dve", lambda e: e.tensor_tensor_scan(out=Bc[:, ::-1], data0=RST[:], data1=G[:, ::-1], initial=0.0,
                                                           op0=ALU.mult, op1=ALU.add), reads=[tG, tCONST], writes=[tB])
                lastc = 0
            yield
            S.op("dve", lambda e: e.scalar_tensor_tensor(out=F, in0=F, scalar=omc, in1=XK, op0=ALU.mult, op1=ALU.mult),
                 reads=[tF, tXK, tLB], writes=[tF])
            yield
            bbT, tbb, X, tX_ = Bc, tB, G, tG
            QB, tQB, KM, tKM, KT, tKT, AT, tAT = (dQB[ts], tdQB[ts], dKM[ts], tdKM[ts], dKT[ts], tdKT[ts], dAT[ts], tdAT[ts])
            QD, tQD, KH, tKH, AT2, tAT2 = dK2[ts], tdK2[ts], dKH[ts], tdKH[ts], dAT2[ts], tdAT2[ts]
            sm = SMD[:, ts, par, :]
            EDC, EC, EH, EH2 = sm[:, 0:16], sm[:, 16:24], sm[:, 32:48], sm[:, 48:64]
            tsm = tSMD[ts][par]
            X2, tX2 = E2[ts], tE2[ts]
            S.op("act", lambda e: e.activation(out=EDC, in_=bbT[:, lastc::32], func=AF.Exp), reads=[tbb, tsm], writes=[tsm])
            yield
            S.op("dve", lambda e: e.tensor_tensor(out=EC, in0=EDC[:, 0::2], in1=EDC[:, 1::2], op=ALU.mult), reads=[tsm], writes=[tsm])
            yield
            if d == 0:
                if need_o:
                    S.op("dve", lambda e: e.tensor_copy(out=EH[:, 1::2], in_=EDC[:, 0::2]), reads=[tsm], writes=[tsm])
                    yield
                S.op("dve", lambda e: e.tensor_copy(out=EH2[:, 0::2], in_=EDC[:, 1::2]), reads=[tsm], writes=[tsm])
                yield
            else:
                if need_o:
                    S.op("dve", lambda e: e.tensor_copy(out=EH[:, 0::2], in_=EDC[:, 1::2]), reads=[tsm], writes=[tsm])
                    yield
                S.op("dve", lambda e: e.tensor_copy(out=EH2[:, 1::2], in_=EDC[:, 0::2]), reads=[tsm], writes=[tsm])
                yield
            if need_o:
                S.op("act", lambda e: e.activation(out=X2, in_=bbT, func=AF.Exp, scale=-1.0), reads=[tbb, tX2], writes=[tX2])
                yield
                S.op("dve", lambda e: e.tensor_tensor(out=KM, in0=F, in1=X2, op=ALU.mult), reads=[tF, tX2], writes=[tKM])
                yield
                S.op("act", lambda e: e.activation(out=X, in_=bbT, func=AF.Exp), reads=[tbb, tX_], writes=[tX_])
                yield
                Y, tY = X2, tX2
            else:
                Y, tY = X, tX_
            blh = bbT[:, lastc::32].unsqueeze(2).broadcast_to([128, 16, 32])
            S.op("dve", lambda e: e.tensor_tensor(out=v16(Y), in0=blh, in1=v16(bbT), op=ALU.subtract),
                 reads=[tbb, tY], writes=[tY])
            yield
            S.op("act", lambda e: e.activation(out=Y, in_=Y, func=AF.Exp), reads=[tY], writes=[tY])
            yield
            S.op("dve", lambda e: e.tensor_tensor(out=KH, in0=F, in1=Y, op=ALU.mult), reads=[tF, tY], writes=[tKH])
            yield
            K2, tK2 = AT2, tAT2
            S.op("dve", lambda e: e.tensor_tensor(out=v16(K2), in0=v16(KH), in1=EH2.unsqueeze(2).broadcast_to([128, 16, 32]),
                                                  op=ALU.mult), reads=[tKH, tsm], writes=[tK2])
            yield
            if need_o:
                S.op("pool", lambda e: e.tensor_tensor(out=QD, in0=QF, in1=X, op=ALU.mult), reads=[tQF, tX_, tK2], writes=[tQD])
                yield
                S.op("pool", lambda e: e.tensor_tensor(out=v16(QB), in0=v16(QD), in1=EH.unsqueeze(2).broadcast_to([128, 16, 32]),
                                                       op=ALU.mult), reads=[tQD, tsm], writes=[tQB])
                yield
            if KP < 3:
                return None
            pb, tpb = pbankb()
            for i in range(4):
                S.op("pe", lambda e, i=i: e.transpose(out=pb[:, i * 128:(i + 1) * 128], in_=K2[:, i * 128:(i + 1) * 128],
                                                      identity=IDB[:]), reads=[tK2, tCONST], writes=[tpb])
                yield
            KTo, tKTo = dKTo[ts], tdKTo[ts]
            S.op("act", lambda e: e.activation(out=KT[0:64, :], in_=pb[0:64, 0:512], func=AF.Copy), reads=[tpb, tKT], writes=[tKT])
            yield
            S.op("dve", lambda e: e.tensor_copy(out=KTo[64:128, :], in_=pb[64:128, 0:512]), reads=[tpb, tKTo], writes=[tKTo])
            yield
            if need_o:
                pa, tpa = pbank(pin=True)
                pa2, tpa2 = pbank(pin=True)
                for i in range(4):
                    S.op("pe", lambda e, i=i: e.matmul(pa[:, i * 128:(i + 1) * 128], lhsT=KM[:, i * 128:(i + 1) * 128],
                                                       rhs=QD[:, i * 128:(i + 1) * 128], start=True, stop=True),
                         reads=[tKM, tQD], writes=[tpa])
                    yield
                for i in range(4):
                    S.op("pe", lambda e, i=i: e.matmul(pa2[:, i * 128:(i + 1) * 128], lhsT=KH[:, i * 128:(i + 1) * 128],
                                                       rhs=QD[:, i * 128:(i + 1) * 128], start=True, stop=True),
                         reads=[tKH, tQD], writes=[tpa2])
                    yield
                v4 = lambda ap: ap.rearrange("p (i t) -> p i t", i=4)
                S.op("dve", lambda e: e.tensor_tensor(out=v4(AT), in0=v4(pa[:]), in1=MD[d].unsqueeze(1).broadcast_to([128, 4, 128]),
                                                      op=ALU.mult), reads=[tpa, tCONST], writes=[tAT])
                yield
                S.op("dve", lambda e: e.tensor_tensor(out=v4(AT2), in0=v4(pa2[:]), in1=MO[d].unsqueeze(1).broadcast_to([128, 4, 128]),
                                                      op=ALU.mult), reads=[tpa2, tCONST, tKT, tKTo], writes=[tAT2])
                unpin(tpa)
                unpin(tpa2)
                yield
            if KP < 4:
                return None
            KT3 = [KT.rearrange("p (i k) -> p i k", i=4), KTo.rearrange("p (i k) -> p i k", i=4)]
            pp = [pbank(pin=True), pbank(pin=True)]
            for c in range(8):
                i, r = c // 2, c % 2
                ps, tps = pp[c // 4]
                cc = c % 4
                S.op("pe", lambda e, i=i, r=r, ps=ps, cc=cc: e.matmul(ps[:, cc * 128:(cc + 1) * 128], lhsT=KT3[r][:, i, :],
                                                                      rhs=VTK3[:, i, :], start=True, stop=True),
                     reads=[tKT, tKTo, tVTK], writes=[tps])
                yield
            return dict(d=d, EC=EC, tsm=tsm, QB=QB, tQB=tQB, AT=AT, tAT=tAT, AT2=AT2, tAT2=tAT2, pp=pp, VTK3=VTK3, tVTK=tVTK)

        def hg_chain(ctxs, po, tpo, need_o, resets=(), saves=None, nsbuf=0):
            if KP < 5 or any(c is None for c in ctxs):
                return
            yield
            if need_o:
                first = True
                for cx in ctxs:
                    for (nm, tnm) in (("AT", "tAT"), ("AT2", "tAT2")):
                        AT3 = cx[nm].rearrange("p (i t) -> p i t", i=4)
                        for i in range(4):
                            S.op("pe", lambda e, i=i, AT3=AT3, first=first: e.matmul(
                                po[:, i * 128:(i + 1) * 128], lhsT=cx["VTK3"][:, i, :], rhs=AT3[:, i, :], start=first, stop=False,
                                skip_group_check=True), reads=[cx["tVTK"], cx[tnm]], writes=[tpo])
                            first = False
            nd = len(ctxs)
            for step in range(8):
                for xi, cx in enumerate(ctxs):
                    d = cx["d"]
                    c = step if d == 0 else 7 - step
                    sbi = step % 2
                    q_ = cur[d]
                    sv, tsv = S32P[:, d, q_, :], tS32P[d][q_]
                    sn, tsn = S32P[:, d, 1 - q_, :], tS32P[d][1 - q_]
                    if (d, c) in resets:
                        S.op("pool", lambda e, sv=sv: e.memset(sv, 0.0), reads=[tsv], writes=[tsv])
                        S.op("pool", lambda e, d=d, sbi=sbi: e.memset(SBF[:, d * 2 + sbi, :], 0.0),
                             reads=[tSBF[d][sbi]], writes=[tSBF[d][sbi]])
                    if need_o:
                        last = (xi == nd - 1)
                        S.op("pe", lambda e, d=d, c=c, sbi=sbi, cx=cx, last=last: e.matmul(
                            po[:, c * 64:(c + 1) * 64], lhsT=SBF[:, d * 2 + sbi, :], rhs=cx["QB"][:, c * 64:(c + 1) * 64],
                            start=False, stop=last, skip_group_check=True),
                            reads=[tSBF[d][sbi], cx["tQB"]], writes=[tpo])
                    ps, tps = cx["pp"][c // 4]
                    cc = c % 4
                    esc = cx["EC"][:, c:c + 1]
                    S.op("dve", lambda e, sv=sv, sn=sn, ps=ps, cc=cc, esc=esc: e.scalar_tensor_tensor(
                        out=sn, in0=sv, scalar=esc, in1=ps[:, cc * 128:(cc + 1) * 128], op0=ALU.mult, op1=ALU.add),
                        reads=[tsv, cx["tsm"], tps], writes=[tsn])
                    cur[d] = 1 - q_
                    if need_o:
                        nsb = (step + 1) % 2
                        if d == 0:
                            S.op("pool", lambda e, sn=sn, d=d, nsb=nsb: e.tensor_copy(out=SBF[:, d * 2 + nsb, :], in_=sn),
                                 reads=[tsn], writes=[tSBF[d][nsb]])
                        else:
                            S.op("act", lambda e, sn=sn, d=d, nsb=nsb: e.activation(out=SBF[:, d * 2 + nsb, :], in_=sn, func=AF.Copy),
                                 reads=[tsn], writes=[tSBF[d][nsb]])
                    if saves and (d, c) in saves:
                        slot = saves[(d, c)]
                        S.op("pool", lambda e, sn=sn, slot=slot: e.tensor_copy(out=NSST[:, nsbuf, slot, :], in_=sn),
                             reads=[tsn], writes=[tNSST[nsbuf]])
                    yield
            for cx in ctxs:
                for (_, tp) in cx["pp"]:
                    unpin(tp)

        for h in range(int(os.environ.get("KHEADS", "8"))):
            if STAGE < 2:
                break
            wsl = h % 2
            wh, twh = whs[wsl], tWH[wsl]
            if h == 0:
                S.dma("pool", wh, w_hg[h].rearrange("(k p) n -> p k n", p=128), writes=[twh])
            S.dma("sp", SINI[:, wsl, 0, :], sa0[h], writes=[tSINI[wsl]])
            S.dma("sp", SINI[:, wsl, 1, :], sb0[h], writes=[tSINI[wsl]])
            S.op("act", lambda e, wsl=wsl, o_=S32P[:, 1, cur[1], :]: e.activation(out=o_, in_=SINI[:, wsl, 1, :], func=AF.Copy),
                 reads=[tSINI[wsl], tS32P[1][cur[1]]], writes=[tS32P[1][cur[1]]])
            seqB = [("h", b) for b in (3, 2, 1, 0)] + [("o", b) for b in (4, 3, 2)]

            def chain_b(cx, tgt):
                yield from hg_chain([cx], None, None, False)
                if tgt is not None:
                    S.op("pool", lambda e, tgt=tgt, i_=S32P[:, 1, cur[1], :]: e.tensor_copy(out=SBSV[:, tgt - 1, :], in_=i_),
                         reads=[tS32P[1][cur[1]], tSBSV], writes=[tSBSV])
                yield

            def proj_pre(kb, vpar):
                kind, b = kb
                if kind == "h":
                    htv = lambda k, b=b: HTH[:, k, b * 512:(b + 1) * 512]
                    htt = lambda k, i, b=b: HTH[:, k, b * 512 + i * 128: b * 512 + (i + 1) * 128]
                    thts = tHTH[b * 4:(b + 1) * 4]
                else:
                    htv = lambda k, b=b: HTO[:, k, b * 512:(b + 1) * 512]
                    htt = lambda k, i, b=b: HTO[:, k, b * 512 + i * 128: b * 512 + (i + 1) * 128]
                    thts = tHTO[b * 4:(b + 1) * 4]
                zps, tzps = pbank(pin=True)
                hg_proj(wh, twh, htv, thts, 2, zps, tzps)
                hg_vtok(wh, twh, htt, thts, vpar)
                return zps, tzps

            def tgt_of(kb):
                kind, b = kb
                return 4 if kind == "h" and b == 0 else (b - 1 if kind == "o" else None)

            idx = 0
            while idx < len(seqB):
                grp = seqB[idx:idx + 2]
                zs = [proj_pre(kb, gi) for gi, kb in enumerate(grp)]
                gens = [hg_prep(1, h, zs[gi][0], zs[gi][1], False, par=gi, vpar=gi, ts=(1 if gi == 0 else 0)) for gi in range(len(grp))]
                cxs = run_streams(gens)
                for gi, kb in enumerate(grp):
                    run_streams([chain_b(cxs[gi], tgt_of(kb))])
                idx += 2
            if h + 1 < 8:
                S.dma("pool", whs[(h + 1) % 2], w_hg[h + 1].rearrange("(k p) n -> p k n", p=128), writes=[tWH[(h + 1) % 2]])
            def proj_shared(b, vpar):
                htv = lambda k, b=b: HTO[:, k, b * 512:(b + 1) * 512]
                htt = lambda k, i, b=b: HTO[:, k, b * 512 + i * 128: b * 512 + (i + 1) * 128]
                thts = tHTO[b * 4:(b + 1) * 4]
                zq, tzq = pbank()
                hg_proj(wh, twh, htv, thts, 0, zq, tzq)
                S.op("act", lambda e, zq=zq: e.activation(out=QF, in_=zq[:], func=AF.Silu), reads=[tzq], writes=[tQF])
                zg, tzg = pbank()
                hg_proj(wh, twh, htv, thts, 3, zg, tzg)
                S.op("act", lambda e, zg=zg: e.activation(out=ZGSs[vpar], in_=zg[:], func=AF.Silu), reads=[tzg], writes=[tZGSs[vpar]])
                hg_vtok(wh, twh, htt, thts, vpar)

            proj_shared(0, 0)
            for b in range(NBLK if KP >= 6 else 0):
                vpar = b % 2
                htv = lambda k, b=b: HTO[:, k, b * 512:(b + 1) * 512]
                thts = tHTO[b * 4:(b + 1) * 4]
                za, tza = pbank(pin=True)
                hg_proj(wh, twh, htv, thts, 1, za, tza)
                zb, tzb = pbank(pin=True)
                hg_proj(wh, twh, htv, thts, 2, zb, tzb)
                cxa, cxb = run_streams([hg_prep(0, h, za, tza, True, vpar=vpar), hg_prep(1, h, zb, tzb, True, vpar=vpar)])
                resets, saves = (), None
                if b == 0:
                    resets = ((0, 0), (0, 4), (1, 7), (1, 3))
                    saves = {(0, 3): 0, (1, 0): 1, (0, 7): 2, (1, 4): 3}
                elif b == 1:
                    S.op("act", lambda e, wsl=wsl, o_=S32P[:, 0, cur[0], :]: e.activation(out=o_, in_=SINI[:, wsl, 0, :], func=AF.Copy),
                         reads=[tSINI[wsl], tS32P[0][cur[0]]], writes=[tS32P[0][cur[0]]])
                    S.op("act", lambda e, i_=S32P[:, 0, cur[0], :]: e.activation(out=SBF[:, 0, :], in_=i_, func=AF.Copy),
                         reads=[tS32P[0][cur[0]], tSBF[0][0]], writes=[tSBF[0][0]])
                if b >= 1:
                    S.op("act", lambda e, b=b, o_=S32P[:, 1, cur[1], :]: e.activation(out=o_, in_=SBSV[:, b - 1, :], func=AF.Copy),
                         reads=[tSBSV, tS32P[1][cur[1]]], writes=[tS32P[1][cur[1]]])
                    S.op("act", lambda e, i_=S32P[:, 1, cur[1], :]: e.activation(out=SBF[:, 2, :], in_=i_, func=AF.Copy),
                         reads=[tS32P[1][cur[1]], tSBF[1][0]], writes=[tSBF[1][0]])
                po, tpo = pbank(pin=True)
                if b + 1 < NBLK:
                    proj_shared(b + 1, (b + 1) % 2)
                run_streams([hg_chain([cxa, cxb], po, tpo, True, resets, saves, nsbuf=0)])
                if b == 0:
                    S.dma("sp", ns[:, :, h, :, :].rearrange("s d k v -> k (s d) v"), NSST[:, 0, :, :],
                          reads=[tNSST[0]], writes=[T()])
                osq, tosq = BT[9], tBT[9]
                S.op("act", lambda e, po=po, osq=osq: e.activation(out=osq, in_=po[:], func=AF.Square), reads=[tpo], writes=[tosq])
                pr, tpr = pbank()
                S.op("pe", lambda e, pr=pr, osq=osq: e.matmul(pr[:], lhsT=ONEB[:], rhs=osq, start=True, stop=True),
                     reads=[tosq, tCONST], writes=[tpr])
                rs, trs = FT[7], tFT[7]
                S.op("act", lambda e, pr=pr, rs=rs: e.activation(out=rs, in_=pr[:], func=AF.Ln, bias=EPSC[:, 2:3]),
                     reads=[tpr, tCONST], writes=[trs])
                S.op("act", lambda e, rs=rs: e.activation(out=rs, in_=rs, func=AF.Exp, scale=-0.5), reads=[trs], writes=[trs])
                S.op("dve", lambda e, po=po, rs=rs: e.tensor_tensor(out=rs, in0=po[:], in1=rs, op=ALU.mult),
                     reads=[tpo, trs], writes=[trs])
                S.op("dve", lambda e, rs=rs, h=h, b=b, vpar=vpar: e.scalar_tensor_tensor(
                    out=AB3[:, h, b * 512:(b + 1) * 512], in0=rs, scalar=OGS[:, h:h + 1], in1=ZGSs[vpar], op0=ALU.mult, op1=ALU.mult),
                    reads=[trs, tLB, tZGSs[vpar]], writes=[tAB[h][b]])
                unpin(tpo)
        S.barrier()
        if STAGE == 2:
            dump_debug()

        def load_weight(dst, tdst, src):
            S.dma("pool", dst, src, writes=[tdst])

        if STAGE >= 3:
            WO = WAR[:, 0:8192].rearrange("p (k n) -> p k n", k=8)
            tWO = T("wo")
            load_weight(WO, tWO, w_ho.rearrange("(k p) n -> p k n", p=128))
            for tt in range(20):
                s = tt % 2
                b = tt // 4
                S.dma("sp", XS[s], xo[tt * 128:(tt + 1) * 128, :], writes=[tXS[s]])
                for hf in range(2):
                    ps, tps = pbank()
                    for jj in range(4):
                        j = hf * 4 + jj
                        S.op("pe", lambda e, jj=jj, j=j, s=s, ps=ps: e.transpose(
                            out=ps[:, jj * 128:(jj + 1) * 128], in_=XS[s][:, j * 128:(j + 1) * 128], identity=IDF[:]),
                            reads=[tXS[s], tCONST], writes=[tps])
                    S.op("act", lambda e, ps=ps, hf=hf, tt=tt: e.activation(
                        out=XF[:, hf * 4:(hf + 1) * 4, tt * 128:(tt + 1) * 128],
                        in_=ps[:].rearrange("p (j t) -> p j t", j=4), func=AF.Copy),
                        reads=[tps], writes=[tX[hf * 4 + jj][b] for jj in range(4)])
            for b in range(NBLK):
                cond = 0 if b == 0 else 1
                for oc in range(8):
                    ps, tps = pbank()
                    for k in range(8):
                        S.op("pe", lambda e, k=k, oc=oc, b=b, ps=ps: e.matmul(
                            ps[:], lhsT=WO[:, k, oc * 128:(oc + 1) * 128], rhs=AB3[:, k, b * 512:(b + 1) * 512],
                            start=(k == 0), stop=(k == 7)), reads=[tWO, tAB[k][b]], writes=[tps])
                    resid_add(0, 2, oc, b, cond, ps[:], tps)
            S.barrier()
        if STAGE == 3:
            dump_debug()

        def mlp(l):
            for b in range(NBLK):
                cond = 0 if b == 0 else 1
                norm_featmajor(l, 1, b, cond, lambda j, b=b: AB3[:, j, b * 512:(b + 1) * 512], [tAB[j][b] for j in range(8)])
            W1 = [WAR[:, i * 4096:(i + 1) * 4096].rearrange("p (k n) -> p k n", k=8) for i in range(2)]
            W2 = [WAR[:, 8192 + i * 4096: 8192 + (i + 1) * 4096].rearrange("p (k n) -> p k n", k=4) for i in range(2)]
            tW1 = [T(), T()]
            tW2 = [T(), T()]
            HID = [FAR[:, i * 1024:(i + 1) * 1024].bitcast(BF16).rearrange("p (c t) -> p c t", c=4) for i in range(2)]
            tHID = [[T() for _ in range(4)] for _ in range(2)]
            it = 0
            def ld_mlp(G_):
                s_ = G_ % 2
                S.dma("pool", W1[s_], w1[l, :, G_ * 512:(G_ + 1) * 512].rearrange("(k p) n -> p k n", p=128), writes=[tW1[s_]])
                S.dma("pool", W2[s_], w2[l, G_ * 512:(G_ + 1) * 512, :].rearrange("(k p) n -> p k n", p=128), writes=[tW2[s_]])

            ld_mlp(0)
            for G in range(8):
                s = G % 2
                if G + 1 < 8:
                    ld_mlp(G + 1)
                for b in range(NBLK):
                    cond = 0 if b == 0 else 1
                    hs = it % 2
                    it += 1
                    for hc in range(4):
                        ps, tps = pbank()
                        for k in range(8):
                            S.op("pe", lambda e, k=k, hc=hc, b=b, ps=ps, s=s: e.matmul(
                                ps[:], lhsT=W1[s][:, k, hc * 128:(hc + 1) * 128], rhs=AB3[:, k, b * 512:(b + 1) * 512],
                                start=(k == 0), stop=(k == 7)), reads=[tW1[s], tAB[k][b]], writes=[tps])
                        r, tr = BT[hc % 4], tBT[hc % 4]
                        S.op("act", lambda e, ps=ps, r=r: e.activation(out=r, in_=ps[:], func=AF.Relu), reads=[tps], writes=[tr])
                        S.op("pool", lambda e, r=r, hs=hs, hc=hc: e.tensor_tensor(out=HID[hs][:, hc, :], in0=r, in1=r, op=ALU.mult),
                             reads=[tr], writes=[tHID[hs][hc]])
                    for oc in range(8):
                        ps, tps = pbank()
                        for k in range(4):
                            S.op("pe", lambda e, k=k, oc=oc, ps=ps, s=s, hs=hs: e.matmul(
                                ps[:], lhsT=W2[s][:, k, oc * 128:(oc + 1) * 128], rhs=HID[hs][:, k, :],
                                start=(k == 0), stop=(k == 3)), reads=[tW2[s], tHID[hs][k]], writes=[tps])
                        resid_add(l, 5, oc, b, cond, ps[:], tps)
            S.barrier()

        if STAGE >= 4:
            mlp(0)
        if STAGE == 4:
            dump_debug()

        if STAGE >= 5:
            CWI = ABUF[:, 0:16384].rearrange("p (k n) -> p k n", k=8)
            HTB = ABUF[:, 16384:20480].rearrange("p (k t) -> p k t", k=8)
            CWO = WAR[:, 0:8192].rearrange("p (k n) -> p k n", k=8)
            WST = WAR[:, 8192:9216].rearrange("p (g t) -> p g t", g=8)
            T2 = WAR[:, 10240:12288].bitcast(F32).rearrange("p (g t) -> p g t", g=8)
            UT = BAR[:, 0:4096].rearrange("p (k t) -> p k t", k=8)
            VN = [BAR[:, 4096:5120], BAR[:, 5120:6144]]
            tCWI, tCWO, tWST, tT2 = T(), T(), T(), T()
            tHTB = [T() for _ in range(8)]
            tUT = tBT[0:8]
            tVNs = [[tBT[8], tBT[9]], [tBT[10], tBT[11]]]
            load_weight(CWI, tCWI, cm_wi.rearrange("(k p) n -> p k n", p=128))
            load_weight(CWO, tCWO, cm_wo.rearrange("(k p) n -> p k n", p=128))
            load_weight(WST, tWST, wsT.rearrange("p (g t) -> p g t", g=8))
            S.dma("sp", XS[0], bsb, writes=[tXS[0]])
            for hf in range(2):
                ps, tps = pbank()
                S.op("pe", lambda e, ps=ps, hf=hf: e.matmul(ps[:], lhsT=ONEB[:], rhs=WAR[:, 8192 + hf * 512: 8192 + (hf + 1) * 512],
                                                            start=True, stop=True), reads=[tWST, tCONST], writes=[tps])
                for gg in range(4):
                    g = hf * 4 + gg
                    S.op("dve", lambda e, ps=ps, gg=gg, g=g: e.scalar_tensor_tensor(
                        out=T2[:, g, :], in0=ps[:, gg * 128:(gg + 1) * 128], scalar=VEC[:, 184 + g:185 + g],
                        in1=XS[0][:, g * 128:(g + 1) * 128], op0=ALU.mult, op1=ALU.add),
                        reads=[tps, tVEC, tXS[0]], writes=[tT2])
            S.barrier()
            for b in range(NBLK):
                cond = 0 if b == 0 else 1
                norm_featmajor(1, 0, b, cond, lambda j: HTB[:, j, :], tHTB)
                for uc in range(8):
                    ps, tps = pbank()
                    for k in range(8):
                        S.op("pe", lambda e, k=k, uc=uc, ps=ps: e.matmul(ps[:], lhsT=CWI[:, k, uc * 128:(uc + 1) * 128], rhs=HTB[:, k, :],
                                                                         start=(k == 0), stop=(k == 7)),
                             reads=[tCWI, tHTB[k]], writes=[tps])
                    S.op("act", lambda e, ps=ps, uc=uc: e.activation(out=UT[:, uc, :], in_=ps[:], func=AF.Gelu_apprx_tanh),
                         reads=[tps], writes=[tUT[uc]])
                def l1_tile(tt, b=b, cond=cond):
                    s = tt % 2
                    vg, tvg = XS[s], tXS[s]
                    st = SMALL[:, 16 + s * 8: 16 + s * 8 + 8]
                    for hf in range(2):
                        ps, tps = pbank()
                        for k in range(8):
                            S.op("pe", lambda e, k=k, hf=hf, tt=tt, ps=ps: e.matmul(
                                ps[:], lhsT=HTB[:, k, tt * 128:(tt + 1) * 128], rhs=CWI[:, k, 1024 + hf * 512: 1024 + (hf + 1) * 512],
                                start=(k == 0), stop=(k == 7)), reads=[tCWI, tHTB[k]], writes=[tps])
                        S.op("act", lambda e, ps=ps, hf=hf, vg=vg, st=st: e.activation(
                            out=vg[:, hf * 512:(hf + 1) * 512], in_=ps[:], func=AF.Gelu_apprx_tanh, accum_out=st[:, hf:hf + 1]),
                            reads=[tps], writes=[tvg, tSM2[s]])
                    S.op("act", lambda e, vg=vg, s=s, st=st: e.activation(out=VN[s], in_=vg, func=AF.Square, accum_out=st[:, 2:3]),
                         reads=[tvg], writes=tVNs[s] + [tSM2[s]])
                    yield
                    S.op("dve", lambda e, st=st: e.tensor_tensor(out=st[:, 3:4], in0=st[:, 0:1], in1=st[:, 1:2], op=ALU.add),
                         reads=[tSM2[s]], writes=[tSM2[s]])
                    yield
                    S.op("dve", lambda e, st=st: e.tensor_scalar(out=st[:, 3:4], in0=st[:, 3:4], scalar1=1.0 / D, scalar2=None, op0=ALU.mult),
                         reads=[tSM2[s]], writes=[tSM2[s]])
                    yield
                    S.op("dve", lambda e, st=st: e.tensor_tensor(out=st[:, 4:5], in0=st[:, 3:4], in1=st[:, 3:4], op=ALU.mult),
                         reads=[tSM2[s]], writes=[tSM2[s]])
                    yield
                    S.op("dve", lambda e, st=st: e.scalar_tensor_tensor(out=st[:, 5:6], in0=st[:, 2:3], scalar=1.0 / D, in1=st[:, 4:5],
                                                                        op0=ALU.mult, op1=ALU.subtract),
                         reads=[tSM2[s]], writes=[tSM2[s]])
                    yield
                    S.op("act", lambda e, st=st: e.activation(out=st[:, 5:6], in_=st[:, 5:6], func=AF.Sqrt, bias=EPSC[:, 0:1]),
                         reads=[tSM2[s], tCONST], writes=[tSM2[s]])
                    yield
                    S.op("dve", lambda e, st=st: e.reciprocal(out=st[:, 5:6], in_=st[:, 5:6]), reads=[tSM2[s]], writes=[tSM2[s]])
                    yield
                    S.op("dve", lambda e, vg=vg, s=s, st=st: e.tensor_scalar(out=VN[s], in0=vg, scalar1=st[:, 3:4], scalar2=st[:, 5:6],
                                                                             op0=ALU.subtract, op1=ALU.mult),
                         reads=[tvg, tSM2[s]], writes=tVNs[s])
                    yield
                    pm = [pbank(pin=True), pbank(pin=True)]
                    for g in range(8):
                        ps, tps = pm[g // 4]
                        gg = g % 4
                        S.op("pe", lambda e, g=g, gg=gg, ps=ps, s=s: e.matmul(ps[:, gg * 128:(gg + 1) * 128], lhsT=VN[s][:, g * 128:(g + 1) * 128],
                                                                              rhs=WST[:, g, :], start=True, stop=True),
                             reads=tVNs[s] + [tWST], writes=[tps])
                    for g in range(8):
                        ps, tps = pm[g // 4]
                        gg = g % 4
                        tmp, ttmp = FT[4 + g % 3], tFT[4 + g % 3]
                        S.op("dve", lambda e, g=g, gg=gg, ps=ps, tmp=tmp: e.scalar_tensor_tensor(
                            out=tmp[:, 0:128], in0=ps[:, gg * 128:(gg + 1) * 128], scalar=VEC[:, 176 + g:177 + g], in1=T2[:, g, :],
                            op0=ALU.mult, op1=ALU.add), reads=[tps, tVEC, tT2], writes=[ttmp])
                        S.op("pool", lambda e, g=g, tt=tt, tmp=tmp: e.tensor_tensor(
                            out=UT[:, g, tt * 128:(tt + 1) * 128], in0=UT[:, g, tt * 128:(tt + 1) * 128], in1=tmp[:, 0:128], op=ALU.mult),
                            reads=[ttmp, tUT[g]], writes=[tUT[g]])
                        if g % 2 == 1:
                            yield
                    unpin(pm[0][1])
                    unpin(pm[1][1])
                run_streams([l1_tile(0), l1_tile(1)])
                run_streams([l1_tile(2), l1_tile(3)])
                for oc in range(8):
                    ps, tps = pbank()
                    for k in range(8):
                        S.op("pe", lambda e, k=k, oc=oc, ps=ps: e.matmul(ps[:], lhsT=CWO[:, k, oc * 128:(oc + 1) * 128], rhs=UT[:, k, :],
                                                                         start=(k == 0), stop=(k == 7)),
                             reads=[tCWO, tUT[k]], writes=[tps])
                    resid_add(1, 2, oc, b, cond, ps[:], tps)
            S.barrier()
        if STAGE == 5:
            dump_debug()
        if STAGE >= 6:
            mlp(1)
        if STAGE == 6:
            dump_debug()

        ty = T("y")
        if STAGE >= 7:
            for b in range(NBLK):
                pr, tpr = pbank()
                for j in range(8):
                    sq, tsq = BT[8 + j % 4], tBT[8 + j % 4]
                    S.op("act", lambda e, j=j, sq=sq, b=b: e.activation(out=sq, in_=XB(j, b), func=AF.Square),
                         reads=[tX[j][b]], writes=[tsq])
                    S.op("pe", lambda e, j=j, sq=sq, pr=pr: e.matmul(pr[:], lhsT=ONEB[:], rhs=sq, start=(j == 0), stop=(j == 7)),
                         reads=[tsq, tCONST], writes=[tpr])
                rs, trs = FT[7], tFT[7]
                S.op("act", lambda e, pr=pr, rs=rs: e.activation(out=rs, in_=pr[:], func=AF.Ln, bias=EPSC[:, 1:2]),
                     reads=[tpr, tCONST], writes=[trs])
                S.op("act", lambda e, rs=rs: e.activation(out=rs, in_=rs, func=AF.Exp, scale=-0.5), reads=[trs], writes=[trs])
                for j in range(8):
                    S.op("dve", lambda e, j=j, b=b, rs=rs: e.scalar_tensor_tensor(
                        out=XB(j, b), in0=XB(j, b), scalar=FGS[:, j:j + 1], in1=rs, op0=ALU.mult, op1=ALU.mult),
                        reads=[tX[j][b], tLB, trs], writes=[tX[j][b]])
                for ti in range(4):
                    tt = b * 4 + ti
                    s = tt % 2
                    for hf in range(2):
                        ps, tps = pbank()
                        for jj in range(4):
                            j = hf * 4 + jj
                            S.op("pe", lambda e, jj=jj, j=j, tt=tt, ps=ps: e.transpose(
                                out=ps[:, jj * 128:(jj + 1) * 128], in_=XF[:, j, tt * 128:(tt + 1) * 128], identity=IDF[:]),
                                reads=[tX[j][b], tCONST], writes=[tps])
                        S.op("act", lambda e, ps=ps, hf=hf, s=s: e.activation(out=XS[s][:, hf * 512:(hf + 1) * 512], in_=ps[:], func=AF.Copy),
                             reads=[tps, tXS[s]], writes=[tXS[s]])
                    S.dma("sp", y[tt * 128:(tt + 1) * 128, :], XS[s], reads=[tXS[s]], writes=[ty])
        S.barrier()
        with nc.Block() as block:
            S.replay(block)
    return nc


_NC_CACHE = {}


def _prep_inputs(inp):
    f = lambda a: np.ascontiguousarray(np.asarray(a, dtype=np.float32))
    x_prompt, x_sample = f(inp["x_prompt"]), f(inp["x_sample"])
    st, c, c_ctx = f(inp["state_hgrn"]), f(inp["c"]), f(inp["c_ctx"])
    hw = f(inp["hgrn_w_in"])[0]
    fm = lambda v: np.ascontiguousarray(v.reshape(8, 128).T)
    maps = []
    for core in range(8):
        bq, half = core // 2, core % 2
        rev = half == 1
        R = (lambda a: a[::-1]) if rev else (lambda a: a)
        ca, cb = x_prompt[2 * core], x_prompt[2 * core + 1]
        lat = x_sample[bq]
        own = lat[half * 2048:(half + 1) * 2048]
        oth = lat[(1 - half) * 2048:(2 - half) * 2048]
        xo = np.concatenate([R(ca), R(cb), R(own)], axis=0)
        xh = R(oth)
        cT = np.zeros((128, 16), np.float32)
        cc = np.stack([c_ctx, c[bq]], axis=1)
        cT[:] = cc.reshape(8, 128, 2).transpose(1, 0, 2).reshape(128, 16)
        vec = np.zeros((128, NV), np.float32)
        for l in range(2):
            vec[:, l * 16:l * 16 + 8] = fm(inp["norm_mix_g"][l])
            vec[:, l * 16 + 8:l * 16 + 16] = fm(inp["norm_mlp_g"][l])
            ab = f(inp["ada_b"])[l].reshape(6, 8, 128)
            vec[:, 40 + l * 48: 88 + l * 48] = ab.transpose(2, 0, 1).reshape(128, 48)
        vec[:, 32:40] = fm(inp["final_norm_g"])
        vec[:, 136:144] = f(inp["hgrn_onorm_g"])[0].T
        lbl = f(inp["hgrn_lb_logits"])
        dA, dB = (1, 0) if rev else (0, 1)
        vec[:, 144:152] = fm(lbl[0, dA]); vec[:, 152:160] = fm(lbl[0, dB])
        vec[:, 160:168] = fm(lbl[1, dA]); vec[:, 168:176] = fm(lbl[1, dB])
        vec[:, 176:184] = fm(inp["cm_ln_g"][0]); vec[:, 184:192] = fm(inp["cm_ln_b"][0])
        q, ff, fb, ii, gg = [hw[:, k * 1024:(k + 1) * 1024] for k in range(5)]
        fA, fB = (fb, ff) if rev else (ff, fb)
        w_hg = np.stack([np.concatenate([m[:, h * 128:(h + 1) * 128] for m in (q, fA, fB, gg, ii)], axis=1) for h in range(8)])
        ws = f(inp["cm_w_s"])[0]
        bs = f(inp["cm_b_s"])[0]
        if rev:
            ws = ws[:, ::-1, ::-1]
            bs = bs[:, ::-1]
        wsT = np.ascontiguousarray(ws.transpose(2, 0, 1).reshape(128, 1024))
        bsb = np.ascontiguousarray(np.broadcast_to(bs.reshape(1, 1024), (128, 1024)))
        sa0 = st[bq, 0, 1 if rev else 0]
        sb0 = st[bq, 0, 0 if rev else 1]
        maps.append(dict(
            xo=np.ascontiguousarray(xo), xh=np.ascontiguousarray(xh), cT=cT, vecs=vec,
            ada_w=f(inp["ada_w"]), w_hg=np.ascontiguousarray(w_hg), w_ho=f(inp["hgrn_w_out"])[0],
            w1=f(inp["mlp_w1"]), w2=f(inp["mlp_w2"]), cm_wi=f(inp["cm_w_in"])[0], cm_wo=f(inp["cm_w_out"])[0],
            wsT=wsT, bsb=bsb, sa0=np.ascontiguousarray(sa0), sb0=np.ascontiguousarray(sb0)))
    return maps


def kernel(**inputs):
    maps = _prep_inputs(inputs)
    if "nc" not in _NC_CACHE:
        _NC_CACHE["nc"] = build_program()
    nc = _NC_CACHE["nc"]
    res = run_bass_kernel_spmd(nc, maps, core_ids=list(range(8)))
    y_prompt = np.zeros((16, 256, D), np.float32)
    y_sample = np.zeros((4, 4096, D), np.float32)
    new_state = np.zeros((16, 1, 2, 8, 128, 128), np.float32)
    for core in range(8):
        r = res.results[core]
        bq, half = core // 2, core % 2
        rev = half == 1
        R = (lambda a: a[::-1]) if rev else (lambda a: a)
        yy = np.asarray(r["y"])
        y_prompt[2 * core] = R(yy[0:256])
        y_prompt[2 * core + 1] = R(yy[256:512])
        y_sample[bq, half * 2048:(half + 1) * 2048] = R(yy[512:])
        nsr = np.asarray(r["ns"])
        for s in range(2):
            if rev:
                new_state[2 * core + s, 0, 0] = nsr[s, 1]
                new_state[2 * core + s, 0, 1] = nsr[s, 0]
            else:
                new_state[2 * core + s, 0, 0] = nsr[s, 0]
                new_state[2 * core + s, 0, 1] = nsr[s, 1]
        if DEBUG:
            _NC_CACHE.setdefault("dbg", {})[core] = (np.asarray(r["dbgx"]), np.asarray(r["dbga"]))
    return y_prompt, y_sample, new_state
```
